# Optimizing a Trainium2 kernel written in Bass

```python
import math
import jax, jax.numpy as jnp
from jax import lax
import numpy as np

D_MODEL = 2048
BATCH = 4
SEQ = 4096
DEPTH = 1
DEC_BATCH = 128
DEC_SEQ = 4
PAST_LEN = 16384
PAGE_SIZE = 128

RMS_EPS = 1e-6
MIX_WIDTH = D_MODEL
HGRN_EXPAND = 128
HGRN_VAL_WIDTH = MIX_WIDTH // 2
HGRN_HEADS = HGRN_VAL_WIDTH // HGRN_EXPAND
HGRN_DK = HGRN_EXPAND
HGRN_DV = HGRN_VAL_WIDTH // HGRN_HEADS
HGRN_KEY_WIDTH = HGRN_HEADS * HGRN_DK
HGRN_CHUNK = 16
SWA_WIDTH = MIX_WIDTH - HGRN_VAL_WIDTH
SWA_HEAD_DIM = 64
SWA_HEADS = SWA_WIDTH // SWA_HEAD_DIM
SWA_KV_HEADS = 2
SWA_GROUP = SWA_HEADS // SWA_KV_HEADS
SWA_KV_WIDTH = SWA_KV_HEADS * SWA_HEAD_DIM
SWA_WINDOW = 128
REL_BUCKETS = 32
REL_MAX_EXACT = REL_BUCKETS // 2
REL_MAX_DIST = 128
N_MEM = 256
MEM_HEADS = 4
MEM_HEAD_DIM = 128
MEM_WIDTH = MEM_HEADS * MEM_HEAD_DIM
FFN_HIDDEN = -(-8 * D_MODEL // (3 * 256)) * 256
IN_SIZES = (HGRN_KEY_WIDTH, HGRN_KEY_WIDTH, HGRN_VAL_WIDTH, HGRN_VAL_WIDTH,
            SWA_WIDTH, SWA_KV_WIDTH, SWA_KV_WIDTH)
IN_WIDTH = sum(IN_SIZES)
IN_SPLITS = tuple(sum(IN_SIZES[:i + 1]) for i in range(len(IN_SIZES) - 1))

kernel_name = "hymba_hgrn2_swa_sink_memxattn_step"


def rmsnorm(x, w, eps=RMS_EPS):
    xf = x.astype(jnp.float32)
    xf = xf * lax.rsqrt(jnp.mean(xf * xf, axis=-1, keepdims=True) + eps)
    return (xf * w.astype(jnp.float32)).astype(x.dtype)


def hgrn_lower_bounds(lb_logits):
    p = jax.nn.softmax(lb_logits.astype(jnp.float32), axis=0)
    return jnp.cumsum(p, axis=0)[:DEPTH]


def hgrn2_scan(q, k, g, v, s0):
    B, L, H, DK = q.shape
    C = math.gcd(L, HGRN_CHUNK)
    N = L // C
    qc, kc, gc, vc = (t.reshape(B, N, C, H, t.shape[-1]) for t in (q, k, g, v))
    G = jnp.cumsum(gc, axis=2)
    causal = jnp.tril(jnp.ones((C, C), bool))[None, None, :, :, None, None]
    diff = G[:, :, :, None] - G[:, :, None, :]
    decay = jnp.exp(jnp.where(causal, diff, -jnp.inf))
    attn = jnp.einsum('bnthc,bnshc,bntshc->bnhts', qc, kc, decay)
    o_intra = jnp.einsum('bnhts,bnshe->bnthe', attn, vc)
    g_last = G[:, :, -1]
    q_in = qc * jnp.exp(G)
    k_up = kc * jnp.exp(g_last[:, :, None] - G)

    def step(S, xs):
        q_n, k_n, v_n, gl_n = xs
        o = jnp.einsum('bthc,bhce->bthe', q_n, S)
        S = jnp.exp(gl_n)[..., None] * S + jnp.einsum('bshc,bshe->bhce', k_n, v_n)
        return S, o

    xs = tuple(jnp.moveaxis(t, 1, 0) for t in (q_in, k_up, vc, g_last))
    s_final, o_inter = lax.scan(step, s0.astype(jnp.float32), xs)
    o = o_intra + jnp.moveaxis(o_inter, 0, 1)
    return o.reshape(B, L, H, v.shape[-1]), s_final


def hgrn_group(hq, hf, hi, hg, lb, onorm_w, s0):
    B, L, _ = hq.shape
    f = lb + (1.0 - lb) * jax.nn.sigmoid(hf.astype(jnp.float32))
    heads = lambda t, d: t.astype(jnp.float32).reshape(B, L, HGRN_HEADS, d)
    q = heads(hq, HGRN_DK)
    g = heads(jnp.log(f), HGRN_DK)
    k = heads(1.0 - f, HGRN_DK)
    i = heads(hi, HGRN_DV)
    o, s_new = hgrn2_scan(q, k, g, i, s0)
    o = rmsnorm(o, onorm_w) * jax.nn.silu(heads(hg, HGRN_DV))
    return o.reshape(B, L, HGRN_VAL_WIDTH).astype(hq.dtype), s_new


def rel_bucket(dist):
    n = jnp.maximum(dist, 0)
    nf = jnp.maximum(n, 1).astype(jnp.float32)
    large = REL_MAX_EXACT + (jnp.log(nf / REL_MAX_EXACT) / math.log(REL_MAX_DIST / REL_MAX_EXACT)
                             * (REL_BUCKETS - REL_MAX_EXACT)).astype(jnp.int32)
    large = jnp.minimum(large, REL_BUCKETS - 1)
    return jnp.where(n < REL_MAX_EXACT, n, large)


def swa_band(q, k, v, key_valid, sinks, rel_bias):
    B, N, Qn, _ = q.shape
    Kn = k.shape[2]
    qh = q.astype(jnp.float32).reshape(B, N, Qn, SWA_KV_HEADS, SWA_GROUP, SWA_HEAD_DIM)
    kh = k.astype(jnp.float32).reshape(B, N, Kn, SWA_KV_HEADS, SWA_HEAD_DIM)
    vh = v.astype(jnp.float32).reshape(B, N, Kn, SWA_KV_HEADS, SWA_HEAD_DIM)
    dist = (jnp.arange(Qn)[:, None] + (Kn - Qn)) - jnp.arange(Kn)[None, :]
    bias = rel_bias.astype(jnp.float32)[rel_bucket(dist)]
    bias = jnp.transpose(bias, (2, 0, 1)).reshape(SWA_KV_HEADS, SWA_GROUP, Qn, Kn)
    s = jnp.einsum('bnqkgd,bnskd->bnkgqs', qh, kh) * (SWA_HEAD_DIM ** -0.5) + bias
    mask = ((dist >= 0) & (dist < SWA_WINDOW))[None] & key_valid[:, None, :]
    s = jnp.where(mask[None, :, None, None], s, -jnp.inf)
    sink = sinks.astype(jnp.float32).reshape(SWA_KV_HEADS, SWA_GROUP)[:, :, None, None]
    m = jnp.maximum(jnp.max(s, axis=-1, keepdims=True), sink)
    p = jnp.exp(s - m)
    p = p / (jnp.sum(p, axis=-1, keepdims=True) + jnp.exp(sink - m))
    o = jnp.einsum('bnkgqs,bnskd->bnqkgd', p, vh)
    return o.reshape(B, N, Qn, SWA_WIDTH).astype(q.dtype)


def token_mixer(h, s0, k_buf, v_buf, lb, w_in, hgrn_onorm_w, swa_sinks, rel_bias, w_out):
    B, L, _ = h.shape
    hq, hf, hi, hg, sq, sk, sv = jnp.split(h @ w_in, IN_SPLITS, axis=-1)
    a_out, s_new = hgrn_group(hq, hf, hi, hg, lb, hgrn_onorm_w, s0)
    if k_buf is None:
        nb = L // SWA_WINDOW
        qb = sq.reshape(B, nb, SWA_WINDOW, SWA_WIDTH)

        def band(t):
            tb = t.reshape(B, nb, SWA_WINDOW, SWA_KV_WIDTH)
            prev = jnp.concatenate([jnp.zeros_like(tb[:, :1]), tb[:, :-1]], axis=1)
            return jnp.concatenate([prev, tb], axis=2)

        kb, vb = band(sk), band(sv)
        key_valid = (jnp.arange(nb)[:, None] > 0) | (jnp.arange(2 * SWA_WINDOW)[None, :] >= SWA_WINDOW)
        k_last, v_last = sk[:, -SWA_WINDOW:], sv[:, -SWA_WINDOW:]
    else:
        qb = sq[:, None]
        k_all = jnp.concatenate([k_buf.reshape(B, SWA_WINDOW, SWA_KV_WIDTH), sk], axis=1)
        v_all = jnp.concatenate([v_buf.reshape(B, SWA_WINDOW, SWA_KV_WIDTH), sv], axis=1)
        kb, vb = k_all[:, None], v_all[:, None]
        key_valid = jnp.ones((1, SWA_WINDOW + L), bool)
        k_last, v_last = k_all[:, -SWA_WINDOW:], v_all[:, -SWA_WINDOW:]
    b_out = swa_band(qb, kb, vb, key_valid, swa_sinks, rel_bias).reshape(B, L, SWA_WIDTH)
    y = jnp.concatenate([a_out, b_out], axis=-1) @ w_out
    kv_shape = (B, SWA_WINDOW, SWA_KV_HEADS, SWA_HEAD_DIM)
    return y, s_new, k_last.reshape(kv_shape), v_last.reshape(kv_shape)


def memory_kv(mem, mem_norm_w, w_mk, w_mv):
    B = mem.shape[0]
    m = rmsnorm(mem, mem_norm_w)
    k = (m @ w_mk).reshape(B, N_MEM, MEM_HEADS, MEM_HEAD_DIM)
    v = (m @ w_mv).reshape(B, N_MEM, MEM_HEADS, MEM_HEAD_DIM)
    return k, v


def memory_attend(h, mem_k, mem_v, w_mq, w_mo):
    B, L, _ = h.shape
    q = (h @ w_mq).reshape(B, L, MEM_HEADS, MEM_HEAD_DIM).astype(jnp.float32)
    s = jnp.einsum('blhd,bmhd->bhlm', q, mem_k.astype(jnp.float32)) * (MEM_HEAD_DIM ** -0.5)
    p = jax.nn.softmax(s, axis=-1)
    o = jnp.einsum('bhlm,bmhd->blhd', p, mem_v.astype(jnp.float32))
    return o.reshape(B, L, MEM_WIDTH).astype(h.dtype) @ w_mo


def swiglu(h, w_gate, w_up, w_down):
    return (jax.nn.silu(h @ w_gate) * (h @ w_up)) @ w_down


def decoder_layer(x, s0, k_buf, v_buf, mem_k, mem_v, lb, norm_mix_w, w_in, hgrn_onorm_w,
                  swa_sinks, rel_bias, w_out, norm_xattn_w, w_mq, w_mo, norm_ffn_w,
                  w_gate, w_up, w_down):
    y, s_new, k_new, v_new = token_mixer(rmsnorm(x, norm_mix_w), s0, k_buf, v_buf, lb, w_in,
                                         hgrn_onorm_w, swa_sinks, rel_bias, w_out)
    x = x + y
    x = x + memory_attend(rmsnorm(x, norm_xattn_w), mem_k, mem_v, w_mq, w_mo)
    x = x + swiglu(rmsnorm(x, norm_ffn_w), w_gate, w_up, w_down)
    return x, s_new, k_new, v_new


def setup_inputs(seed: int = 0) -> dict:
    key = jax.random.key(seed)
    ks = jax.random.split(key, 32)
    nrm = lambda k, shape, scale=1.0: scale * jax.random.normal(k, shape, jnp.float32)
    gain = lambda k, shape: 1.0 + 0.05 * jax.random.normal(k, shape, jnp.float32)
    kv_shape = (DEPTH, DEC_BATCH, SWA_WINDOW, SWA_KV_HEADS, SWA_HEAD_DIM)
    mem_shape = (DEPTH, DEC_BATCH, N_MEM, MEM_HEADS, MEM_HEAD_DIM)
    return {
        "x_prompt": nrm(ks[0], (BATCH, SEQ, D_MODEL)),
        "x_sample": nrm(ks[1], (DEC_BATCH, DEC_SEQ, D_MODEL)),
        "state_hgrn": nrm(ks[2], (DEPTH, DEC_BATCH, HGRN_HEADS, HGRN_DK, HGRN_DV), 0.5),
        "cache_swa_k": nrm(ks[3], kv_shape),
        "cache_swa_v": nrm(ks[4], kv_shape),
        "cache_mem_k": nrm(ks[5], mem_shape),
        "cache_mem_v": nrm(ks[6], mem_shape),
        "mem_prompt": nrm(ks[7], (BATCH, N_MEM, D_MODEL)),
        "norm_mix_w": gain(ks[8], (DEPTH, D_MODEL)),
        "w_in": nrm(ks[9], (DEPTH, D_MODEL, IN_WIDTH), D_MODEL ** -0.5),
        "hgrn_lb_logits": nrm(ks[10], (DEPTH + 1, HGRN_KEY_WIDTH), 0.5),
        "hgrn_onorm_w": gain(ks[11], (DEPTH, HGRN_DV)),
        "swa_sinks": nrm(ks[12], (DEPTH, SWA_HEADS), 0.5),
        "rel_bias": nrm(ks[13], (REL_BUCKETS, SWA_HEADS), 0.2),
        "w_out": nrm(ks[14], (DEPTH, MIX_WIDTH, D_MODEL), MIX_WIDTH ** -0.5),
        "norm_xattn_w": gain(ks[15], (DEPTH, D_MODEL)),
        "mem_norm_w": gain(ks[16], (DEPTH, D_MODEL)),
        "w_mq": nrm(ks[17], (DEPTH, D_MODEL, MEM_WIDTH), D_MODEL ** -0.5),
        "w_mk": nrm(ks[18], (DEPTH, D_MODEL, MEM_WIDTH), D_MODEL ** -0.5),
        "w_mv": nrm(ks[19], (DEPTH, D_MODEL, MEM_WIDTH), D_MODEL ** -0.5),
        "w_mo": nrm(ks[20], (DEPTH, MEM_WIDTH, D_MODEL), MEM_WIDTH ** -0.5),
        "norm_ffn_w": gain(ks[21], (DEPTH, D_MODEL)),
        "w_gate": nrm(ks[22], (DEPTH, D_MODEL, FFN_HIDDEN), D_MODEL ** -0.5),
        "w_up": nrm(ks[23], (DEPTH, D_MODEL, FFN_HIDDEN), D_MODEL ** -0.5),
        "w_down": nrm(ks[24], (DEPTH, FFN_HIDDEN, D_MODEL), FFN_HIDDEN ** -0.5),
        "norm_final_w": gain(ks[25], (D_MODEL,)),
    }


def reference(x_prompt, x_sample, state_hgrn, cache_swa_k, cache_swa_v, cache_mem_k, cache_mem_v,
              mem_prompt, norm_mix_w, w_in, hgrn_lb_logits, hgrn_onorm_w, swa_sinks, rel_bias,
              w_out, norm_xattn_w, mem_norm_w, w_mq, w_mk, w_mv, w_mo, norm_ffn_w, w_gate, w_up,
              w_down, norm_final_w):
    lbs = hgrn_lower_bounds(hgrn_lb_logits)
    xp, xs = x_prompt, x_sample
    p_s, p_k, p_v, p_mk, p_mv, s_s, s_k, s_v = [], [], [], [], [], [], [], []
    for l in range(DEPTH):
        shared = (lbs[l], norm_mix_w[l], w_in[l], hgrn_onorm_w[l], swa_sinks[l], rel_bias,
                  w_out[l], norm_xattn_w[l], w_mq[l], w_mo[l], norm_ffn_w[l], w_gate[l],
                  w_up[l], w_down[l])
        mk, mv = memory_kv(mem_prompt, mem_norm_w[l], w_mk[l], w_mv[l])
        s0 = jnp.zeros((BATCH, HGRN_HEADS, HGRN_DK, HGRN_DV), jnp.float32)
        xp, sp, kp, vp = decoder_layer(xp, s0, None, None, mk, mv, *shared)
        xs, ss, ksn, vsn = decoder_layer(xs, state_hgrn[l], cache_swa_k[l], cache_swa_v[l],
                                         cache_mem_k[l], cache_mem_v[l], *shared)
        p_s.append(sp); p_k.append(kp); p_v.append(vp); p_mk.append(mk); p_mv.append(mv)
        s_s.append(ss); s_k.append(ksn); s_v.append(vsn)
    y_prompt = rmsnorm(xp, norm_final_w)
    y_sample = rmsnorm(xs, norm_final_w)
    return (y_prompt, y_sample,
            jnp.stack(p_s).astype(x_prompt.dtype), jnp.stack(p_k), jnp.stack(p_v),
            jnp.stack(p_mk), jnp.stack(p_mv),
            jnp.stack(s_s).astype(state_hgrn.dtype), jnp.stack(s_k), jnp.stack(s_v))
```

```python
import contextlib
import math
import numpy as np
import concourse.bass as bass
import concourse.mybir as mybir
from concourse.bass_utils import run_bass_kernel_spmd

F32 = mybir.dt.float32
BF16 = mybir.dt.bfloat16
AF = mybir.ActivationFunctionType
ALU = mybir.AluOpType
AX = mybir.AxisListType

D = 2048
NK = 16
FF = 5632
NCH = 22
EPS = 1e-6
NTILE = 16
WS = 64
NSEQ = 16


class _Op:
    __slots__ = ("eng", "fn", "deps", "sig", "seq", "sigcount", "dma_key", "dma_cnt", "waits")


class Prog:
    ENGS = ("pe", "act", "dve", "pool", "sp")

    def __init__(self, nc):
        self.nc = nc
        self.ops = {e: [] for e in self.ENGS}
        self.last_writer = {}
        self.readers = {}
        self.dma_count = {}
        self.last_dma = {}
        self.bar = []

    def _add(self, eng, fn, reads, writes, sig, dma_key=None):
        o = _Op()
        o.eng, o.fn, o.sig, o.dma_key = eng, fn, sig, dma_key
        o.dma_cnt = 0
        deps = list(self.bar)
        for k in reads:
            w = self.last_writer.get(k)
            if w is not None:
                deps.append(w)
            if k.startswith("bank"):
                deps.extend(r for r in self.readers.get(k, ()) if r.eng != eng)
        for k in writes:
            w = self.last_writer.get(k)
            if w is not None:
                deps.append(w)
            deps.extend(self.readers.get(k, ()))
        o.deps = deps
        if dma_key is not None:
            c = self.dma_count.get(dma_key, 0) + 1
            self.dma_count[dma_key] = c
            o.dma_cnt = c
            self.last_dma[dma_key] = o
        for k in reads:
            self.readers.setdefault(k, []).append(o)
        for k in writes:
            self.last_writer[k] = o
            self.readers[k] = []
        o.seq = len(self.ops[eng])
        self.ops[eng].append(o)
        return o

    def op(self, eng, fn, reads=(), writes=(), sig=True):
        return self._add(eng, fn, tuple(reads), tuple(writes), sig)

    def dma(self, fn, key, reads=(), writes=(), eng="sp"):
        return self._add(eng, fn, tuple(reads), tuple(writes), True, dma_key=key)

    def barrier(self):
        bar = []
        for e in self.ENGS:
            for o in reversed(self.ops[e]):
                if o.dma_key is None:
                    o.sig = True
                    bar.append(o)
                    break
        bar.extend(self.last_dma.values())
        self.bar = bar
        self.last_writer = {}
        self.readers = {}

    def emit(self):
        nc = self.nc
        es = contextlib.ExitStack()
        with es:
            sems = {e: es.enter_context(nc.semaphore("s_" + e)) for e in self.ENGS}
            dsems = {k: es.enter_context(nc.semaphore("d_%d" % i)) for i, k in enumerate(self.dma_count)}
            for e in self.ENGS:
                lst = self.ops[e]
                for o in reversed(lst):
                    if o.dma_key is None:
                        o.sig = True
                        break
                cnt = 0
                pend = []
                for o in lst:
                    if o.dma_key is not None:
                        o.sigcount = None
                        continue
                    pend.append(o)
                    if o.sig:
                        cnt += 1
                        for p in pend:
                            p.sigcount = cnt
                        pend = []
            for e in self.ENGS:
                waited = {}
                for o in self.ops[e]:
                    need = {}
                    for d in o.deps:
                        if d.dma_key is not None:
                            s, v = ("d", d.dma_key), 16 * d.dma_cnt
                        else:
                            if d.eng == "pe" and e == "pe" and o.dma_key is None:
                                continue
                            s, v = ("e", d.eng), d.sigcount
                        if v > need.get(s, 0):
                            need[s] = v
                    if o.dma_key is not None and o.dma_cnt > 1:
                        s, v = ("d", o.dma_key), 16 * (o.dma_cnt - 1)
                        if v > need.get(s, 0):
                            need[s] = v
                    ws = []
                    for s, v in need.items():
                        if v > waited.get(s, 0):
                            waited[s] = v
                            ws.append((s, v))
                    o.waits = ws
            block = es.enter_context(nc.Block())

            def run(e):
                def body(engobj):
                    for o in self.ops[e]:
                        for (kind, k), v in o.waits:
                            engobj.wait_ge(sems[k] if kind == "e" else dsems[k], v)
                        ins = o.fn(engobj)
                        if o.dma_key is not None:
                            ins.then_inc(dsems[o.dma_key], 16)
                        elif o.sig:
                            ins.then_inc(sems[e], 1)
                    last = {}
                    for o in self.ops[e]:
                        if o.dma_key is not None:
                            last[o.dma_key] = max(last.get(o.dma_key, 0), o.dma_cnt)
                    for k, c in last.items():
                        engobj.wait_ge(dsems[k], 16 * c)
                return body

            block.tensor(run("pe"))
            block.scalar(run("act"))
            block.vector(run("dve"))
            block.gpsimd(run("pool"))
            block.sync(run("sp"))
        return nc


class Buf:
    def __init__(self, ap, key):
        self.ap = ap
        self.k = key

    def __getitem__(self, idx):
        return self.ap[idx]


class Arena:
    def __init__(self, tensor, nbytes, tag):
        self.t, self.nbytes, self.tag = tensor, nbytes, tag
        self.off = 0
        self.cnt = 0
        self.hi = 0

    def reset(self):
        self.off = 0

    def alloc(self, dtype, fshape, name, rows=128):
        es = 4 if dtype == F32 else 2
        n = 1
        for s in fshape:
            n *= s
        size = (n * es + 31) // 32 * 32
        off = self.off
        assert off + size <= self.nbytes, (self.tag, name, off, size, self.nbytes)
        self.off = off + size
        self.hi = max(self.hi, self.off)
        v = self.t[0:rows, off // 4:(off + n * es + 3) // 4]
        if dtype != F32:
            v = v.bitcast(dtype)
            v = v[:, 0:n]
        if len(fshape) == 2:
            v = v.rearrange("p (a b) -> p a b", b=fshape[1])
        elif len(fshape) == 3:
            v = v.rearrange("p (a b c) -> p a b c", b=fshape[1], c=fshape[2])
        elif len(fshape) == 4:
            v = v.rearrange("p (a b c d) -> p a b c d", b=fshape[1], c=fshape[2], d=fshape[3])
        self.cnt += 1
        return Buf(v, "%s.%s.%d" % (self.tag, name, self.cnt))


def bcast(ap, shape):
    return ap.to_broadcast(list(shape))


U_HGA = 0
U_HGB = 8
U_SQ = 16
U_SKVA = 20
U_SKVB = 21
U_WO = 22
U_MQ = 30
U_MK = 32
U_MV = 34
U_MO = 36
U_FG = 38
U_FU = 60
U_FD = 82
NUNIT = 104


def build_program(cfg):
    nc = bass.Bass("TRN2", target_bir_lowering=False)
    es = contextlib.ExitStack()
    with es:
        es.enter_context(nc.allow_non_contiguous_dma(reason="small parameter vectors"))
        _build(nc, es, cfg)
    return nc


def _build(nc, es, cfg):
    def din(name, shape):
        return nc.dram_tensor(name, list(shape), F32, kind="ExternalInput").ap()

    def dout(name, shape):
        return nc.dram_tensor(name, list(shape), F32, kind="ExternalOutput").ap()

    xm = din("xm", [2048, D]); xp = din("xp", [2048, D]); xs = din("xs", [WS, D])
    st_in = din("st", [NSEQ, 8, 128, 128])
    ck_in = din("ck", [NSEQ, 128, 128]); cv_in = din("cv", [NSEQ, 128, 128])
    cmk_in = din("cmk", [NSEQ, 256, 512]); cmv_in = din("cmv", [NSEQ, 256, 512])
    mem_in = din("mem", [256, D]); flag_in = din("flag", [128, 1]); oh_in = din("oh", [32, 128])
    nmix_in = din("nmix", [D]); w_in = din("w_in", [D, 5376]); lbl_in = din("lbl", [2, 1024])
    onw_in = din("onw", [128]); sinks_in = din("sinks", [16]); rb_in = din("rb", [32, 16])
    w_out = din("w_out", [D, D]); nx_in = din("nx", [D]); nmem_in = din("nmem", [D])
    w_mq = din("w_mq", [D, 512]); w_mk = din("w_mk", [D, 512]); w_mv = din("w_mv", [D, 512])
    w_mo = din("w_mo", [512, D]); nffn_in = din("nffn", [D])
    w_gate = din("w_gate", [D, FF]); w_up = din("w_up", [D, FF]); w_down = din("w_down", [FF, D])
    nfin_in = din("nfin", [D])

    y_out = dout("y", [2048, D]); ys_out = dout("ys", [WS, D])
    pst_out = dout("pst", [8, 128, 128]); pk_out = dout("pk", [128, 128]); pv_out = dout("pv", [128, 128])
    pmk_out = dout("pmk", [256, 512]); pmv_out = dout("pmv", [256, 512])
    sst_out = dout("sst", [NSEQ, 8, 128, 128]); ssk_out = dout("ssk", [NSEQ, 128, 128]); ssv_out = dout("ssv", [NSEQ, 128, 128])
    dbg_out = dout("dbg", [5, 128, D]) if cfg.get("dbg") else None

    wsc = nc.dram_tensor("wsc", [NUNIT, 128, 4096], BF16).ap()
    zsc = nc.dram_tensor("zsc", [16, 384], F32).ap()

    P = Prog(nc)

    def sb(name, shape, dt):
        return Buf(es.enter_context(nc.sbuf_tensor(name, list(shape), dt))[:], name)

    NWB = 5
    wbufs = [sb("wb%d" % i, [128, 4096], BF16) for i in range(NWB)]
    xres = es.enter_context(nc.sbuf_tensor("xres", [128, 5, D], F32))[:]
    hT = sb("hT", [128, NK, 576], BF16)
    stg32 = [sb("stg32_%d" % i, [128, 1024], F32) for i in range(2)]
    stg16 = [sb("stg16_%d" % i, [128, 1024], BF16) for i in range(2)]
    xn = sb("xn", [128, D], BF16)
    junk = xn
    ident = sb("ident", [128, 128], BF16)
    ones_bf = sb("ones_bf", [128, 128], BF16)
    nw = sb("nw", [128, 4, NK], F32)
    small = sb("small", [128, 64], F32)
    ARENA1 = 46 * 1024
    ARENA2 = 18 * 1024 + 512
    a1 = Arena(es.enter_context(nc.sbuf_tensor("arena1", [128, ARENA1 // 4], F32)), ARENA1, "a1")
    a2 = Arena(es.enter_context(nc.sbuf_tensor("arena2", [128, ARENA2 // 4], F32)), ARENA2, "a2")

    banks = [Buf(es.enter_context(nc.psum_tensor("bank%d" % i, [128, 512], F32))[:], "bank%d" % i) for i in range(8)]

    def xk(t):
        return "xres%d" % t

    P.op("pool", lambda e: e.memset(ident[:], 0.0), writes=[ident.k])
    P.op("pool", lambda e: e.affine_select(out=ident[:], in_=ident[:], pattern=[[-1, 128]], compare_op=ALU.not_equal,
                                           fill=1.0, base=0, channel_multiplier=1), reads=[ident.k], writes=[ident.k])
    P.op("pool", lambda e: e.memset(ones_bf[:], 1.0), writes=[ones_bf.k])
    for i, src in enumerate([nmix_in, nx_in, nffn_in, nmem_in]):
        P.dma(lambda e, i=i, src=src: e.dma_start(out=nw[:, i, :], in_=src.rearrange("(k p) -> p k", p=128)),
              key="nw%d" % i, writes=[nw.k])

    wstate = {"i": 0}

    def wload(unit):
        b = wbufs[wstate["i"] % NWB]
        wstate["i"] += 1
        P.dma(lambda e, b=b, unit=unit: e.dma_start(out=b[:], in_=wsc[unit]), key=b.k,
              reads=["wsc%d" % unit], writes=[b.k])
        return b

    cast_items = []
    cstate = {"i": 0, "engs": ("act", "dve")}
    CQ = {"act": "act", "dve": "sp", "pool": "pool"}
    cslots = [(stg32[i], stg16[i]) for i in range(2)]

    def cast_copy(eng, o_ap, i_ap, rk, wk):
        if eng == "act":
            P.op("act", lambda e: e.copy(o_ap, i_ap), reads=[rk], writes=[wk])
        else:
            P.op(eng, lambda e: e.tensor_copy(o_ap, i_ap), reads=[rk], writes=[wk])

    def cast_piece(src_ap, ncols, dst_fn, perm=None):
        st = {}

        def stage_a():
            i = cstate["i"]
            cstate["i"] += 1
            st["i"] = i
            st["slot"] = cslots[i % len(cslots)]
            s32 = st["slot"][0]
            P.dma(lambda e: e.dma_start(out=s32[:, 0:ncols], in_=src_ap), key=s32.k, writes=[s32.k])

        def stage_b():
            i = st["i"]
            s32, s16 = st["slot"]
            eng = cstate["engs"][i % len(cstate["engs"])]
            pairs = [(s16[:, 0:ncols], s32[:, 0:ncols])] if perm is None else perm(s16, s32)
            for (o_ap, i_ap) in pairs:
                cast_copy(eng, o_ap, i_ap, s32.k, s16.k)
            for (d_ap, s_ap, units) in dst_fn(s16):
                P.dma(lambda e, d_ap=d_ap, s_ap=s_ap: e.dma_start(out=d_ap, in_=s_ap), key=s16.k + "o" + CQ[eng],
                      reads=[s16.k], writes=["wsc%d" % u for u in units], eng=CQ[eng])
        cast_items.append((stage_a, stage_b))

    def units_ap(u0, nu, col0, ncol):
        return wsc[u0:u0 + nu, :, col0:col0 + ncol].rearrange("u p c -> p u c")

    def gen_cast_items():
        for k in range(NK if cfg.get("mixer", True) else 0):
            rows = slice(k * 128, (k + 1) * 128)
            for piece, ubase in ((0, U_HGA), (1, U_HGB)):
                def perm(s16, s32):
                    return [(s16[:, 0:2048].rearrange("p (h s c) -> p h s c", h=8, s=2),
                             s32[:, 0:2048].rearrange("p (s h c) -> p h s c", h=8, s=2))]
                for hh in range(2):
                    def perm2(s16, s32):
                        return [(s16[:, 0:1024].rearrange("p (h s c) -> p h s c", h=4, s=2),
                                 s32[:, 0:1024].rearrange("p (s h c) -> p h s c", h=4, s=2))]
                    src = w_in[rows, piece * 2048:(piece + 1) * 2048].rearrange("p (s h c) -> p s h c", s=2, h=8)[:, :, hh * 4:(hh + 1) * 4, :]
                    def dst(s16, k=k, ubase=ubase, hh=hh):
                        return [(units_ap(ubase + hh * 4, 4, k * 256, 256),
                                 s16[:, 0:1024].rearrange("p (h c) -> p h c", h=4),
                                 list(range(ubase + hh * 4, ubase + hh * 4 + 4)))]
                    cast_piece3(src, dst, perm2)
            def dst(s16, k=k):
                return [(units_ap(U_SQ, 4, k * 256, 256), s16[:, 0:1024].rearrange("p (u c) -> p u c", u=4),
                         list(range(U_SQ, U_SQ + 4)))]
            cast_piece(w_in[rows, 4096:5120], 1024, dst)
            def perm3(s16, s32):
                return [(s16[:, 0:256], s32[:, 0:256]),
                        (s16[:, 256:512].rearrange("p (kv r c) -> p kv r c", kv=2, r=2),
                         bcast(s32[:, 0:128].rearrange("p (kv r c) -> p kv r c", kv=2, r=1), [128, 2, 2, 64]))]
            def dst(s16, k=k):
                return [(units_ap(U_SKVA, 2, k * 256, 256), s16[:, 0:512].rearrange("p (u c) -> p u c", u=2),
                         [U_SKVA, U_SKVB])]
            cast_piece(w_in[rows, 5120:5376], 256, dst, perm3)
        for (wm, ub) in ((w_mk, U_MK), (w_mv, U_MV), (w_mq, U_MQ)):
            for kk in range(8 if cfg.get("xattn", True) else 0):
                src = wm[kk * 256:(kk + 1) * 256, :].rearrange("(k p) c -> p k c", p=128)
                def dst(s16, kk=kk, ub=ub):
                    v = s16[:, 0:1024].rearrange("p (k u c) -> p k u c", k=2, u=2)
                    return [(wsc[ub + u, :, kk * 512:(kk + 1) * 512].rearrange("p (k c) -> p k c", k=2), v[:, :, u, :], [ub + u])
                            for u in range(2)]
                cast_piece3(src, dst, None)
        for kc in range(NK if (cfg.get("mixer", True) and not DD) else 0):
            for half in range(2):
                src = w_out[kc * 128:(kc + 1) * 128, half * 1024:(half + 1) * 1024]
                def dst(s16, kc=kc, half=half):
                    kh, kl = kc // 8, kc % 8
                    return [(wsc[U_WO + (half * 2 + nn) * 2 + kh, :, kl * 512:(kl + 1) * 512], s16[:, nn * 512:(nn + 1) * 512],
                             [U_WO + (half * 2 + nn) * 2 + kh]) for nn in range(2)]
                cast_piece(src, 1024, dst)
        for kc in range(4 if (cfg.get("xattn", True) and not DD) else 0):
            for half in range(2):
                src = w_mo[kc * 128:(kc + 1) * 128, half * 1024:(half + 1) * 1024]
                def dst(s16, kc=kc, half=half):
                    return [(wsc[U_MO + kc // 2, :, (kc % 2) * 2048 + half * 1024:(kc % 2) * 2048 + (half + 1) * 1024],
                             s16[:, 0:1024], [U_MO + kc // 2])]
                cast_piece(src, 1024, dst)
        for (wm, ub) in ((w_gate, U_FG), (w_up, U_FU)):
            for k in range(NK if cfg.get("ffn", True) else 0):
                for c0 in range(0, FF, 1024):
                    ncol = min(1024, FF - c0)
                    nu = ncol // 256
                    src = wm[k * 128:(k + 1) * 128, c0:c0 + ncol]
                    def dst(s16, k=k, c0=c0, nu=nu, ub=ub, ncol=ncol):
                        return [(units_ap(ub + c0 // 256, nu, k * 256, 256), s16[:, 0:ncol].rearrange("p (u c) -> p u c", u=nu),
                                 list(range(ub + c0 // 256, ub + c0 // 256 + nu)))]
                    cast_piece(src, ncol, dst)
        for r in range(FF // 128 if (cfg.get("ffn", True) and not DD) else 0):
            for half in range(2):
                src = w_down[r * 128:(r + 1) * 128, half * 1024:(half + 1) * 1024]
                def dst(s16, r=r, half=half):
                    j, kk = r // 2, r % 2
                    return [(wsc[U_FD + j, :, kk * 2048 + half * 1024:kk * 2048 + (half + 1) * 1024], s16[:, 0:1024], [U_FD + j])]
                cast_piece(src, 1024, dst)

    def cast_piece3(src_ap3, dst_fn, perm):
        st = {}

        def stage_a():
            i = cstate["i"]
            cstate["i"] += 1
            st["i"] = i
            st["slot"] = cslots[i % len(cslots)]
            s32 = st["slot"][0]
            shp = src_ap3.shape
            if len(shp) == 3:
                dview = s32[:, 0:1024].rearrange("p (a b) -> p a b", a=shp[1])
            else:
                dview = s32[:, 0:1024].rearrange("p (a b c) -> p a b c", a=shp[1], b=shp[2])
            P.dma(lambda e: e.dma_start(out=dview, in_=src_ap3), key=s32.k, writes=[s32.k])

        def stage_b():
            i = st["i"]
            s32, s16 = st["slot"]
            eng = cstate["engs"][i % len(cstate["engs"])]
            pairs = [(s16[:, 0:1024], s32[:, 0:1024])] if perm is None else perm(s16, s32)
            for (o_ap, i_ap) in pairs:
                cast_copy(eng, o_ap, i_ap, s32.k, s16.k)
            for (d_ap, s_ap, units) in dst_fn(s16):
                P.dma(lambda e, d_ap=d_ap, s_ap=s_ap: e.dma_start(out=d_ap, in_=s_ap), key=s16.k + "o" + CQ[eng],
                      reads=[s16.k], writes=["wsc%d" % u for u in units], eng=CQ[eng])
        cast_items.append((stage_a, stage_b))

    DD = cfg.get("dd", True)
    gen_cast_items()
    dd_items = []
    ddc = {"i": 0}

    def dd_add(dst_ap, src_ap, units):
        def item():
            i = ddc["i"]
            ddc["i"] += 1
            P.dma(lambda e: e.dma_start(out=dst_ap, in_=src_ap), key="dd%d" % (i % 8), writes=["wsc%d" % u for u in units], eng="pool")
        dd_items.append(item)

    if DD:
        if cfg.get("mixer", True):
            wo_v = wsc[U_WO:U_WO + 8].rearrange("(n kh) p c -> kh p n c", kh=2)
            for kc in range(NK):
                kh, kl = kc // 8, kc % 8
                dd_add(wo_v[kh][:, :, kl * 512:(kl + 1) * 512], w_out[kc * 128:(kc + 1) * 128, :].rearrange("p (n c) -> p n c", n=4),
                       [U_WO + n * 2 + kh for n in range(4)])
        if cfg.get("xattn", True):
            for kc in range(4):
                dd_add(wsc[U_MO + kc // 2, :, (kc % 2) * 2048:(kc % 2 + 1) * 2048], w_mo[kc * 128:(kc + 1) * 128, :], [U_MO + kc // 2])
        if cfg.get("ffn", True):
            for r in range(FF // 128):
                dd_add(wsc[U_FD + r // 2, :, (r % 2) * 2048:(r % 2 + 1) * 2048], w_down[r * 128:(r + 1) * 128, :], [U_FD + r // 2])

    def do_dd(n=None):
        c = 0
        while dd_items and (n is None or c < n):
            dd_items.pop(0)()
            c += 1

    pending_b = []

    def do_casts(n=None, flush=False):
        cnt = 0
        while (cast_items or pending_b) and (n is None or cnt < n):
            depth = len(cslots) - 1
            while cast_items and len(pending_b) <= depth:
                a_, b_ = cast_items.pop(0)
                a_()
                pending_b.append(b_)
            pending_b.pop(0)()
            cnt += 1
        if flush:
            while pending_b:
                pending_b.pop(0)()

    def norm_tile(src_ap, rows, nwi, col0, src_key, eps_scale=1.0 / D):
        ss = small[0:rows, 0:1]
        rs = small[0:rows, 1:2]
        P.op("act", lambda e: e.activation(junk[0:rows, :], src_ap, AF.Square, accum_out=ss), reads=[src_key], writes=[junk.k, "ss"])
        P.op("act", lambda e: e.activation(rs, ss, AF.Ln, scale=eps_scale, bias=EPS), reads=["ss"], writes=["rs"])
        P.op("act", lambda e: e.activation(rs, rs, AF.Exp, scale=-0.5), reads=["rs"], writes=["rs"])
        P.op("act", lambda e: e.activation(xn[0:rows, :], src_ap, AF.Copy, scale=rs), reads=[src_key, "rs"], writes=[xn.k])
        for g4 in range(4):
            bk = banks[g4 % 2]
            pt = bk[:, 0:256].bitcast(BF16).rearrange("p (a b) -> p a b", a=4)
            for j in range(4):
                k = g4 * 4 + j
                P.op("pe", lambda e, k=k, j=j, pt=pt: e.transpose(pt[:, j, 0:rows], xn[0:rows, k * 128:(k + 1) * 128], ident[0:rows, 0:rows]),
                     reads=[xn.k, ident.k], writes=[bk.k], sig=(j == 3))
            P.op("dve", lambda e, g4=g4, pt=pt: e.tensor_tensor(hT[:, g4 * 4:(g4 + 1) * 4, col0:col0 + rows], pt[:, :, 0:rows],
                                                              bcast(nw[:, nwi, g4 * 4:(g4 + 1) * 4].unsqueeze(2), [128, 4, rows]), ALU.mult),
                 reads=[bk.k, nw.k], writes=[hT.k])

    def resid_add(t, rows, cols, bank):
        P.op("dve", lambda e: e.tensor_tensor(xres[0:rows, t, cols], bank[0:rows, :], xres[0:rows, t, cols], ALU.add),
             reads=[bank.k, xk(t)], writes=[xk(t)])

    def ffn_block(groups):
        W = sum(r for (_, r, _) in groups)
        has_s = W > 512
        a2.reset()
        hTc = [a2.alloc(BF16, [2, 576], "hc%d" % i) for i in range(4)]
        sg = [a2.alloc(BF16, [576], "sg%d" % i) for i in range(2)]
        hci = 0
        cnt = 0
        for jj in range(NCH // 2):
            hs = []
            for sub in range(2):
                j = jj * 2 + sub
                wg = wload(U_FG + j)
                wu = wload(U_FU + j)
                hc = hTc[hci % 4]
                hci += 1
                hs.append(hc)
                for m in range(2):
                    bg, bu, bs_ = banks[(cnt % 2) * 2], banks[(cnt % 2) * 2 + 1], banks[4 + cnt % 2]
                    s_ = sg[cnt % 2]
                    cnt += 1
                    for (wb, bk, so) in ((wg, bg, 0), (wu, bu, 64)):
                        wv = wb[:].rearrange("p (k c) -> p k c", k=NK)
                        for k in range(NK):
                            P.op("pe", lambda e, wv=wv, k=k, m=m, bk=bk: e.matmul(bk[:, 0:512], wv[:, k, m * 128:(m + 1) * 128], hT[:, k, 0:512],
                                                                                   start=(k == 0), stop=(k == NK - 1)),
                                 reads=[wb.k, hT.k], writes=[bk.k], sig=(k == NK - 1))
                        if has_s:
                            for k in range(NK):
                                P.op("pe", lambda e, wv=wv, k=k, m=m, so=so, bs_=bs_: e.matmul(bs_[:, so:so + 64], wv[:, k, m * 128:(m + 1) * 128], hT[:, k, 512:576],
                                                                                                 start=(k == 0), stop=(k == NK - 1)),
                                     reads=[wb.k, hT.k], writes=[bs_.k], sig=(k == NK - 1))
                    P.op("act", lambda e, s_=s_, bg=bg: e.activation(s_[:, 0:512], bg[:, 0:512], AF.Silu), reads=[bg.k], writes=[s_.k])
                    P.op("dve", lambda e, s_=s_, bu=bu, hc=hc, m=m: e.tensor_tensor(hc[:, m, 0:512], bu[:, 0:512], s_[:, 0:512], ALU.mult),
                         reads=[bu.k, s_.k], writes=[hc.k])
                    if has_s:
                        P.op("act", lambda e, s_=s_, bs_=bs_: e.activation(s_[:, 512:576], bs_[:, 0:64], AF.Silu), reads=[bs_.k], writes=[s_.k + "s"])
                        P.op("dve", lambda e, s_=s_, bs_=bs_, hc=hc, m=m: e.tensor_tensor(hc[:, m, 512:576], bs_[:, 64:128], s_[:, 512:576], ALU.mult),
                             reads=[bs_.k, s_.k + "s"], writes=[hc.k])
            wds = [wload(U_FD + jj * 2 + sub) for sub in range(2)]
            dcnt = 0
            for (t, rows, c0) in groups:
                for n in range(4):
                    bk = banks[6 + dcnt % 2]
                    dcnt += 1
                    idx = 0
                    for sub in range(2):
                        wdv = wds[sub][:].rearrange("p (k c) -> p k c", k=2)
                        for kk in range(2):
                            P.op("pe", lambda e, hc=hs[sub], kk=kk, wdv=wdv, n=n, bk=bk, idx=idx, c0=c0, rows=rows:
                                 e.matmul(bk[0:rows, :], hc[:, kk, c0:c0 + rows], wdv[:, kk, n * 512:(n + 1) * 512], start=(idx == 0), stop=(idx == 3)),
                                 reads=[hs[sub].k, wds[sub].k], writes=[bk.k], sig=(idx == 3))
                            idx += 1
                    resid_add(t, rows, slice(n * 512, (n + 1) * 512), bk)


    mkT = sb("mkT", [128, 4, 256], BF16)
    mv_sb = sb("mv_sb", [128, 2, 512], BF16)

    def mem_kv():
        a1.reset()
        for t in range(2):
            P.dma(lambda e, t=t: e.dma_start(out=xres[:, t, :], in_=mem_in[t * 128:(t + 1) * 128, :]), key=xk(t), writes=[xk(t)])
        for t in range(2):
            norm_tile(xres[:, t, :], 128, 3, t * 128, xk(t))
        ost = [a1.alloc(F32, [256], "ost%d" % i) for i in range(2)]
        oc = 0
        for (ub, dst, is_k) in ((U_MK, pmk_out, True), (U_MV, pmv_out, False)):
            for u in range(2):
                wb = wload(ub + u)
                wv = wb[:].rearrange("p (k c) -> p k c", k=NK)
                if is_k:
                    for mt in range(2):
                        bk = banks[2 + mt]
                        for k in range(NK):
                            P.op("pe", lambda e, wv=wv, k=k, mt=mt, bk=bk: e.matmul(bk[:, 0:256], wv[:, k, mt * 128:(mt + 1) * 128], hT[:, k, 0:256],
                                                                                   start=(k == 0), stop=(k == NK - 1)),
                                 reads=[wb.k, hT.k], writes=[bk.k], sig=(k == NK - 1))
                        P.op("act", lambda e, bk=bk, u=u, mt=mt: e.copy(mkT[:, u * 2 + mt, :], bk[:, 0:256]), reads=[bk.k], writes=[mkT.k])
                for mtile in range(2):
                    bk = banks[4 + mtile]
                    for k in range(NK):
                        P.op("pe", lambda e, wv=wv, k=k, mtile=mtile, bk=bk: e.matmul(bk[:, 0:256], hT[:, k, mtile * 128:(mtile + 1) * 128], wv[:, k, :],
                                                                                     start=(k == 0), stop=(k == NK - 1)),
                             reads=[wb.k, hT.k], writes=[bk.k], sig=(k == NK - 1))
                    o = ost[oc % 2]
                    oc += 1
                    P.op("dve", lambda e, o=o, bk=bk: e.tensor_copy(o[:], bk[:, 0:256]), reads=[bk.k], writes=[o.k])
                    if not is_k:
                        P.op("act", lambda e, bk=bk, mtile=mtile, u=u: e.copy(mv_sb[:, mtile, u * 256:(u + 1) * 256], bk[:, 0:256]), reads=[bk.k], writes=[mv_sb.k])
                    P.dma(lambda e, o=o, dst=dst, mtile=mtile, u=u: e.dma_start(out=dst[mtile * 128:(mtile + 1) * 128, u * 256:(u + 1) * 256], in_=o[:]),
                          key=o.k + "o", reads=[o.k], eng="pool")

    def xattn_block(groups):
        W = sum(r for (_, r, _) in groups)
        has_s = W > 512
        a1.reset(); a2.reset()
        qT = a2.alloc(BF16, [4, 576], "qT")
        oxT = a2.alloc(BF16, [4, 576], "oxT")
        sc_ = 128.0 ** -0.5
        for u in range(2):
            wb = wload(U_MQ + u)
            wv = wb[:].rearrange("p (k c) -> p k c", k=NK)
            for mt in range(2):
                h = u * 2 + mt
                bk = banks[2 + mt]
                for k in range(NK):
                    P.op("pe", lambda e, wv=wv, k=k, mt=mt, bk=bk: e.matmul(bk[:, 0:512], wv[:, k, mt * 128:(mt + 1) * 128], hT[:, k, 0:512],
                                                                           start=(k == 0), stop=(k == NK - 1)),
                         reads=[wb.k, hT.k], writes=[bk.k], sig=(k == NK - 1))
                P.op("act", lambda e, bk=bk, h=h: e.activation(qT[:, h, 0:512], bk[:, 0:512], AF.Copy, scale=sc_), reads=[bk.k], writes=[qT.k])
                if has_s:
                    bs_ = banks[4]
                    ks = bs_.k
                    for k in range(NK):
                        P.op("pe", lambda e, wv=wv, k=k, mt=mt, h=h: e.matmul(bs_[:, h * 64:(h + 1) * 64], wv[:, k, mt * 128:(mt + 1) * 128], hT[:, k, 512:576],
                                                                             start=(k == 0), stop=(k == NK - 1)),
                             reads=[wb.k, hT.k], writes=[ks], sig=(k == NK - 1))
                    P.op("act", lambda e, h=h: e.activation(qT[:, h, 512:576], bs_[:, h * 64:(h + 1) * 64], AF.Copy, scale=sc_), reads=[ks], writes=[qT.k])
        xp_ = cfg.get("x_parts", "memkv,q,attn,mo")
        pT = [a1.alloc(BF16, [512], "pT%d" % i) for i in range(4)]
        rcp = [a1.alloc(F32, [512], "rcp%d" % i) for i in range(2)]
        pc = 0
        for h in range(4 if "attn" in xp_ else 0):
            pts = []
            for mb in range(2):
                bk = banks[pc % 2]
                p_ = pT[pc % 4]
                pc += 1
                P.op("pe", lambda e, bk=bk, h=h, mb=mb: e.matmul(bk[:, 0:512], mkT[:, h, mb * 128:(mb + 1) * 128], qT[:, h, 0:512], start=True, stop=True),
                     reads=[mkT.k, qT.k], writes=[bk.k])
                P.op("act", lambda e, bk=bk, p_=p_: e.activation(p_[:], bk[:, 0:512], AF.Exp), reads=[bk.k], writes=[p_.k])
                pts.append(p_)
            bo, bsum = banks[2 + (h % 2) * 2], banks[3 + (h % 2) * 2]
            for mb in range(2):
                P.op("pe", lambda e, bo=bo, h=h, mb=mb, p_=pts[mb]: e.matmul(bo[:, 0:512], mv_sb[:, mb, h * 128:(h + 1) * 128], p_[:], start=(mb == 0), stop=(mb == 1)),
                     reads=[mv_sb.k, pts[mb].k], writes=[bo.k], sig=(mb == 1))
            for mb in range(2):
                P.op("pe", lambda e, bsum=bsum, mb=mb, p_=pts[mb]: e.matmul(bsum[:, 0:512], ones_bf[:], p_[:], start=(mb == 0), stop=(mb == 1)),
                     reads=[ones_bf.k, pts[mb].k], writes=[bsum.k], sig=(mb == 1))
            r_ = rcp[h % 2]
            P.op("dve", lambda e, r_=r_, bsum=bsum: e.reciprocal(r_[:], bsum[:, 0:512]), reads=[bsum.k], writes=[r_.k])
            P.op("dve", lambda e, r_=r_, bo=bo, h=h: e.tensor_tensor(oxT[:, h, 0:512], bo[:, 0:512], r_[:], ALU.mult), reads=[bo.k, r_.k], writes=[oxT.k])
        if has_s:
            ckf = [a1.alloc(F32, [2, 512], "ckf%d" % i) for i in range(2)]
            cvf = [a1.alloc(F32, [2, 512], "cvf%d" % i) for i in range(2)]
            ckb = [a1.alloc(BF16, [2, 512], "ckb%d" % i) for i in range(2)]
            cvb = [a1.alloc(BF16, [2, 512], "cvb%d" % i) for i in range(2)]
            kTn = [a1.alloc(BF16, [4, 256], "kTn%d" % i) for i in range(2)]
            pTn = [a1.alloc(BF16, [32], "pTn%d" % i) for i in range(2)]
            rcs = [a1.alloc(F32, [16], "rcs%d" % i) for i in range(2)]
            for n in range(NSEQ):
                i2 = n % 2
                bsc, bpv, btr = (banks[5], banks[6], banks[7]) if i2 == 0 else (banks[1], banks[2], banks[0])
                P.dma(lambda e, n=n, i2=i2: e.dma_start(out=ckf[i2][:], in_=cmk_in[n].rearrange("(mb p) c -> p mb c", p=128)), key=ckf[i2].k, writes=[ckf[i2].k])
                P.dma(lambda e, n=n, i2=i2: e.dma_start(out=cvf[i2][:], in_=cmv_in[n].rearrange("(mb p) c -> p mb c", p=128)), key=cvf[i2].k, writes=[cvf[i2].k])
                P.op("pool", lambda e, i2=i2: e.tensor_copy(ckb[i2][:], ckf[i2][:]), reads=[ckf[i2].k], writes=[ckb[i2].k])
                P.op("pool", lambda e, i2=i2: e.tensor_copy(cvb[i2][:], cvf[i2][:]), reads=[cvf[i2].k], writes=[cvb[i2].k])
                ktp = btr[:, 0:512].bitcast(BF16).rearrange("p (h m) -> p h m", h=4)
                for h in range(4):
                    for mb in range(2):
                        P.op("pe", lambda e, h=h, mb=mb, i2=i2, ktp=ktp: e.transpose(ktp[:, h, mb * 128:(mb + 1) * 128], ckb[i2][:, mb, h * 128:(h + 1) * 128], ident[:]),
                             reads=[ckb[i2].k, ident.k], writes=[btr.k], sig=(h == 3 and mb == 1))
                P.op("act", lambda e, i2=i2, ktp=ktp: e.copy(kTn[i2][:], ktp), reads=[btr.k], writes=[kTn[i2].k])
                ksc = bsc.k
                for h in range(4):
                    for mb in range(2):
                        c = n * 32 + (h * 2 + mb) * 4
                        P.op("pe", lambda e, h=h, mb=mb, i2=i2, c=c, n=n, bsc=bsc: e.matmul(bsc[:, c:c + 4], kTn[i2][:, h, mb * 128:(mb + 1) * 128], qT[:, h, 512 + 4 * n:516 + 4 * n],
                                                                                   start=True, stop=True),
                             reads=[kTn[i2].k, qT.k], writes=[ksc], sig=(h == 3 and mb == 1))
                P.op("act", lambda e, n=n, i2=i2, bsc=bsc: e.activation(pTn[i2][:], bsc[:, n * 32:(n + 1) * 32], AF.Exp), reads=[ksc], writes=[pTn[i2].k])
                kpv = bpv.k
                for h in range(4):
                    for mb in range(2):
                        P.op("pe", lambda e, h=h, mb=mb, i2=i2, n=n, bpv=bpv: e.matmul(bpv[:, n * 16 + h * 4:n * 16 + h * 4 + 4], cvb[i2][:, mb, h * 128:(h + 1) * 128],
                                                                             pTn[i2][:, (h * 2 + mb) * 4:(h * 2 + mb) * 4 + 4], start=(mb == 0), stop=(mb == 1)),
                             reads=[cvb[i2].k, pTn[i2].k], writes=[kpv], sig=False)
                for h in range(4):
                    for mb in range(2):
                        P.op("pe", lambda e, h=h, mb=mb, i2=i2, n=n, bpv=bpv: e.matmul(bpv[:, 256 + n * 16 + h * 4:256 + n * 16 + h * 4 + 4], ones_bf[:],
                                                                             pTn[i2][:, (h * 2 + mb) * 4:(h * 2 + mb) * 4 + 4], start=(mb == 0), stop=(mb == 1)),
                             reads=[ones_bf.k, pTn[i2].k], writes=[kpv], sig=(h == 3 and mb == 1))
                P.op("dve", lambda e, n=n, i2=i2, bpv=bpv: e.reciprocal(rcs[i2][:], bpv[:, 256 + n * 16:256 + (n + 1) * 16]), reads=[kpv], writes=[rcs[i2].k])
                P.op("dve", lambda e, n=n, i2=i2, bpv=bpv: e.tensor_tensor(oxT[:, :, 512 + 4 * n:516 + 4 * n], bpv[:, n * 16:(n + 1) * 16].rearrange("p (h t) -> p h t", h=4),
                                                                  rcs[i2][:].rearrange("p (h t) -> p h t", h=4), ALU.mult),
                     reads=[kpv, rcs[i2].k], writes=[oxT.k])
        wmo = [wload(U_MO + i) for i in range(2)]
        dcnt = 0
        for (t, rows, c0) in (groups if "mo" in xp_ else []):
            for n in range(4):
                bk = banks[dcnt % 2]
                dcnt += 1
                for kc in range(4):
                    wv = wmo[kc // 2][:].rearrange("p (k c) -> p k c", k=2)
                    P.op("pe", lambda e, kc=kc, wv=wv, n=n, bk=bk, c0=c0, rows=rows: e.matmul(bk[0:rows, :], oxT[:, kc, c0:c0 + rows], wv[:, kc % 2, n * 512:(n + 1) * 512],
                                                                                             start=(kc == 0), stop=(kc == 3)),
                         reads=[oxT.k, wmo[kc // 2].k], writes=[bk.k], sig=(kc == 3))
                resid_add(t, rows, slice(n * 512, (n + 1) * 512), bk)

    Sst = sb("Sst", [128, 8, 128], F32)
    lb = sb("lb_sb", [128, 8], F32)
    onw = sb("onw_sb", [128, 1], F32)

    def setup_hgrn():
        a1.reset()
        lt = a1.alloc(F32, [2, 8], "lt")
        P.dma(lambda e: e.dma_start(out=lt[:], in_=lbl_in.rearrange("s (h c) -> c s h", c=128)), key="lt", writes=[lt.k])
        P.dma(lambda e: e.dma_start(out=onw[:], in_=onw_in.rearrange("(p o) -> p o", o=1)), key="onw", writes=[onw.k])
        P.op("dve", lambda e: e.tensor_tensor(lb[:], lt[:, 1, :], lt[:, 0, :], ALU.subtract), reads=[lt.k], writes=[lb.k])
        P.op("act", lambda e: e.activation(lb[:], lb[:], AF.Exp), reads=[lb.k], writes=[lb.k])
        P.op("dve", lambda e: e.tensor_scalar(lb[:], lb[:], 1.0, None, ALU.add), reads=[lb.k], writes=[lb.k])
        P.op("dve", lambda e: e.reciprocal(lb[:], lb[:]), reads=[lb.k], writes=[lb.k])
        P.op("pool", lambda e: e.memset(Sst[:], 0.0), writes=[Sst.k])

    def hgrn_alloc(has_s):
        a1.reset()
        B = {}
        B["e"] = a1.alloc(F32, [576], "e")
        B["a"] = a1.alloc(F32, [576], "a")
        B["b"] = a1.alloc(F32, [576], "b")
        B["tmp"] = [a1.alloc(F32, [512], "tmp%d" % i) for i in range(2)]
        B["tmp2"] = [a1.alloc(BF16, [576], "tmp2%d" % i) for i in range(2)]
        B["KV"] = a1.alloc(BF16, [9, 512], "KV")
        B["Qh"] = a1.alloc(BF16, [512], "Qh")
        B["Qt"] = a1.alloc(BF16, [576], "Qt")
        B["vsb"] = a1.alloc(BF16, [4, 128], "vsb")
        B["KupT"] = a1.alloc(BF16, [4, 128], "KupT")
        B["Sbf"] = a1.alloc(BF16, [4, 128], "Sbf")
        B["At"] = a1.alloc(BF16, [512], "At")
        B["Rb"] = a1.alloc(F32, [4, 9], "Rb")
        B["egt"] = a1.alloc(F32, [4], "egt")
        P.op("pool", lambda e: e.memset(B["KV"][:], 0.0), writes=[B["KV"].k])
        P.op("pool", lambda e: e.memset(B["Rb"][:], 0.0), writes=[B["Rb"].k])
        if has_s:
            B["Ks0"] = a1.alloc(BF16, [64], "Ks0")
            B["Kups"] = a1.alloc(BF16, [64], "Kups")
            B["KupTs"] = a1.alloc(BF16, [128], "KupTs", rows=64)
            B["Kpad"] = a1.alloc(BF16, [16, 128], "Kpad", rows=64)
            B["vs"] = a1.alloc(BF16, [128], "vs", rows=64)
            B["Ats"] = a1.alloc(BF16, [64], "Ats", rows=64)
            B["S0"] = a1.alloc(F32, [8, 128], "S0")
            B["S0bf"] = a1.alloc(BF16, [8, 128], "S0bf")
            B["Sn"] = a1.alloc(F32, [8, 128], "Sn")
            B["egts"] = a1.alloc(F32, [16], "egts")
            B["osi"] = a1.alloc(F32, [64], "osi")
            B["os"] = a1.alloc(F32, [64], "os")
        return B

    def onorm_gate(B, o_src, o_key, g_src, g_key, cols, out_ap, out_key, bsum):
        n = cols.stop - cols.start
        sq, r_, e2, t1 = B["tmp2"][0], B["e"], B["b"], B["a"]
        P.op("act", lambda e: e.activation(sq[:, cols], o_src, AF.Square), reads=[o_key], writes=[sq.k])
        P.op("pe", lambda e: e.matmul(bsum[:, 0:n], ones_bf[:], sq[:, cols], start=True, stop=True), reads=[ones_bf.k, sq.k], writes=[bsum.k])
        P.op("act", lambda e: e.activation(r_[:, cols], bsum[:, 0:n], AF.Ln, scale=1.0 / 128, bias=EPS), reads=[bsum.k], writes=[r_.k])
        P.op("act", lambda e: e.activation(r_[:, cols], r_[:, cols], AF.Exp, scale=-0.5), reads=[r_.k], writes=[r_.k])
        P.op("act", lambda e: e.activation(e2[:, cols], g_src, AF.Exp, scale=-1.0), reads=[g_key], writes=[e2.k])
        P.op("dve", lambda e: e.tensor_scalar(e2[:, cols], e2[:, cols], 1.0, None, ALU.add), reads=[e2.k], writes=[e2.k])
        P.op("dve", lambda e: e.reciprocal(e2[:, cols], e2[:, cols]), reads=[e2.k], writes=[e2.k])
        P.op("dve", lambda e: e.scalar_tensor_tensor(t1[:, cols], o_src, onw[:, 0:1], r_[:, cols], ALU.mult, ALU.mult), reads=[o_key, onw.k, r_.k], writes=[t1.k])
        P.op("dve", lambda e: e.tensor_tensor(e2[:, cols], g_src, e2[:, cols], ALU.mult), reads=[g_key, e2.k], writes=[e2.k])
        P.op("dve", lambda e: e.tensor_tensor(out_ap, t1[:, cols], e2[:, cols], ALU.mult), reads=[t1.k, e2.k], writes=[out_key])

    def hgrn_wload(h):
        return (wload(U_HGA + h), wload(U_HGB + h))

    def hgrn_head(B, h, has_s, abT, state_only, blk, wts):
        W = 576 if has_s else 512
        bq, bf, bg, bi, bsmp, bA, bo, bst = banks
        e_, a_, b_ = B["e"], B["a"], B["b"]
        KV, Rb, egt = B["KV"], B["Rb"], B["egt"]
        wa, wb2 = wts
        wav = wa[:].rearrange("p (k c) -> p k c", k=NK)
        wbv = wb2[:].rearrange("p (k c) -> p k c", k=NK)

        def proj_fm(bank, wv, wk, c0, ncol0, ncol1, hcols):
            for k in range(NK):
                P.op("pe", lambda e, k=k: e.matmul(bank[:, ncol0:ncol1], wv[:, k, c0:c0 + 128], hT[:, k, hcols], start=(k == 0), stop=(k == NK - 1)),
                     reads=[wk, hT.k], writes=[bank.k], sig=(k == NK - 1))
        if not state_only:
            proj_fm(bq, wav, wa.k, 0, 0, 512, slice(0, 512))
        proj_fm(bf, wav, wa.k, 128, 0, 512, slice(0, 512))
        if not state_only:
            proj_fm(bg, wbv, wb2.k, 128, 0, 512, slice(0, 512))
        for t in range(4):
            for k in range(NK):
                P.op("pe", lambda e, k=k, t=t: e.matmul(bi[:, t * 128:(t + 1) * 128], hT[:, k, t * 128:(t + 1) * 128], wbv[:, k, 0:128], start=(k == 0), stop=(k == NK - 1)),
                     reads=[wb2.k, hT.k], writes=[bi.k], sig=(k == NK - 1 and t == 3))
        if has_s:
            proj_fm(bsmp, wav, wa.k, 0, 0, 64, slice(512, 576))
            proj_fm(bsmp, wav, wa.k, 128, 64, 128, slice(512, 576))
            proj_fm(bsmp, wbv, wb2.k, 128, 128, 192, slice(512, 576))
            for k in range(NK):
                P.op("pe", lambda e, k=k: e.matmul(bsmp[0:64, 192:320], hT[:, k, 512:576], wbv[:, k, 0:128], start=(k == 0), stop=(k == NK - 1)),
                     reads=[wb2.k, hT.k], writes=[bsmp.k], sig=(k == NK - 1))
        P.op("act", lambda e: e.activation(e_[:, 0:512], bf[:, 0:512], AF.Exp, scale=-1.0), reads=[bf.k], writes=[e_.k])
        if has_s:
            P.op("act", lambda e: e.activation(e_[:, 512:576], bsmp[:, 64:128], AF.Exp, scale=-1.0), reads=[bsmp.k], writes=[e_.k])
        P.op("act", lambda e: e.activation(a_[:, 0:W], e_[:, 0:W], AF.Ln, scale=lb[:, h:h + 1], bias=1.0), reads=[e_.k, lb.k], writes=[a_.k])
        P.op("act", lambda e: e.activation(b_[:, 0:W], e_[:, 0:W], AF.Ln, bias=1.0), reads=[e_.k], writes=[b_.k])
        P.op("dve", lambda e: e.tensor_tensor(a_[:, 0:W], a_[:, 0:W], b_[:, 0:W], ALU.subtract), reads=[a_.k, b_.k], writes=[a_.k])
        P.op("act", lambda e: e.activation(b_[:, 0:W], a_[:, 0:W], AF.Exp), reads=[a_.k], writes=[b_.k])
        P.op("dve", lambda e: e.tensor_scalar(b_[:, 0:W], b_[:, 0:W], -1.0, 1.0, ALU.mult, ALU.add), reads=[b_.k], writes=[b_.k])
        P.op("dve", lambda e: e.tensor_tensor_scan(e_[:, 0:W], rmask[:, 0:W], a_[:, 0:W], 0.0, ALU.mult, ALU.add), reads=[rmask.k, a_.k], writes=[e_.k])
        Gl = e_[:, 0:512].rearrange("p (t s) -> p t s", s=128)
        kk = b_[:, 0:512].rearrange("p (t s) -> p t s", s=128)
        P.op("pool", lambda e: e.tensor_copy(Rb[:, :, 1:9], e_[:, 0:512].rearrange("p (t i u) -> p t i u", t=4, u=16)[:, :, :, 15]), reads=[e_.k], writes=[Rb.k])
        vi = 0
        for i in (range(9) if not state_only else [8]):
            L = 16 * (i + 1) if i < 8 else 128
            tm, t2 = B["tmp"][vi % 2], B["tmp2"][vi % 2]
            vi += 1
            tmv = tm[:].rearrange("p (t s) -> p t s", s=128)[:, :, 0:L]
            t2v = t2[:, 0:512].rearrange("p (t s) -> p t s", s=128)[:, :, 0:L]
            P.op("dve", lambda e, i=i, L=L, tmv=tmv: e.tensor_tensor(tmv, bcast(Rb[:, :, i:i + 1], [128, 4, L]), Gl[:, :, 0:L], ALU.subtract), reads=[Rb.k, e_.k], writes=[tm.k])
            P.op("act", lambda e, tmv=tmv, t2v=t2v: e.activation(t2v, tmv, AF.Exp), reads=[tm.k], writes=[t2.k])
            P.op("pool", lambda e, i=i, L=L, t2v=t2v: e.tensor_tensor(KV[:, i, :].rearrange("p (t s) -> p t s", s=128)[:, :, 0:L], t2v, kk[:, :, 0:L], ALU.mult),
                 reads=[t2.k, b_.k], writes=[KV.k])
        P.op("act", lambda e: e.activation(egt[:], Rb[:, :, 8], AF.Exp), reads=[Rb.k], writes=[egt.k])
        vsb, KupT, Sbf, At = B["vsb"], B["KupT"], B["Sbf"], B["At"]
        P.op("act", lambda e: e.copy(vsb[:].rearrange("p t c -> p (t c)"), bi[:, 0:512]), reads=[bi.k], writes=[vsb.k])
        tv = bA[:, 0:256].bitcast(BF16).rearrange("p (a b) -> p a b", a=4)
        for t in range(4):
            P.op("pe", lambda e, t=t: e.transpose(tv[:, t, :], KV[:, 8, t * 128:(t + 1) * 128], ident[:]), reads=[KV.k, ident.k], writes=[bA.k], sig=(t == 3))
        P.op("act", lambda e: e.copy(KupT[:], tv), reads=[bA.k], writes=[KupT.k])
        for t in range(4):
            if not state_only:
                P.op("act", lambda e, t=t: e.copy(Sbf[:, t, :], Sst[:, h, :]), reads=[Sst.k + str(h)], writes=[Sbf.k])
            P.op("pe", lambda e, t=t: e.matmul(bst[:, 0:128], KupT[:, t, :], vsb[:, t, :], start=True, stop=True), reads=[KupT.k, vsb.k], writes=[bst.k])
            P.op("dve", lambda e, t=t: e.scalar_tensor_tensor(Sst[:, h, :], Sst[:, h, :], egt[:, t:t + 1], bst[:, 0:128], ALU.mult, ALU.add),
                 reads=[Sst.k + str(h), egt.k, bst.k], writes=[Sst.k + str(h)])
        if blk == 3 and not state_only:
            P.dma(lambda e: e.dma_start(out=pst_out[h], in_=Sst[:, h, :]), key="pst", reads=[Sst.k + str(h)], eng="pool")
        if state_only:
            return
        Qh, Qt = B["Qh"], B["Qt"]
        tm, t2 = B["tmp"][vi % 2], B["tmp2"][vi % 2]
        vi += 1
        P.op("dve", lambda e: e.tensor_tensor(tm[:].rearrange("p (t i u) -> p t i u", t=4, u=16), e_[:, 0:512].rearrange("p (t i u) -> p t i u", t=4, u=16),
                                              bcast(Rb[:, :, 0:8].unsqueeze(3), [128, 4, 8, 16]), ALU.subtract), reads=[e_.k, Rb.k], writes=[tm.k])
        P.op("act", lambda e: e.activation(t2[:, 0:512], tm[:], AF.Exp), reads=[tm.k], writes=[t2.k])
        P.op("dve", lambda e: e.tensor_tensor(Qh[:], bq[:, 0:512], t2[:, 0:512], ALU.mult), reads=[bq.k, t2.k], writes=[Qh.k])
        t2b = B["tmp2"][vi % 2]
        vi += 1
        P.op("act", lambda e: e.activation(t2b[:, 0:W], e_[:, 0:W], AF.Exp), reads=[e_.k], writes=[t2b.k])
        P.op("dve", lambda e: e.tensor_tensor(Qt[:, 0:512], bq[:, 0:512], t2b[:, 0:512], ALU.mult), reads=[bq.k, t2b.k], writes=[Qt.k])
        if has_s:
            P.op("dve", lambda e: e.tensor_tensor(Qt[:, 512:576], bsmp[:, 0:64], t2b[:, 512:576], ALU.mult), reads=[bsmp.k, t2b.k], writes=[Qt.k])
        for t in range(4):
            for i in range(8):
                c = t * 128 + i * 16
                P.op("pe", lambda e, t=t, i=i, c=c: e.matmul(bA[:, c:c + 16], KV[:, i, t * 128:(t + 1) * 128], Qh[:, c:c + 16], start=True, stop=True),
                     reads=[KV.k, Qh.k], writes=[bA.k], sig=(t == 3 and i == 7))
        P.op("dve", lambda e: e.tensor_tensor(At[:].rearrange("p (t s) -> p t s", s=128), bA[:, 0:512].rearrange("p (t s) -> p t s", s=128),
                                              bcast(cm01[:].unsqueeze(1), [128, 4, 128]), ALU.mult), reads=[bA.k, cm01.k], writes=[At.k])
        for t in range(4):
            P.op("pe", lambda e, t=t: e.matmul(bo[:, t * 128:(t + 1) * 128], Sbf[:, t, :], Qt[:, t * 128:(t + 1) * 128], start=True, stop=False),
                 reads=[Sbf.k, Qt.k], writes=[bo.k], sig=False)
            P.op("pe", lambda e, t=t: e.matmul(bo[:, t * 128:(t + 1) * 128], vsb[:, t, :], At[:, t * 128:(t + 1) * 128], start=False, stop=True),
                 reads=[vsb.k, At.k], writes=[bo.k], sig=(t == 3))
        onorm_gate(B, bo[:, 0:512], bo.k, bg[:, 0:512], bg.k, slice(0, 512), abT[:, h, 0:512], abT.k, bA)
        if has_s:
            hgrn_sample(B, h, abT)

    def hgrn_sample(B, h, abT):
        bq, bf, bg, bi, bsmp, bA, bo, bst = banks
        e_, b_ = B["e"], B["b"]
        Qt = B["Qt"]
        Gs = e_[:, 512:576].rearrange("p (n t) -> p n t", t=4)
        ks = b_[:, 512:576]
        tm, t2 = B["tmp"][0], B["tmp2"][1]
        Ks0, Kups, KupTs, Kpad, vs, Ats = B["Ks0"], B["Kups"], B["KupTs"], B["Kpad"], B["vs"], B["Ats"]
        S0, S0bf, Sn, egts, osi, os_ = B["S0"], B["S0bf"], B["Sn"], B["egts"], B["osi"], B["os"]
        P.op("act", lambda e: e.activation(t2[:, 0:64], e_[:, 512:576], AF.Exp, scale=-1.0), reads=[e_.k], writes=[t2.k])
        P.op("pool", lambda e: e.tensor_tensor(Ks0[:], t2[:, 0:64], ks, ALU.mult), reads=[t2.k, b_.k], writes=[Ks0.k])
        P.op("dve", lambda e: e.tensor_tensor(tm[:, 0:64].rearrange("p (n t) -> p n t", t=4), bcast(Gs[:, :, 3:4], [128, 16, 4]), Gs, ALU.subtract), reads=[e_.k], writes=[tm.k])
        P.op("act", lambda e: e.activation(t2[:, 64:128], tm[:, 0:64], AF.Exp), reads=[tm.k], writes=[t2.k])
        P.op("pool", lambda e: e.tensor_tensor(Kups[:], t2[:, 64:128], ks, ALU.mult), reads=[t2.k, b_.k], writes=[Kups.k])
        P.op("act", lambda e: e.activation(egts[:], Gs[:, :, 3], AF.Exp), reads=[e_.k], writes=[egts.k])
        P.op("act", lambda e: e.copy(vs[:], bsmp[0:64, 192:320]), reads=[bsmp.k], writes=[vs.k])
        P.op("pe", lambda e: e.matmul(bst[0:64, 128:192], Ks0[:], Qt[:, 512:576], start=True, stop=True), reads=[Ks0.k, Qt.k], writes=[bst.k])
        P.op("dve", lambda e: e.tensor_tensor(Ats[:], bst[0:64, 128:192], bdc[:], ALU.mult), reads=[bst.k, bdc.k], writes=[Ats.k])
        tvs = bst[0:64, 256:320].bitcast(BF16)
        P.op("pe", lambda e: e.transpose(tvs, Kups[:], ident[:]), reads=[Kups.k, ident.k], writes=[bst.k])
        P.op("act", lambda e: e.copy(KupTs[:], tvs), reads=[bst.k], writes=[KupTs.k])
        P.op("dve", lambda e: e.tensor_tensor(Kpad[:], bcast(KupTs[:].unsqueeze(1), [64, 16, 128]), bcast(msn[:].unsqueeze(2), [64, 16, 128]), ALU.mult),
             reads=[KupTs.k, msn.k], writes=[Kpad.k])
        P.op("pe", lambda e: e.matmul(bo[:, 64:128], vs[:], Ats[:], start=True, stop=True), reads=[vs.k, Ats.k], writes=[bo.k])
        P.op("act", lambda e: e.copy(osi[:], bo[:, 64:128]), reads=[bo.k], writes=[osi.k])
        for half in range(2):
            n0 = half * 8
            P.dma(lambda e, n0=n0: e.dma_start(out=S0[:], in_=st_in[n0:n0 + 8, h].rearrange("n c e -> c n e")), key=S0.k, writes=[S0.k])
            P.op("pool", lambda e: e.tensor_copy(S0bf[:], S0[:]), reads=[S0.k], writes=[S0bf.k])
            for i in range(8):
                n = n0 + i
                P.op("pe", lambda e, i=i, n=n: e.matmul(bo[:, 4 * n:4 * n + 4], S0bf[:, i, :], Qt[:, 512 + 4 * n:516 + 4 * n], start=True, stop=True),
                     reads=[S0bf.k, Qt.k], writes=[bo.k], sig=(i == 7))
            for i in range(8):
                n = n0 + i
                bk = (bq, bf)[i % 2]
                P.op("pe", lambda e, n=n, bk=bk: e.matmul(bk[:, 0:128], Kpad[:, n, :], vs[:], start=True, stop=True), reads=[Kpad.k, vs.k], writes=[bk.k])
                P.op("dve", lambda e, i=i, n=n, bk=bk: e.scalar_tensor_tensor(Sn[:, i, :], S0[:, i, :], egts[:, n:n + 1], bk[:, 0:128], ALU.mult, ALU.add),
                     reads=[S0.k, egts.k, bk.k], writes=[Sn.k])
            P.dma(lambda e, n0=n0: e.dma_start(out=sst_out[n0:n0 + 8, h].rearrange("n c e -> c n e"), in_=Sn[:]), key=Sn.k + "o", reads=[Sn.k], eng="pool")
        P.op("dve", lambda e: e.tensor_tensor(os_[:], bo[:, 0:64], osi[:], ALU.add), reads=[bo.k, osi.k], writes=[os_.k])
        onorm_gate(B, os_[:], os_.k, bsmp[:, 128:192], bsmp.k, slice(512, 576), abT[:, h, 512:576], abT.k, bA)

    def hgrn_block(blk, has_s, abT):
        B = hgrn_alloc(has_s)
        wts = hgrn_wload(0)
        for h in range(8):
            nxt = hgrn_wload(h + 1) if h < 7 else None
            hgrn_head(B, h, has_s, abT, False, blk, wts)
            wts = nxt

    EBo = sb("EBo", [128, 16, 128], BF16)
    EBp = sb("EBp", [128, 16, 128], BF16)
    mask_new = sb("mask_new", [64, 16, 64], BF16)
    sinkexp = sb("sinkexp", [128, 16], F32)
    flag_sb = sb("flag_sb", [128, 1], F32)
    kT_prev = sb("kT_prev", [128, 2, 128], BF16)
    v1_prev = sb("v1_prev", [128, 2, 65], BF16)
    cm01 = sb("cm01", [128, 128], BF16)
    msn = sb("msn", [64, 16], BF16)
    bdc = sb("bdc", [64, 64], BF16)
    rmask = sb("rmask", [128, 576], F32)

    def setup_tables():
        a1.reset()
        P.dma(lambda e: e.dma_start(out=flag_sb[:], in_=flag_in), key="flag", writes=[flag_sb.k])
        P.dma(lambda e: e.dma_start(out=sinkexp[:], in_=sinks_in.partition_broadcast(128)), key="sink", writes=[sinkexp.k])
        P.op("act", lambda e: e.activation(sinkexp[:], sinkexp[:], AF.Exp), reads=[sinkexp.k], writes=[sinkexp.k])
        P.op("pool", lambda e: e.memset(cm01[:], 1.0), writes=[cm01.k])
        P.op("pool", lambda e: e.affine_select(out=cm01[:], in_=cm01[:], pattern=[[1, 128]], compare_op=ALU.is_ge, fill=0.0, base=0, channel_multiplier=-1),
             reads=[cm01.k], writes=[cm01.k])
        P.op("pool", lambda e: e.memset(msn[:], 1.0), writes=[msn.k])
        P.op("pool", lambda e: e.affine_select(out=msn[:], in_=msn[:], pattern=[[-4, 16]], compare_op=ALU.is_ge, fill=0.0, base=0, channel_multiplier=1),
             reads=[msn.k], writes=[msn.k])
        P.op("pool", lambda e: e.affine_select(out=msn[:], in_=msn[:], pattern=[[4, 16]], compare_op=ALU.is_ge, fill=0.0, base=3, channel_multiplier=-1),
             reads=[msn.k], writes=[msn.k])
        P.op("pool", lambda e: e.tensor_tensor(bdc[:].rearrange("p (n t) -> p n t", t=4), cm01[0:64, 0:64].rearrange("p (n t) -> p n t", t=4),
                                               bcast(msn[:].unsqueeze(2), [64, 16, 4]), ALU.mult), reads=[cm01.k, msn.k], writes=[bdc.k])
        P.op("pool", lambda e: e.memset(rmask[:], 1.0), writes=[rmask.k])
        P.op("pool", lambda e: e.memset(rmask[:, 0:512].rearrange("p (t s) -> p t s", s=128)[:, :, 0:1], 0.0), writes=[rmask.k])
        P.op("pool", lambda e: e.memset(rmask[:, 512:576].rearrange("p (t s) -> p t s", s=4)[:, :, 0:1], 0.0), writes=[rmask.k])
        rbs = a1.alloc(F32, [16], "rbs", rows=32)
        ohs = a1.alloc(F32, [128], "ohs", rows=32)
        Z = a1.alloc(F32, [384], "Z", rows=16)
        P.dma(lambda e: e.dma_start(out=rbs[:], in_=rb_in), key="rbs", writes=[rbs.k])
        P.dma(lambda e: e.dma_start(out=ohs[:], in_=oh_in), key="ohs", writes=[ohs.k])
        b0 = banks[0]
        P.op("pe", lambda e: e.matmul(b0[0:16, 0:128], rbs[:], ohs[:], start=True, stop=True), reads=[rbs.k, ohs.k], writes=[b0.k])
        P.op("pool", lambda e: e.memset(Z[:], 0.0), writes=[Z.k])
        P.op("act", lambda e: e.activation(Z[:, 128:256], b0[0:16, 0:128], AF.Exp), reads=[b0.k, Z.k], writes=[Z.k])
        P.dma(lambda e: e.dma_start(out=zsc, in_=Z[:]), key="zsc", reads=[Z.k], writes=["zsc"])
        HK = a1.alloc(F32, [2, 16, 128], "HK")
        HKb = a1.alloc(BF16, [2, 16, 128], "HKb")
        Jm = a1.alloc(BF16, [128], "Jm")
        for i, off in enumerate((1, 129)):
            P.dma(lambda e, i=i, off=off: e.dma_start(out=HK[:, i], in_=bass.AP(zsc.tensor, off, [[1, 128], [384, 16], [1, 128]])),
                  key="HK%d" % i, reads=["zsc"], writes=[HK.k])
        P.op("dve", lambda e: e.tensor_copy(HKb[:], HK[:]), reads=[HK.k], writes=[HKb.k])
        P.op("pool", lambda e: e.memset(Jm[:], 0.0), writes=[Jm.k])
        P.op("pool", lambda e: e.affine_select(out=Jm[:], in_=Jm[:], pattern=[[1, 128]], compare_op=ALU.not_equal, fill=1.0, base=-127, channel_multiplier=1),
             reads=[Jm.k], writes=[Jm.k])
        for i, tab in enumerate((EBo, EBp)):
            for c in range(4):
                bk = banks[1 + (i * 4 + c) % 2]
                P.op("pe", lambda e, i=i, c=c, bk=bk: e.matmul(bk[:, 0:512], Jm[:], HKb[:, i, c * 4:(c + 1) * 4, :], start=True, stop=True),
                     reads=[Jm.k, HKb.k], writes=[bk.k])
                P.op("act", lambda e, tab=tab, c=c, bk=bk: e.copy(tab[:, c * 4:(c + 1) * 4, :], bk[:, 0:512].rearrange("p (h t) -> p h t", h=4)),
                     reads=[bk.k], writes=[tab.k])
        RepT = a1.alloc(BF16, [16, 4], "RepT")
        Xr = a1.alloc(BF16, [16, 16, 4], "Xr")
        P.op("pool", lambda e: e.memset(RepT[:], 0.0), writes=[RepT.k])
        P.op("pool", lambda e: e.affine_select(out=RepT[:], in_=RepT[:], pattern=[[0, 16], [-1, 4]], compare_op=ALU.not_equal, fill=1.0, base=0, channel_multiplier=1),
             reads=[RepT.k], writes=[RepT.k])
        P.op("dve", lambda e: e.tensor_copy(Xr[:], bcast(EBo[:, :, 0:4].unsqueeze(2), [128, 16, 16, 4])), reads=[EBo.k], writes=[Xr.k])
        for c in range(2):
            bk = banks[3 + c]
            P.op("pe", lambda e, c=c, bk=bk: e.matmul(bk[0:64, 0:512], RepT[:].rearrange("p a b -> p (a b)"), Xr[:, c * 8:(c + 1) * 8].rearrange("p h n t -> p (h n t)"),
                                                      start=True, stop=True), reads=[RepT.k, Xr.k], writes=[bk.k])
            P.op("dve", lambda e, c=c, bk=bk: e.tensor_tensor(mask_new[:, c * 8:(c + 1) * 8, :].rearrange("p h (n t) -> p h n t", t=4),
                                                              bk[0:64, 0:512].rearrange("p (h n t) -> p h n t", h=8, t=4),
                                                              bcast(msn[:].unsqueeze(1).unsqueeze(3), [64, 8, 16, 4]), ALU.mult),
                 reads=[bk.k, msn.k], writes=[mask_new.k])
        P.dma(lambda e: e.dma_start(out=ssk_out[:, 0:124, :], in_=ck_in[:, 4:128, :]), key="sskc", eng="pool")
        P.dma(lambda e: e.dma_start(out=ssv_out[:, 0:124, :], in_=cv_in[:, 4:128, :]), key="ssvc", eng="pool")

    def heads_view(tab, kv, half):
        return tab.rearrange("p (kv j u) x -> p kv j u x", kv=2, u=2)[:, kv, :, half, :]

    def swa_block(blk, groups, abT, first_core_tile):
        W = sum(r for (_, r, _) in groups)
        has_s = W > 512
        a1.reset()
        qT = a1.alloc(BF16, [8, 576], "qT")
        kT = a1.alloc(BF16, [2, 576], "kT")
        v1 = a1.alloc(BF16, [5, 2, 65], "v1")
        mark = a1.off
        kvf = [a1.alloc(F32, [256], "kvf%d" % i) for i in range(2)]
        Eb = [a1.alloc(BF16, [512], "Eb%d" % i) for i in range(2)]
        PT = [a1.alloc(BF16, [512], "PT%d" % i) for i in range(4)]
        btok = [a1.alloc(BF16, [1024], "btok%d" % i) for i in range(2)]
        den = a1.alloc(F32, [4], "den")
        if first_core_tile:
            EBpf = a1.alloc(BF16, [16, 128], "EBpf")
            P.op("dve", lambda e: e.tensor_scalar(EBpf[:], EBp[:], flag_sb[:, 0:1], None, ALU.mult), reads=[EBp.k, flag_sb.k], writes=[EBpf.k])
        P.op("pool", lambda e: e.memset(v1[:, :, :, 64:65], 1.0), writes=[v1.k])
        for kv in range(2):
            for half in range(2):
                wb = wload(U_SQ + kv * 2 + half)
                wv = wb[:].rearrange("p (k c) -> p k c", k=NK)
                for mt in range(2):
                    j = kv * 4 + half * 2 + mt
                    bk = banks[mt]
                    for k in range(NK):
                        P.op("pe", lambda e, wv=wv, k=k, mt=mt, bk=bk: e.matmul(bk[:, 0:512], wv[:, k, mt * 128:(mt + 1) * 128], hT[:, k, 0:512],
                                                                               start=(k == 0), stop=(k == NK - 1)),
                             reads=[wb.k, hT.k], writes=[bk.k], sig=(k == NK - 1))
                    P.op("act", lambda e, bk=bk, j=j: e.activation(qT[:, j, 0:512], bk[:, 0:512], AF.Copy, scale=0.125), reads=[bk.k], writes=[qT.k])
                    if has_s:
                        bs_ = banks[4 + mt]
                        for k in range(NK):
                            P.op("pe", lambda e, wv=wv, k=k, mt=mt, bs_=bs_: e.matmul(bs_[:, 0:64], wv[:, k, mt * 128:(mt + 1) * 128], hT[:, k, 512:576],
                                                                                     start=(k == 0), stop=(k == NK - 1)),
                                 reads=[wb.k, hT.k], writes=[bs_.k], sig=(k == NK - 1))
                        P.op("act", lambda e, bs_=bs_, j=j: e.activation(qT[:, j, 512:576], bs_[:, 0:64], AF.Copy, scale=0.125), reads=[bs_.k], writes=[qT.k])
        wb = wload(U_SKVB)
        wv = wb[:].rearrange("p (k c) -> p k c", k=NK)
        for kv in range(2):
            bk = banks[2 + kv]
            for k in range(NK):
                P.op("pe", lambda e, wv=wv, k=k, kv=kv, bk=bk: e.matmul(bk[:, 0:512], wv[:, k, kv * 128:(kv + 1) * 128], hT[:, k, 0:512],
                                                                       start=(k == 0), stop=(k == NK - 1)),
                     reads=[wb.k, hT.k], writes=[bk.k], sig=(k == NK - 1))
            P.op("act", lambda e, bk=bk, kv=kv: e.copy(kT[:, kv, 0:512], bk[:, 0:512]), reads=[bk.k], writes=[kT.k])
            if has_s:
                bs_ = banks[6 + kv]
                for k in range(NK):
                    P.op("pe", lambda e, wv=wv, k=k, kv=kv, bs_=bs_: e.matmul(bs_[:, 0:64], wv[:, k, kv * 128:(kv + 1) * 128], hT[:, k, 512:576],
                                                                             start=(k == 0), stop=(k == NK - 1)),
                         reads=[wb.k, hT.k], writes=[bs_.k], sig=(k == NK - 1))
                P.op("act", lambda e, bs_=bs_, kv=kv: e.copy(kT[:, kv, 512:576], bs_[:, 0:64]), reads=[bs_.k], writes=[kT.k])
        wb = wload(U_SKVA)
        wv = wb[:].rearrange("p (k c) -> p k c", k=NK)
        for gi, (t, rows, c0) in enumerate(groups):
            bk = banks[gi % 2]
            for k in range(NK):
                P.op("pe", lambda e, wv=wv, k=k, bk=bk, c0=c0, rows=rows: e.matmul(bk[0:rows, 0:256], hT[:, k, c0:c0 + rows], wv[:, k, :],
                                                                                  start=(k == 0), stop=(k == NK - 1)),
                     reads=[wb.k, hT.k], writes=[bk.k], sig=(k == NK - 1))
            P.op("act", lambda e, bk=bk, t=t, rows=rows: e.copy(v1[0:rows, t, :, 0:64], bk[0:rows, 128:256].rearrange("p (kv d) -> p kv d", kv=2)),
                 reads=[bk.k], writes=[v1.k])
            if blk == 3 and (t == 3 or rows != 128):
                kf = kvf[gi % 2]
                P.op("dve", lambda e, kf=kf, bk=bk, rows=rows: e.tensor_copy(kf[0:rows, :], bk[0:rows, 0:256]), reads=[bk.k], writes=[kf.k])
                if rows == 128:
                    P.dma(lambda e, kf=kf: e.dma_start(out=pk_out, in_=kf[:, 0:128]), key=kf.k + "a", reads=[kf.k], eng="pool")
                    P.dma(lambda e, kf=kf: e.dma_start(out=pv_out, in_=kf[:, 128:256]), key=kf.k + "b", reads=[kf.k], eng="pool")
                else:
                    for n in range(NSEQ):
                        P.dma(lambda e, kf=kf, n=n: e.dma_start(out=ssk_out[n, 124:128, :], in_=kf[4 * n:4 * n + 4, 0:128]), key=kf.k + "a", reads=[kf.k], eng="pool")
                        P.dma(lambda e, kf=kf, n=n: e.dma_start(out=ssv_out[n, 124:128, :], in_=kf[4 * n:4 * n + 4, 128:256]), key=kf.k + "b", reads=[kf.k], eng="pool")
        sc = 0
        oc = 0
        for t in range(4):
            bt = btok[t % 2]
            for kv in range(2):
                for half in range(2):
                    hs = slice(half * 64, (half + 1) * 64)
                    pts = []
                    for kb in range(2):
                        if kb == 0:
                            if t == 0:
                                kap, kkey = kT_prev[hs, kv, :], kT_prev.k
                                tab = EBpf if first_core_tile else EBp
                            else:
                                kap, kkey = kT[hs, kv, (t - 1) * 128:t * 128], kT.k
                                tab = EBp
                        else:
                            kap, kkey = kT[hs, kv, t * 128:(t + 1) * 128], kT.k
                            tab = EBo
                        bk = banks[sc % 2]
                        e_, p_ = Eb[sc % 2], PT[sc % 4]
                        sc += 1
                        P.op("pe", lambda e, bk=bk, kap=kap, kv=kv, hs=hs, t=t: e.matmul(bk[:, 0:512], kap, qT[hs, kv * 4:(kv + 1) * 4, t * 128:(t + 1) * 128],
                                                                                        start=True, stop=True), reads=[kkey, qT.k], writes=[bk.k])
                        P.op("act", lambda e, bk=bk, e_=e_: e.activation(e_[:], bk[:, 0:512], AF.Exp), reads=[bk.k], writes=[e_.k])
                        P.op("pool", lambda e, e_=e_, p_=p_, tab=tab, kv=kv, half=half: e.tensor_tensor(p_[:].rearrange("p (j t) -> p j t", j=4),
                                                                                                     e_[:].rearrange("p (j t) -> p j t", j=4),
                                                                                                     heads_view(tab[:], kv, half), ALU.mult),
                             reads=[e_.k, tab.k], writes=[p_.k])
                        pts.append(p_)
                    bo = banks[2 + oc % 2]
                    oc += 1
                    ov = bo[:, 0:260].rearrange("p (j c) -> p j c", j=4)
                    for j in range(4):
                        for kb in range(2):
                            if kb == 0:
                                vap, vkey = (v1_prev[:, kv, :], v1_prev.k) if t == 0 else (v1[:, t - 1, kv, :], v1.k)
                            else:
                                vap, vkey = v1[:, t, kv, :], v1.k
                            P.op("pe", lambda e, ov=ov, j=j, kb=kb, vap=vap, p_=pts[kb]: e.matmul(ov[:, j, :], p_[:, j * 128:(j + 1) * 128], vap, start=(kb == 0), stop=(kb == 1)),
                                 reads=[pts[kb].k, vkey], writes=[bo.k], sig=(j == 3 and kb == 1))
                    P.op("dve", lambda e, ov=ov, kv=kv, half=half: e.tensor_tensor(den[:].unsqueeze(2), ov[:, :, 64:65],
                                                                                  heads_view(sinkexp[:].unsqueeze(2), kv, half), ALU.add),
                         reads=[bo.k, sinkexp.k], writes=[den.k])
                    P.op("dve", lambda e: e.reciprocal(den[:], den[:]), reads=[den.k], writes=[den.k])
                    P.op("dve", lambda e, ov=ov, bt=bt, kv=kv, half=half: e.tensor_tensor(heads_view(bt[:].rearrange("p (h d) -> p h d", d=64), kv, half), ov[:, :, 0:64],
                                                                                         bcast(den[:].unsqueeze(2), [128, 4, 64]), ALU.mult),
                         reads=[bo.k, den.k], writes=[bt.k])
            btr = banks[4 + t % 2]
            tv = btr[:, 0:512].bitcast(BF16).rearrange("p (a b) -> p a b", a=8)
            for pr in range(8):
                P.op("pe", lambda e, tv=tv, pr=pr, bt=bt: e.transpose(tv[:, pr, :], bt[:, pr * 128:(pr + 1) * 128], ident[:]), reads=[bt.k, ident.k], writes=[btr.k], sig=(pr == 7))
            P.op("act", lambda e, tv=tv, t=t: e.copy(abT[:, 8:16, t * 128:(t + 1) * 128], tv), reads=[btr.k], writes=[abT.k])
        if has_s:
            P.barrier()
            a1.off = mark
            swa_sample(qT, kT, v1, abT)
        P.op("pool", lambda e: e.tensor_copy(kT_prev[:], kT[:, :, 384:512]), reads=[kT.k], writes=[kT_prev.k])
        P.op("pool", lambda e: e.tensor_copy(v1_prev[:], v1[:, 3]), reads=[v1.k], writes=[v1_prev.k])

    def swa_sample(qT, kT, v1, abT):
        ckf = [a1.alloc(F32, [4, 128], "ckf0")] * 2
        ckd = a1.alloc(BF16, [8, 2, 2, 64], "ckd")
        v1c = a1.alloc(BF16, [16, 2, 65], "v1c")
        kcT = a1.alloc(BF16, [2, 16, 128], "kcT")
        Ppad = a1.alloc(BF16, [16, 4, 64], "Ppad")
        Ec = a1.alloc(BF16, [256], "Ec")
        En = a1.alloc(BF16, [256], "En", rows=64)
        Pn = a1.alloc(BF16, [4, 64], "Pn", rows=64)
        bts = a1.alloc(BF16, [1024], "bts", rows=64)
        dens = a1.alloc(F32, [4], "dens", rows=64)
        P.op("pool", lambda e: e.memset(Ppad[:], 0.0), writes=[Ppad.k])
        P.op("pool", lambda e: e.memset(v1c[:, :, :, 64:65], 1.0), writes=[v1c.k])
        for g4 in range(4):
            f_ = ckf[g4 % 2]
            P.dma(lambda e, g4=g4, f_=f_: e.dma_start(out=f_[:], in_=cv_in[g4 * 4:(g4 + 1) * 4].rearrange("n s c -> s n c")), key=f_.k, writes=[f_.k])
            P.op("dve", lambda e, g4=g4, f_=f_: e.tensor_copy(v1c[:, g4 * 4:(g4 + 1) * 4, :, 0:64], f_[:].rearrange("p n (kv d) -> p n kv d", kv=2)),
                 reads=[f_.k], writes=[v1c.k])
        tc_ = 0
        for h8 in range(2):
            for g4 in (2 * h8, 2 * h8 + 1):
                f_ = ckf[g4 % 2]
                P.dma(lambda e, g4=g4, f_=f_: e.dma_start(out=f_[:], in_=ck_in[g4 * 4:(g4 + 1) * 4].rearrange("n s c -> s n c")), key=f_.k, writes=[f_.k])
                P.op("dve", lambda e, g4=g4, f_=f_: e.tensor_copy(ckd[:, (g4 % 2) * 4:(g4 % 2) * 4 + 4], bcast(f_[:].rearrange("p n (kv d) -> p n kv d", kv=2).unsqueeze(3), [128, 4, 2, 2, 64])),
                     reads=[f_.k], writes=[ckd.k])
            for kv in range(2):
                for n4 in (2 * h8, 2 * h8 + 1):
                    btr = banks[tc_ % 2]
                    tc_ += 1
                    tv = btr[:, 0:256].bitcast(BF16).rearrange("p (a b) -> p a b", a=4)
                    for i in range(4):
                        n = n4 * 4 + i
                        P.op("pe", lambda e, tv=tv, i=i, n=n, kv=kv: e.transpose(tv[:, i, :], ckd[:, n % 8, kv].rearrange("p a d -> p (a d)"), ident[:]),
                             reads=[ckd.k, ident.k], writes=[btr.k], sig=(i == 3))
                    P.op("act", lambda e, tv=tv, kv=kv, n4=n4: e.copy(kcT[:, kv, n4 * 4:(n4 + 1) * 4, :], tv), reads=[btr.k], writes=[kcT.k])
        pd = Ppad[:]
        oc = 0
        for kv in range(2):
            for half in range(2):
                hs = slice(half * 64, (half + 1) * 64)
                bS, bN = banks[2], banks[3]
                for m in range(NSEQ):
                    P.op("pe", lambda e, m=m, kv=kv, hs=hs: e.matmul(bS[:, m * 16:(m + 1) * 16], kcT[hs, kv, m, :], qT[hs, kv * 4:(kv + 1) * 4, 512 + 4 * m:516 + 4 * m],
                                                                    start=True, stop=True), reads=[kcT.k, qT.k], writes=[bS.k], sig=(m == NSEQ - 1))
                P.op("pe", lambda e, kv=kv, hs=hs: e.matmul(bN[0:64, 0:256], kT[hs, kv, 512:576], qT[hs, kv * 4:(kv + 1) * 4, 512:576], start=True, stop=True),
                     reads=[kT.k, qT.k], writes=[bN.k])
                P.op("act", lambda e: e.activation(Ec[:], bS[:, 0:256], AF.Exp), reads=[bS.k], writes=[Ec.k])
                P.op("act", lambda e: e.activation(En[:], bN[0:64, 0:256], AF.Exp), reads=[bN.k], writes=[En.k])
                diag = bass.AP(pd.tensor, pd.offset, [list(pd.ap[0]), [260, 16], [64, 4], [1, 4]])
                P.op("dve", lambda e, diag=diag, kv=kv, half=half: e.tensor_tensor(diag, Ec[:].rearrange("p (m j t) -> p m j t", m=16, j=4),
                                                                                  bcast(heads_view(EBp[:, :, 0:4], kv, half).unsqueeze(1), [128, 16, 4, 4]), ALU.mult),
                     reads=[Ec.k, EBp.k], writes=[Ppad.k])
                P.op("pool", lambda e, kv=kv, half=half: e.tensor_tensor(Pn[:], En[:].rearrange("p (j x) -> p j x", j=4), heads_view(mask_new[:], kv, half), ALU.mult),
                     reads=[En.k, mask_new.k], writes=[Pn.k])
                bo = banks[4 + oc % 2]
                oc += 1
                ov = bo[0:64, 0:260].rearrange("p (j c) -> p j c", j=4)
                for j in range(4):
                    for m in range(NSEQ):
                        P.op("pe", lambda e, ov=ov, j=j, m=m, kv=kv: e.matmul(ov[:, j, :], Ppad[:, m, j, :], v1c[:, m, kv, :], start=(m == 0), stop=False),
                             reads=[Ppad.k, v1c.k], writes=[bo.k], sig=False)
                    P.op("pe", lambda e, ov=ov, j=j, kv=kv: e.matmul(ov[:, j, :], Pn[:, j, :], v1[0:64, 4, kv, :], start=False, stop=True),
                         reads=[Pn.k, v1.k], writes=[bo.k], sig=(j == 3))
                P.op("dve", lambda e, ov=ov, kv=kv, half=half: e.tensor_tensor(dens[:].unsqueeze(2), ov[:, :, 64:65],
                                                                              heads_view(sinkexp[0:64, :].unsqueeze(2), kv, half), ALU.add),
                     reads=[bo.k, sinkexp.k], writes=[dens.k])
                P.op("dve", lambda e: e.reciprocal(dens[:], dens[:]), reads=[dens.k], writes=[dens.k])
                P.op("dve", lambda e, ov=ov, kv=kv, half=half: e.tensor_tensor(heads_view(bts[:].rearrange("p (h d) -> p h d", d=64), kv, half), ov[:, :, 0:64],
                                                                              bcast(dens[:].unsqueeze(2), [64, 4, 64]), ALU.mult),
                     reads=[bo.k, dens.k], writes=[bts.k])
        btr = banks[6]
        tv = btr[:, 0:256].bitcast(BF16).rearrange("p (a b) -> p a b", a=8)
        for pr in range(8):
            P.op("pe", lambda e, tv=tv, pr=pr: e.transpose(tv[:, pr, :], bts[:, pr * 128:(pr + 1) * 128], ident[0:64, 0:64]), reads=[bts.k, ident.k], writes=[btr.k], sig=(pr == 7))
        P.op("act", lambda e, tv=tv: e.copy(abT[:, 8:16, 512:576], tv), reads=[btr.k], writes=[abT.k])

    def wout_block(groups, abT):
        dcnt = 0
        for n in range(4):
            wo = [wload(U_WO + n * 2 + i) for i in range(2)]
            for (t, rows, c0) in groups:
                bk = banks[6 + dcnt % 2]
                dcnt += 1
                for kc in range(16):
                    wv = wo[kc // 8][:].rearrange("p (k c) -> p k c", k=8)
                    P.op("pe", lambda e, kc=kc, wv=wv, bk=bk, c0=c0, rows=rows: e.matmul(bk[0:rows, :], abT[:, kc, c0:c0 + rows], wv[:, kc % 8, :],
                                                                                        start=(kc == 0), stop=(kc == 15)),
                         reads=[abT.k, wo[kc // 8].k], writes=[bk.k], sig=(kc == 15))
                resid_add(t, rows, slice(n * 512, (n + 1) * 512), bk)

    def prefix_phase():
        for pb in range(4):
            P.barrier()
            for t in range(4):
                P.dma(lambda e, t=t, pb=pb: e.dma_start(out=xres[:, t, :], in_=xp[pb * 512 + t * 128: pb * 512 + (t + 1) * 128, :]), key=xk(t), writes=[xk(t)])
            for t in range(4):
                norm_tile(xres[:, t, :], 128, 0, t * 128, xk(t))
            if cfg.get("hgrn", True):
                B = hgrn_alloc(False)
                wts = hgrn_wload(0)
                for h in range(8):
                    nxt = hgrn_wload(h + 1) if h < 7 else None
                    hgrn_head(B, h, False, None, True, -1, wts)
                    wts = nxt
                    do_dd(2)
                    do_casts(cfg.get("cast_per_head", 7))
        P.barrier()
        a1.reset()
        wb = wload(U_SKVB)
        wv = wb[:].rearrange("p (k c) -> p k c", k=NK)
        for kv in range(2):
            bk = banks[2 + kv]
            for k in range(NK):
                P.op("pe", lambda e, wv=wv, k=k, kv=kv, bk=bk: e.matmul(bk[:, 0:128], wv[:, k, kv * 128:(kv + 1) * 128], hT[:, k, 384:512],
                                                                       start=(k == 0), stop=(k == NK - 1)),
                     reads=[wb.k, hT.k], writes=[bk.k], sig=(k == NK - 1))
            P.op("act", lambda e, bk=bk, kv=kv: e.copy(kT_prev[:, kv, :], bk[:, 0:128]), reads=[bk.k], writes=[kT_prev.k])
        wb = wload(U_SKVA)
        wv = wb[:].rearrange("p (k c) -> p k c", k=NK)
        bk = banks[4]
        for k in range(NK):
            P.op("pe", lambda e, wv=wv, k=k, bk=bk: e.matmul(bk[:, 0:256], hT[:, k, 384:512], wv[:, k, :], start=(k == 0), stop=(k == NK - 1)),
                 reads=[wb.k, hT.k], writes=[bk.k], sig=(k == NK - 1))
        P.op("pool", lambda e: e.memset(v1_prev[:, :, 64:65], 1.0), writes=[v1_prev.k])
        P.op("act", lambda e, bk=bk: e.copy(v1_prev[:, :, 0:64], bk[:, 128:256].rearrange("p (kv d) -> p kv d", kv=2)), reads=[bk.k, v1_prev.k], writes=[v1_prev.k])

    def final_out(groups, blk):
        a1.reset()
        nfw = a1.alloc(F32, [D], "nfw")
        ybuf = [a1.alloc(F32, [D], "ybuf%d" % i) for i in range(2)]
        P.dma(lambda e: e.dma_start(out=nfw[:], in_=nfin_in.partition_broadcast(128)), key=nfw.k, writes=[nfw.k])
        for gi, (t, rows, c0) in enumerate(groups):
            src = xres[0:rows, t, :]
            ss = small[0:rows, 2:3]
            rs = small[0:rows, 3:4]
            yb = ybuf[gi % 2]
            P.op("act", lambda e, src=src, rows=rows, ss=ss: e.activation(junk[0:rows, :], src, AF.Square, accum_out=ss), reads=[xk(t)], writes=[junk.k, "fss"])
            P.op("act", lambda e, rs=rs, ss=ss: e.activation(rs, ss, AF.Ln, scale=1.0 / D, bias=EPS), reads=["fss"], writes=["frs"])
            P.op("act", lambda e, rs=rs: e.activation(rs, rs, AF.Exp, scale=-0.5), reads=["frs"], writes=["frs"])
            P.op("dve", lambda e, src=src, rows=rows, rs=rs, yb=yb: e.scalar_tensor_tensor(yb[0:rows, :], src, rs, nfw[0:rows, :], ALU.mult, ALU.mult),
                 reads=[xk(t), "frs", nfw.k], writes=[yb.k])
            if rows == 128:
                dst = y_out[blk * 512 + gi * 128: blk * 512 + (gi + 1) * 128, :]
            else:
                dst = ys_out
            P.dma(lambda e, dst=dst, yb=yb, rows=rows: e.dma_start(out=dst, in_=yb[0:rows, :]), key=yb.k + "o", reads=[yb.k], eng="pool")

    a2.reset()
    for i in range(3):
        cslots.append((a2.alloc(F32, [1024], "xs32_%d" % i), a2.alloc(BF16, [1024], "xs16_%d" % i)))
    n_win = sum(1 for _ in range(NK)) * 6 if cfg.get("mixer", True) else 0
    do_casts(n_win if cfg.get("prefix", True) and cfg.get("mixer", True) and cfg.get("hgrn", True) else None)
    cstate["engs"] = ("dve", "act")
    if cfg.get("mixer", True):
        P.barrier()
        setup_tables()
        P.barrier()
        setup_hgrn()
        P.barrier()
        if cfg.get("prefix", True):
            prefix_phase()
    do_dd()
    do_casts(None, flush=True)
    P.barrier()
    del cslots[2:]
    if cfg.get("xattn", True) and "memkv" in cfg.get("x_parts", "memkv"):
        P.barrier()
        mem_kv()
    for blk in cfg.get("blocks", (0, 1, 2, 3)):
        P.barrier()
        groups = [(t, 128, t * 128) for t in range(4)]
        if blk == 3:
            groups.append((4, WS, 512))
        for (t, rows, c0) in groups:
            src = xm[blk * 512 + t * 128: blk * 512 + (t + 1) * 128, :] if rows == 128 else xs
            P.dma(lambda e, src=src, t=t, rows=rows: e.dma_start(out=xres[0:rows, t, :], in_=src), key=xk(t), writes=[xk(t)])
        if cfg.get("mixer", True):
            for (t, rows, c0) in groups:
                norm_tile(xres[0:rows, t, :], rows, 0, c0, xk(t))
            a2.reset()
            abT = a2.alloc(BF16, [16, 576], "abT")
            if not cfg.get("hgrn", True):
                P.op("pool", lambda e, abT=abT: e.memset(abT[:, 0:8, :], 0.0), writes=[abT.k])
            else:
                hgrn_block(blk, blk == 3, abT)
                P.barrier()
            if cfg.get("swa", True):
                swa_block(blk, groups, abT, blk == 0)
            else:
                P.op("pool", lambda e, abT=abT: e.memset(abT[:, 8:16, :], 0.0), writes=[abT.k])
            if not cfg.get("swa", True):
                P.barrier()
            wout_block(groups, abT)
            P.barrier()
        if cfg.get("xattn", True):
            for (t, rows, c0) in groups:
                norm_tile(xres[0:rows, t, :], rows, 1, c0, xk(t))
            xattn_block(groups)
            P.barrier()
        if cfg.get("ffn", True):
            for (t, rows, c0) in groups:
                norm_tile(xres[0:rows, t, :], rows, 2, c0, xk(t))
            ffn_block(groups)
        P.barrier()
        final_out(groups, blk)
    P.emit()


_CACHE = {}


def _rel_bucket_onehot():
    n = np.arange(128)
    nf = np.maximum(n, 1).astype(np.float32)
    large = 16 + (np.log(nf / np.float32(16)) / np.float32(math.log(128 / 16)) * np.float32(16)).astype(np.int32)
    large = np.minimum(large, 31)
    b = np.where(n < 16, n, large)
    oh = np.zeros((32, 128), np.float32)
    oh[b, n] = 1.0
    return oh


def kernel(cfg=None, **inp):
    cfg = cfg or {}
    key = repr(sorted(cfg.items()))
    if key not in _CACHE:
        _CACHE[key] = build_program(cfg)
    nc = _CACHE[key]
    f = lambda a: np.ascontiguousarray(np.asarray(a, dtype=np.float32))
    xpr = f(inp["x_prompt"]); xsa = f(inp["x_sample"])
    shared = {
        "oh": _rel_bucket_onehot(),
        "nmix": f(inp["norm_mix_w"][0]), "w_in": f(inp["w_in"][0]), "lbl": f(inp["hgrn_lb_logits"]),
        "onw": f(inp["hgrn_onorm_w"][0]), "sinks": f(inp["swa_sinks"][0]), "rb": f(inp["rel_bias"]),
        "w_out": f(inp["w_out"][0]), "nx": f(inp["norm_xattn_w"][0]), "nmem": f(inp["mem_norm_w"][0]),
        "w_mq": f(inp["w_mq"][0]), "w_mk": f(inp["w_mk"][0]), "w_mv": f(inp["w_mv"][0]), "w_mo": f(inp["w_mo"][0]),
        "nffn": f(inp["norm_ffn_w"][0]), "w_gate": f(inp["w_gate"][0]), "w_up": f(inp["w_up"][0]),
        "w_down": f(inp["w_down"][0]), "nfin": f(inp["norm_final_w"]),
    }
    zeros_half = np.zeros((2048, D), np.float32)
    in_maps = []
    for c in range(8):
        b, half = c // 2, c % 2
        sl = slice(c * NSEQ, (c + 1) * NSEQ)
        m = dict(shared)
        m["xm"] = f(xpr[b, half * 2048:(half + 1) * 2048])
        m["xp"] = f(xpr[b, 0:2048]) if half == 1 else zeros_half
        m["xs"] = f(xsa[sl].reshape(WS, D))
        m["st"] = f(inp["state_hgrn"][0, sl])
        m["ck"] = f(inp["cache_swa_k"][0, sl].reshape(NSEQ, 128, 128))
        m["cv"] = f(inp["cache_swa_v"][0, sl].reshape(NSEQ, 128, 128))
        m["cmk"] = f(inp["cache_mem_k"][0, sl].reshape(NSEQ, 256, 512))
        m["cmv"] = f(inp["cache_mem_v"][0, sl].reshape(NSEQ, 256, 512))
        m["mem"] = f(inp["mem_prompt"][b])
        m["flag"] = np.full((128, 1), float(half), np.float32)
        in_maps.append(m)
    res = run_bass_kernel_spmd(nc, in_maps, core_ids=list(range(8)))
    R = res.results
    y_prompt = np.stack([np.concatenate([R[2 * b]["y"], R[2 * b + 1]["y"]], 0) for b in range(4)])
    y_sample = np.concatenate([R[c]["ys"].reshape(NSEQ, 4, D) for c in range(8)], 0)
    p_state = np.stack([R[2 * b + 1]["pst"] for b in range(4)])[None]
    p_k = np.stack([R[2 * b + 1]["pk"].reshape(128, 2, 64) for b in range(4)])[None]
    p_v = np.stack([R[2 * b + 1]["pv"].reshape(128, 2, 64) for b in range(4)])[None]
    p_mk = np.stack([R[2 * b]["pmk"].reshape(256, 4, 128) for b in range(4)])[None]
    p_mv = np.stack([R[2 * b]["pmv"].reshape(256, 4, 128) for b in range(4)])[None]
    s_state = np.concatenate([R[c]["sst"] for c in range(8)], 0)[None]
    s_k = np.concatenate([R[c]["ssk"].reshape(NSEQ, 128, 2, 64) for c in range(8)], 0)[None]
    s_v = np.concatenate([R[c]["ssv"].reshape(NSEQ, 128, 2, 64) for c in range(8)], 0)[None]
    outs = (y_prompt, y_sample, p_state, p_k, p_v, p_mk, p_mv, s_state, s_k, s_v)
    outs = tuple(np.ascontiguousarray(o, dtype=np.float32) for o in outs)
    if cfg.get("raw"):
        return outs, R
    return outs
```

```python
import contextlib
import math
import numpy as np
import concourse.bass as bass
import concourse.mybir as mybir
from concourse.bass_utils import run_bass_kernel_spmd

F32 = mybir.dt.float32
BF16 = mybir.dt.bfloat16
AF = mybir.ActivationFunctionType
ALU = mybir.AluOpType
AX = mybir.AxisListType

D = 2048
NK = 16
FF = 5632
NCH = 22
EPS = 1e-6
NTILE = 16
WS = 64
NSEQ = 16


class _Op:
    __slots__ = ("eng", "fn", "deps", "sig", "seq", "sigcount", "dma_key", "dma_cnt", "waits")


class Prog:
    ENGS = ("pe", "act", "dve", "pool", "sp")

    def __init__(self, nc):
        self.nc = nc
        self.ops = {e: [] for e in self.ENGS}
        self.last_writer = {}
        self.readers = {}
        self.dma_count = {}
        self.last_dma = {}
        self.bar = []

    def _add(self, eng, fn, reads, writes, sig, dma_key=None):
        o = _Op()
        o.eng, o.fn, o.sig, o.dma_key = eng, fn, sig, dma_key
        o.dma_cnt = 0
        deps = list(self.bar)
        for k in reads:
            w = self.last_writer.get(k)
            if w is not None:
                deps.append(w)
            if k.startswith("bank"):
                deps.extend(r for r in self.readers.get(k, ()) if r.eng != eng)
        for k in writes:
            w = self.last_writer.get(k)
            if w is not None:
                deps.append(w)
            deps.extend(self.readers.get(k, ()))
        o.deps = deps
        if dma_key is not None:
            c = self.dma_count.get(dma_key, 0) + 1
            self.dma_count[dma_key] = c
            o.dma_cnt = c
            self.last_dma[dma_key] = o
        for k in reads:
            self.readers.setdefault(k, []).append(o)
        for k in writes:
            self.last_writer[k] = o
            self.readers[k] = []
        o.seq = len(self.ops[eng])
        self.ops[eng].append(o)
        return o

    def op(self, eng, fn, reads=(), writes=(), sig=True):
        return self._add(eng, fn, tuple(reads), tuple(writes), sig)

    def dma(self, fn, key, reads=(), writes=(), eng="sp"):
        return self._add(eng, fn, tuple(reads), tuple(writes), True, dma_key=key)

    def barrier(self):
        bar = []
        for e in self.ENGS:
            for o in reversed(self.ops[e]):
                if o.dma_key is None:
                    o.sig = True
                    bar.append(o)
                    break
        bar.extend(self.last_dma.values())
        self.bar = bar
        self.last_writer = {}
        self.readers = {}

    def emit(self):
        nc = self.nc
        es = contextlib.ExitStack()
        with es:
            sems = {e: es.enter_context(nc.semaphore("s_" + e)) for e in self.ENGS}
            dsems = {k: es.enter_context(nc.semaphore("d_%d" % i)) for i, k in enumerate(self.dma_count)}
            for e in self.ENGS:
                lst = self.ops[e]
                for o in reversed(lst):
                    if o.dma_key is None:
                        o.sig = True
                        break
                cnt = 0
                pend = []
                for o in lst:
                    if o.dma_key is not None:
                        o.sigcount = None
                        continue
                    pend.append(o)
                    if o.sig:
                        cnt += 1
                        for p in pend:
                            p.sigcount = cnt
                        pend = []
            for e in self.ENGS:
                waited = {}
                for o in self.ops[e]:
                    need = {}
                    for d in o.deps:
                        if d.dma_key is not None:
                            s, v = ("d", d.dma_key), 16 * d.dma_cnt
                        else:
                            if d.eng == "pe" and e == "pe" and o.dma_key is None:
                                continue
                            s, v = ("e", d.eng), d.sigcount
                        if v > need.get(s, 0):
                            need[s] = v
                    if o.dma_key is not None and o.dma_cnt > 1:
                        s, v = ("d", o.dma_key), 16 * (o.dma_cnt - 1)
                        if v > need.get(s, 0):
                            need[s] = v
                    ws = []
                    for s, v in need.items():
                        if v > waited.get(s, 0):
                            waited[s] = v
                            ws.append((s, v))
                    o.waits = ws
            block = es.enter_context(nc.Block())

            def run(e):
                def body(engobj):
                    for o in self.ops[e]:
                        for (kind, k), v in o.waits:
                            engobj.wait_ge(sems[k] if kind == "e" else dsems[k], v)
                        ins = o.fn(engobj)
                        if o.dma_key is not None:
                            ins.then_inc(dsems[o.dma_key], 16)
                        elif o.sig:
                            ins.then_inc(sems[e], 1)
                    last = {}
                    for o in self.ops[e]:
                        if o.dma_key is not None:
                            last[o.dma_key] = max(last.get(o.dma_key, 0), o.dma_cnt)
                    for k, c in last.items():
                        engobj.wait_ge(dsems[k], 16 * c)
                return body

            block.tensor(run("pe"))
            block.scalar(run("act"))
            block.vector(run("dve"))
            block.gpsimd(run("pool"))
            block.sync(run("sp"))
        return nc


class Buf:
    def __init__(self, ap, key):
        self.ap = ap
        self.k = key

    def __getitem__(self, idx):
        return self.ap[idx]


class Arena:
    def __init__(self, tensor, nbytes, tag):
        self.t, self.nbytes, self.tag = tensor, nbytes, tag
        self.off = 0
        self.cnt = 0
        self.hi = 0

    def reset(self):
        self.off = 0

    def alloc(self, dtype, fshape, name, rows=128):
        es = 4 if dtype == F32 else 2
        n = 1
        for s in fshape:
            n *= s
        size = (n * es + 31) // 32 * 32
        off = self.off
        assert off + size <= self.nbytes, (self.tag, name, off, size, self.nbytes)
        self.off = off + size
        self.hi = max(self.hi, self.off)
        v = self.t[0:rows, off // 4:(off + n * es + 3) // 4]
        if dtype != F32:
            v = v.bitcast(dtype)
            v = v[:, 0:n]
        if len(fshape) == 2:
            v = v.rearrange("p (a b) -> p a b", b=fshape[1])
        elif len(fshape) == 3:
            v = v.rearrange("p (a b c) -> p a b c", b=fshape[1], c=fshape[2])
        elif len(fshape) == 4:
            v = v.rearrange("p (a b c d) -> p a b c d", b=fshape[1], c=fshape[2], d=fshape[3])
        self.cnt += 1
        return Buf(v, "%s.%s.%d" % (self.tag, name, self.cnt))


def bcast(ap, shape):
    return ap.to_broadcast(list(shape))


U_HGA = 0
U_HGB = 8
U_SQ = 16
U_SKVA = 20
U_SKVB = 21
U_WO = 22
U_MQ = 30
U_MK = 32
U_MV = 34
U_MO = 36
U_FG = 38
U_FU = 60
U_FD = 82
NUNIT = 104


def build_program(cfg):
    nc = bass.Bass("TRN2", target_bir_lowering=False)
    es = contextlib.ExitStack()
    with es:
        es.enter_context(nc.allow_non_contiguous_dma(reason="small parameter vectors"))
        _build(nc, es, cfg)
    return nc


def _build(nc, es, cfg):
    def din(name, shape):
        return nc.dram_tensor(name, list(shape), F32, kind="ExternalInput").ap()

    def dout(name, shape):
        return nc.dram_tensor(name, list(shape), F32, kind="ExternalOutput").ap()

    xm = din("xm", [2048, D]); xp = din("xp", [2048, D]); xs = din("xs", [WS, D])
    st_in = din("st", [NSEQ, 8, 128, 128])
    ck_in = din("ck", [NSEQ, 128, 128]); cv_in = din("cv", [NSEQ, 128, 128])
    cmk_in = din("cmk", [NSEQ, 256, 512]); cmv_in = din("cmv", [NSEQ, 256, 512])
    mem_in = din("mem", [256, D]); flag_in = din("flag", [128, 1]); oh_in = din("oh", [32, 128])
    nmix_in = din("nmix", [D]); w_in = din("w_in", [D, 5376]); lbl_in = din("lbl", [2, 1024])
    onw_in = din("onw", [128]); sinks_in = din("sinks", [16]); rb_in = din("rb", [32, 16])
    w_out = din("w_out", [D, D]); nx_in = din("nx", [D]); nmem_in = din("nmem", [D])
    w_mq = din("w_mq", [D, 512]); w_mk = din("w_mk", [D, 512]); w_mv = din("w_mv", [D, 512])
    w_mo = din("w_mo", [512, D]); nffn_in = din("nffn", [D])
    w_gate = din("w_gate", [D, FF]); w_up = din("w_up", [D, FF]); w_down = din("w_down", [FF, D])
    nfin_in = din("nfin", [D])

    y_out = dout("y", [2048, D]); ys_out = dout("ys", [WS, D])
    pst_out = dout("pst", [8, 128, 128]); pk_out = dout("pk", [128, 128]); pv_out = dout("pv", [128, 128])
    pmk_out = dout("pmk", [256, 512]); pmv_out = dout("pmv", [256, 512])
    sst_out = dout("sst", [NSEQ, 8, 128, 128]); ssk_out = dout("ssk", [NSEQ, 128, 128]); ssv_out = dout("ssv", [NSEQ, 128, 128])
    dbg_out = dout("dbg", [5, 128, D]) if cfg.get("dbg") else None

    wsc = nc.dram_tensor("wsc", [NUNIT, 128, 4096], BF16).ap()
    zsc = nc.dram_tensor("zsc", [16, 384], F32).ap()

    P = Prog(nc)

    def sb(name, shape, dt):
        return Buf(es.enter_context(nc.sbuf_tensor(name, list(shape), dt))[:], name)

    NWB = 5
    wbufs = [sb("wb%d" % i, [128, 4096], BF16) for i in range(NWB)]
    xres = es.enter_context(nc.sbuf_tensor("xres", [128, 5, D], F32))[:]
    hT = sb("hT", [128, NK, 576], BF16)
    stg32 = [sb("stg32_%d" % i, [128, 1024], F32) for i in range(2)]
    stg16 = [sb("stg16_%d" % i, [128, 1024], BF16) for i in range(2)]
    xn = sb("xn", [128, D], BF16)
    junk = xn
    ident = sb("ident", [128, 128], BF16)
    ones_bf = sb("ones_bf", [128, 128], BF16)
    nw = sb("nw", [128, 4, NK], F32)
    small = sb("small", [128, 64], F32)
    ARENA1 = 46 * 1024
    ARENA2 = 18 * 1024 + 512
    a1 = Arena(es.enter_context(nc.sbuf_tensor("arena1", [128, ARENA1 // 4], F32)), ARENA1, "a1")
    a2 = Arena(es.enter_context(nc.sbuf_tensor("arena2", [128, ARENA2 // 4], F32)), ARENA2, "a2")

    banks = [Buf(es.enter_context(nc.psum_tensor("bank%d" % i, [128, 512], F32))[:], "bank%d" % i) for i in range(8)]

    def xk(t):
        return "xres%d" % t

    P.op("pool", lambda e: e.memset(ident[:], 0.0), writes=[ident.k])
    P.op("pool", lambda e: e.affine_select(out=ident[:], in_=ident[:], pattern=[[-1, 128]], compare_op=ALU.not_equal,
                                           fill=1.0, base=0, channel_multiplier=1), reads=[ident.k], writes=[ident.k])
    P.op("pool", lambda e: e.memset(ones_bf[:], 1.0), writes=[ones_bf.k])
    for i, src in enumerate([nmix_in, nx_in, nffn_in, nmem_in]):
        P.dma(lambda e, i=i, src=src: e.dma_start(out=nw[:, i, :], in_=src.rearrange("(k p) -> p k", p=128)),
              key="nw%d" % i, writes=[nw.k])

    wstate = {"i": 0}

    def wload(unit):
        b = wbufs[wstate["i"] % NWB]
        wstate["i"] += 1
        P.dma(lambda e, b=b, unit=unit: e.dma_start(out=b[:], in_=wsc[unit]), key=b.k,
              reads=["wsc%d" % unit], writes=[b.k])
        return b

    cast_items = []
    cstate = {"i": 0, "engs": ("act", "dve")}
    CQ = {"act": "act", "dve": "sp", "pool": "pool"}
    cslots = [(stg32[i], stg16[i]) for i in range(2)]

    def cast_copy(eng, o_ap, i_ap, rk, wk):
        if eng == "act":
            P.op("act", lambda e: e.copy(o_ap, i_ap), reads=[rk], writes=[wk])
        else:
            P.op(eng, lambda e: e.tensor_copy(o_ap, i_ap), reads=[rk], writes=[wk])

    def cast_piece(src_ap, ncols, dst_fn, perm=None):
        st = {}

        def stage_a():
            i = cstate["i"]
            cstate["i"] += 1
            st["i"] = i
            st["slot"] = cslots[i % len(cslots)]
            s32 = st["slot"][0]
            P.dma(lambda e: e.dma_start(out=s32[:, 0:ncols], in_=src_ap), key=s32.k, writes=[s32.k])

        def stage_b():
            i = st["i"]
            s32, s16 = st["slot"]
            eng = cstate["engs"][i % len(cstate["engs"])]
            pairs = [(s16[:, 0:ncols], s32[:, 0:ncols])] if perm is None else perm(s16, s32)
            for (o_ap, i_ap) in pairs:
                cast_copy(eng, o_ap, i_ap, s32.k, s16.k)
            for (d_ap, s_ap, units) in dst_fn(s16):
                P.dma(lambda e, d_ap=d_ap, s_ap=s_ap: e.dma_start(out=d_ap, in_=s_ap), key=s16.k + "o" + CQ[eng],
                      reads=[s16.k], writes=["wsc%d" % u for u in units], eng=CQ[eng])
        cast_items.append((stage_a, stage_b))

    def units_ap(u0, nu, col0, ncol):
        return wsc[u0:u0 + nu, :, col0:col0 + ncol].rearrange("u p c -> p u c")

    def gen_cast_items():
        for k in range(NK if cfg.get("mixer", True) else 0):
            rows = slice(k * 128, (k + 1) * 128)
            for piece, ubase in ((0, U_HGA), (1, U_HGB)):
                def perm(s16, s32):
                    return [(s16[:, 0:2048].rearrange("p (h s c) -> p h s c", h=8, s=2),
                             s32[:, 0:2048].rearrange("p (s h c) -> p h s c", h=8, s=2))]
                for hh in range(2):
                    def perm2(s16, s32):
                        return [(s16[:, 0:1024].rearrange("p (h s c) -> p h s c", h=4, s=2),
                                 s32[:, 0:1024].rearrange("p (s h c) -> p h s c", h=4, s=2))]
                    src = w_in[rows, piece * 2048:(piece + 1) * 2048].rearrange("p (s h c) -> p s h c", s=2, h=8)[:, :, hh * 4:(hh + 1) * 4, :]
                    def dst(s16, k=k, ubase=ubase, hh=hh):
                        return [(units_ap(ubase + hh * 4, 4, k * 256, 256),
                                 s16[:, 0:1024].rearrange("p (h c) -> p h c", h=4),
                                 list(range(ubase + hh * 4, ubase + hh * 4 + 4)))]
                    cast_piece3(src, dst, perm2)
            def dst(s16, k=k):
                return [(units_ap(U_SQ, 4, k * 256, 256), s16[:, 0:1024].rearrange("p (u c) -> p u c", u=4),
                         list(range(U_SQ, U_SQ + 4)))]
            cast_piece(w_in[rows, 4096:5120], 1024, dst)
            def perm3(s16, s32):
                return [(s16[:, 0:256], s32[:, 0:256]),
                        (s16[:, 256:512].rearrange("p (kv r c) -> p kv r c", kv=2, r=2),
                         bcast(s32[:, 0:128].rearrange("p (kv r c) -> p kv r c", kv=2, r=1), [128, 2, 2, 64]))]
            def dst(s16, k=k):
                return [(units_ap(U_SKVA, 2, k * 256, 256), s16[:, 0:512].rearrange("p (u c) -> p u c", u=2),
                         [U_SKVA, U_SKVB])]
            cast_piece(w_in[rows, 5120:5376], 256, dst, perm3)
        for (wm, ub) in ((w_mk, U_MK), (w_mv, U_MV), (w_mq, U_MQ)):
            for kk in range(8 if cfg.get("xattn", True) else 0):
                src = wm[kk * 256:(kk + 1) * 256, :].rearrange("(k p) c -> p k c", p=128)
                def dst(s16, kk=kk, ub=ub):
                    v = s16[:, 0:1024].rearrange("p (k u c) -> p k u c", k=2, u=2)
                    return [(wsc[ub + u, :, kk * 512:(kk + 1) * 512].rearrange("p (k c) -> p k c", k=2), v[:, :, u, :], [ub + u])
                            for u in range(2)]
                cast_piece3(src, dst, None)
        for kc in range(NK if (cfg.get("mixer", True) and not DD) else 0):
            for half in range(2):
                src = w_out[kc * 128:(kc + 1) * 128, half * 1024:(half + 1) * 1024]
                def dst(s16, kc=kc, half=half):
                    kh, kl = kc // 8, kc % 8
                    return [(wsc[U_WO + (half * 2 + nn) * 2 + kh, :, kl * 512:(kl + 1) * 512], s16[:, nn * 512:(nn + 1) * 512],
                             [U_WO + (half * 2 + nn) * 2 + kh]) for nn in range(2)]
                cast_piece(src, 1024, dst)
        for kc in range(4 if (cfg.get("xattn", True) and not DD) else 0):
            for half in range(2):
                src = w_mo[kc * 128:(kc + 1) * 128, half * 1024:(half + 1) * 1024]
                def dst(s16, kc=kc, half=half):
                    return [(wsc[U_MO + kc // 2, :, (kc % 2) * 2048 + half * 1024:(kc % 2) * 2048 + (half + 1) * 1024],
                             s16[:, 0:1024], [U_MO + kc // 2])]
                cast_piece(src, 1024, dst)
        for (wm, ub) in ((w_gate, U_FG), (w_up, U_FU)):
            for k in range(NK if cfg.get("ffn", True) else 0):
                for c0 in range(0, FF, 1024):
                    ncol = min(1024, FF - c0)
                    nu = ncol // 256
                    src = wm[k * 128:(k + 1) * 128, c0:c0 + ncol]
                    def dst(s16, k=k, c0=c0, nu=nu, ub=ub, ncol=ncol):
                        return [(units_ap(ub + c0 // 256, nu, k * 256, 256), s16[:, 0:ncol].rearrange("p (u c) -> p u c", u=nu),
                                 list(range(ub + c0 // 256, ub + c0 // 256 + nu)))]
                    cast_piece(src, ncol, dst)
        for r in range(FF // 128 if (cfg.get("ffn", True) and not DD) else 0):
            for half in range(2):
                src = w_down[r * 128:(r + 1) * 128, half * 1024:(half + 1) * 1024]
                def dst(s16, r=r, half=half):
                    j, kk = r // 2, r % 2
                    return [(wsc[U_FD + j, :, kk * 2048 + half * 1024:kk * 2048 + (half + 1) * 1024], s16[:, 0:1024], [U_FD + j])]
                cast_piece(src, 1024, dst)

    def cast_piece3(src_ap3, dst_fn, perm):
        st = {}

        def stage_a():
            i = cstate["i"]
            cstate["i"] += 1
            st["i"] = i
            st["slot"] = cslots[i % len(cslots)]
            s32 = st["slot"][0]
            shp = src_ap3.shape
            if len(shp) == 3:
                dview = s32[:, 0:1024].rearrange("p (a b) -> p a b", a=shp[1])
            else:
                dview = s32[:, 0:1024].rearrange("p (a b c) -> p a b c", a=shp[1], b=shp[2])
            P.dma(lambda e: e.dma_start(out=dview, in_=src_ap3), key=s32.k, writes=[s32.k])

        def stage_b():
            i = st["i"]
            s32, s16 = st["slot"]
            eng = cstate["engs"][i % len(cstate["engs"])]
            pairs = [(s16[:, 0:1024], s32[:, 0:1024])] if perm is None else perm(s16, s32)
            for (o_ap, i_ap) in pairs:
                cast_copy(eng, o_ap, i_ap, s32.k, s16.k)
            for (d_ap, s_ap, units) in dst_fn(s16):
                P.dma(lambda e, d_ap=d_ap, s_ap=s_ap: e.dma_start(out=d_ap, in_=s_ap), key=s16.k + "o" + CQ[eng],
                      reads=[s16.k], writes=["wsc%d" % u for u in units], eng=CQ[eng])
        cast_items.append((stage_a, stage_b))

    DD = cfg.get("dd", True)
    gen_cast_items()
    dd_items = []
    ddc = {"i": 0}

    def dd_add(dst_ap, src_ap, units):
        def item():
            i = ddc["i"]
            ddc["i"] += 1
            P.dma(lambda e: e.dma_start(out=dst_ap, in_=src_ap), key="dd%d" % (i % 8), writes=["wsc%d" % u for u in units], eng="pool")
        dd_items.append(item)

    if DD:
        if cfg.get("mixer", True):
            wo_v = wsc[U_WO:U_WO + 8].rearrange("(n kh) p c -> kh p n c", kh=2)
            for kc in range(NK):
                kh, kl = kc // 8, kc % 8
                dd_add(wo_v[kh][:, :, kl * 512:(kl + 1) * 512], w_out[kc * 128:(kc + 1) * 128, :].rearrange("p (n c) -> p n c", n=4),
                       [U_WO + n * 2 + kh for n in range(4)])
        if cfg.get("xattn", True):
            for kc in range(4):
                dd_add(wsc[U_MO + kc // 2, :, (kc % 2) * 2048:(kc % 2 + 1) * 2048], w_mo[kc * 128:(kc + 1) * 128, :], [U_MO + kc // 2])
        if cfg.get("ffn", True):
            for r in range(FF // 128):
                dd_add(wsc[U_FD + r // 2, :, (r % 2) * 2048:(r % 2 + 1) * 2048], w_down[r * 128:(r + 1) * 128, :], [U_FD + r // 2])

    def do_dd(n=None):
        c = 0
        while dd_items and (n is None or c < n):
            dd_items.pop(0)()
            c += 1

    pending_b = []

    def do_casts(n=None, flush=False):
        cnt = 0
        while (cast_items or pending_b) and (n is None or cnt < n):
            depth = len(cslots) - 1
            while cast_items and len(pending_b) <= depth:
                a_, b_ = cast_items.pop(0)
                a_()
                pending_b.append(b_)
            pending_b.pop(0)()
            cnt += 1
        if flush:
            while pending_b:
                pending_b.pop(0)()

    def norm_tile(src_ap, rows, nwi, col0, src_key, eps_scale=1.0 / D):
        ss = small[0:rows, 0:1]
        rs = small[0:rows, 1:2]
        P.op("act", lambda e: e.activation(junk[0:rows, :], src_ap, AF.Square, accum_out=ss), reads=[src_key], writes=[junk.k, "ss"])
        P.op("act", lambda e: e.activation(rs, ss, AF.Ln, scale=eps_scale, bias=EPS), reads=["ss"], writes=["rs"])
        P.op("act", lambda e: e.activation(rs, rs, AF.Exp, scale=-0.5), reads=["rs"], writes=["rs"])
        P.op("act", lambda e: e.activation(xn[0:rows, :], src_ap, AF.Copy, scale=rs), reads=[src_key, "rs"], writes=[xn.k])
        for g4 in range(4):
            bk = banks[g4 % 2]
            pt = bk[:, 0:256].bitcast(BF16).rearrange("p (a b) -> p a b", a=4)
            for j in range(4):
                k = g4 * 4 + j
                P.op("pe", lambda e, k=k, j=j, pt=pt: e.transpose(pt[:, j, 0:rows], xn[0:rows, k * 128:(k + 1) * 128], ident[0:rows, 0:rows]),
                     reads=[xn.k, ident.k], writes=[bk.k], sig=(j == 3))
            P.op("dve", lambda e, g4=g4, pt=pt: e.tensor_tensor(hT[:, g4 * 4:(g4 + 1) * 4, col0:col0 + rows], pt[:, :, 0:rows],
                                                              bcast(nw[:, nwi, g4 * 4:(g4 + 1) * 4].unsqueeze(2), [128, 4, rows]), ALU.mult),
                 reads=[bk.k, nw.k], writes=[hT.k])

    def resid_add(t, rows, cols, bank):
        P.op("dve", lambda e: e.tensor_tensor(xres[0:rows, t, cols], bank[0:rows, :], xres[0:rows, t, cols], ALU.add),
             reads=[bank.k, xk(t)], writes=[xk(t)])

    def ffn_block(groups):
        W = sum(r for (_, r, _) in groups)
        has_s = W > 512
        a2.reset()
        hTc = [a2.alloc(BF16, [2, 576], "hc%d" % i) for i in range(4)]
        sg = [a2.alloc(BF16, [576], "sg%d" % i) for i in range(2)]
        hci = 0
        cnt = 0
        for jj in range(NCH // 2):
            hs = []
            for sub in range(2):
                j = jj * 2 + sub
                wg = wload(U_FG + j)
                wu = wload(U_FU + j)
                hc = hTc[hci % 4]
                hci += 1
                hs.append(hc)
                for m in range(2):
                    bg, bu, bs_ = banks[(cnt % 2) * 2], banks[(cnt % 2) * 2 + 1], banks[4 + cnt % 2]
                    s_ = sg[cnt % 2]
                    cnt += 1
                    for (wb, bk, so) in ((wg, bg, 0), (wu, bu, 64)):
                        wv = wb[:].rearrange("p (k c) -> p k c", k=NK)
                        for k in range(NK):
                            P.op("pe", lambda e, wv=wv, k=k, m=m, bk=bk: e.matmul(bk[:, 0:512], wv[:, k, m * 128:(m + 1) * 128], hT[:, k, 0:512],
                                                                                   start=(k == 0), stop=(k == NK - 1)),
                                 reads=[wb.k, hT.k], writes=[bk.k], sig=(k == NK - 1))
                        if has_s:
                            for k in range(NK):
                                P.op("pe", lambda e, wv=wv, k=k, m=m, so=so, bs_=bs_: e.matmul(bs_[:, so:so + 64], wv[:, k, m * 128:(m + 1) * 128], hT[:, k, 512:576],
                                                                                                 start=(k == 0), stop=(k == NK - 1)),
                                     reads=[wb.k, hT.k], writes=[bs_.k], sig=(k == NK - 1))
                    P.op("act", lambda e, s_=s_, bg=bg: e.activation(s_[:, 0:512], bg[:, 0:512], AF.Silu), reads=[bg.k], writes=[s_.k])
                    P.op("dve", lambda e, s_=s_, bu=bu, hc=hc, m=m: e.tensor_tensor(hc[:, m, 0:512], bu[:, 0:512], s_[:, 0:512], ALU.mult),
                         reads=[bu.k, s_.k], writes=[hc.k])
                    if has_s:
                        P.op("act", lambda e, s_=s_, bs_=bs_: e.activation(s_[:, 512:576], bs_[:, 0:64], AF.Silu), reads=[bs_.k], writes=[s_.k + "s"])
                        P.op("dve", lambda e, s_=s_, bs_=bs_, hc=hc, m=m: e.tensor_tensor(hc[:, m, 512:576], bs_[:, 64:128], s_[:, 512:576], ALU.mult),
                             reads=[bs_.k, s_.k + "s"], writes=[hc.k])
            wds = [wload(U_FD + jj * 2 + sub) for sub in range(2)]
            dcnt = 0
            for (t, rows, c0) in groups:
                for n in range(4):
                    bk = banks[6 + dcnt % 2]
                    dcnt += 1
                    idx = 0
                    for sub in range(2):
                        wdv = wds[sub][:].rearrange("p (k c) -> p k c", k=2)
                        for kk in range(2):
                            P.op("pe", lambda e, hc=hs[sub], kk=kk, wdv=wdv, n=n, bk=bk, idx=idx, c0=c0, rows=rows:
                                 e.matmul(bk[0:rows, :], hc[:, kk, c0:c0 + rows], wdv[:, kk, n * 512:(n + 1) * 512], start=(idx == 0), stop=(idx == 3)),
                                 reads=[hs[sub].k, wds[sub].k], writes=[bk.k], sig=(idx == 3))
                            idx += 1
                    resid_add(t, rows, slice(n * 512, (n + 1) * 512), bk)


    mkT = sb("mkT", [128, 4, 256], BF16)
    mv_sb = sb("mv_sb", [128, 2, 512], BF16)

    def mem_kv():
        a1.reset()
        for t in range(2):
            P.dma(lambda e, t=t: e.dma_start(out=xres[:, t, :], in_=mem_in[t * 128:(t + 1) * 128, :]), key=xk(t), writes=[xk(t)])
        for t in range(2):
            norm_tile(xres[:, t, :], 128, 3, t * 128, xk(t))
        ost = [a1.alloc(F32, [256], "ost%d" % i) for i in range(2)]
        oc = 0
        for (ub, dst, is_k) in ((U_MK, pmk_out, True), (U_MV, pmv_out, False)):
            for u in range(2):
                wb = wload(ub + u)
                wv = wb[:].rearrange("p (k c) -> p k c", k=NK)
                if is_k:
                    for mt in range(2):
                        bk = banks[2 + mt]
                        for k in range(NK):
                            P.op("pe", lambda e, wv=wv, k=k, mt=mt, bk=bk: e.matmul(bk[:, 0:256], wv[:, k, mt * 128:(mt + 1) * 128], hT[:, k, 0:256],
                                                                                   start=(k == 0), stop=(k == NK - 1)),
                                 reads=[wb.k, hT.k], writes=[bk.k], sig=(k == NK - 1))
                        P.op("act", lambda e, bk=bk, u=u, mt=mt: e.copy(mkT[:, u * 2 + mt, :], bk[:, 0:256]), reads=[bk.k], writes=[mkT.k])
                for mtile in range(2):
                    bk = banks[4 + mtile]
                    for k in range(NK):
                        P.op("pe", lambda e, wv=wv, k=k, mtile=mtile, bk=bk: e.matmul(bk[:, 0:256], hT[:, k, mtile * 128:(mtile + 1) * 128], wv[:, k, :],
                                                                                     start=(k == 0), stop=(k == NK - 1)),
                             reads=[wb.k, hT.k], writes=[bk.k], sig=(k == NK - 1))
                    o = ost[oc % 2]
                    oc += 1
                    P.op("dve", lambda e, o=o, bk=bk: e.tensor_copy(o[:], bk[:, 0:256]), reads=[bk.k], writes=[o.k])
                    if not is_k:
                        P.op("act", lambda e, bk=bk, mtile=mtile, u=u: e.copy(mv_sb[:, mtile, u * 256:(u + 1) * 256], bk[:, 0:256]), reads=[bk.k], writes=[mv_sb.k])
                    P.dma(lambda e, o=o, dst=dst, mtile=mtile, u=u: e.dma_start(out=dst[mtile * 128:(mtile + 1) * 128, u * 256:(u + 1) * 256], in_=o[:]),
                          key=o.k + "o", reads=[o.k], eng="pool")

    def xattn_block(groups):
        W = sum(r for (_, r, _) in groups)
        has_s = W > 512
        a1.reset(); a2.reset()
        qT = a2.alloc(BF16, [4, 576], "qT")
        oxT = a2.alloc(BF16, [4, 576], "oxT")
        sc_ = 128.0 ** -0.5
        for u in range(2):
            wb = wload(U_MQ + u)
            wv = wb[:].rearrange("p (k c) -> p k c", k=NK)
            for mt in range(2):
                h = u * 2 + mt
                bk = banks[2 + mt]
                for k in range(NK):
                    P.op("pe", lambda e, wv=wv, k=k, mt=mt, bk=bk: e.matmul(bk[:, 0:512], wv[:, k, mt * 128:(mt + 1) * 128], hT[:, k, 0:512],
                                                                           start=(k == 0), stop=(k == NK - 1)),
                         reads=[wb.k, hT.k], writes=[bk.k], sig=(k == NK - 1))
                P.op("act", lambda e, bk=bk, h=h: e.activation(qT[:, h, 0:512], bk[:, 0:512], AF.Copy, scale=sc_), reads=[bk.k], writes=[qT.k])
                if has_s:
                    bs_ = banks[4]
                    ks = bs_.k
                    for k in range(NK):
                        P.op("pe", lambda e, wv=wv, k=k, mt=mt, h=h: e.matmul(bs_[:, h * 64:(h + 1) * 64], wv[:, k, mt * 128:(mt + 1) * 128], hT[:, k, 512:576],
                                                                             start=(k == 0), stop=(k == NK - 1)),
                             reads=[wb.k, hT.k], writes=[ks], sig=(k == NK - 1))
                    P.op("act", lambda e, h=h: e.activation(qT[:, h, 512:576], bs_[:, h * 64:(h + 1) * 64], AF.Copy, scale=sc_), reads=[ks], writes=[qT.k])
        xp_ = cfg.get("x_parts", "memkv,q,attn,mo")
        pT = [a1.alloc(BF16, [512], "pT%d" % i) for i in range(4)]
        rcp = [a1.alloc(F32, [512], "rcp%d" % i) for i in range(2)]
        pc = 0
        for h in range(4 if "attn" in xp_ else 0):
            pts = []
            for mb in range(2):
                bk = banks[pc % 2]
                p_ = pT[pc % 4]
                pc += 1
                P.op("pe", lambda e, bk=bk, h=h, mb=mb: e.matmul(bk[:, 0:512], mkT[:, h, mb * 128:(mb + 1) * 128], qT[:, h, 0:512], start=True, stop=True),
                     reads=[mkT.k, qT.k], writes=[bk.k])
                P.op("act", lambda e, bk=bk, p_=p_: e.activation(p_[:], bk[:, 0:512], AF.Exp), reads=[bk.k], writes=[p_.k])
                pts.append(p_)
            bo, bsum = banks[2 + (h % 2) * 2], banks[3 + (h % 2) * 2]
            for mb in range(2):
                P.op("pe", lambda e, bo=bo, h=h, mb=mb, p_=pts[mb]: e.matmul(bo[:, 0:512], mv_sb[:, mb, h * 128:(h + 1) * 128], p_[:], start=(mb == 0), stop=(mb == 1)),
                     reads=[mv_sb.k, pts[mb].k], writes=[bo.k], sig=(mb == 1))
            for mb in range(2):
                P.op("pe", lambda e, bsum=bsum, mb=mb, p_=pts[mb]: e.matmul(bsum[:, 0:512], ones_bf[:], p_[:], start=(mb == 0), stop=(mb == 1)),
                     reads=[ones_bf.k, pts[mb].k], writes=[bsum.k], sig=(mb == 1))
            r_ = rcp[h % 2]
            P.op("dve", lambda e, r_=r_, bsum=bsum: e.reciprocal(r_[:], bsum[:, 0:512]), reads=[bsum.k], writes=[r_.k])
            P.op("dve", lambda e, r_=r_, bo=bo, h=h: e.tensor_tensor(oxT[:, h, 0:512], bo[:, 0:512], r_[:], ALU.mult), reads=[bo.k, r_.k], writes=[oxT.k])
        if has_s:
            ckf = [a1.alloc(F32, [2, 512], "ckf%d" % i) for i in range(2)]
            cvf = [a1.alloc(F32, [2, 512], "cvf%d" % i) for i in range(2)]
            ckb = [a1.alloc(BF16, [2, 512], "ckb%d" % i) for i in range(2)]
            cvb = [a1.alloc(BF16, [2, 512], "cvb%d" % i) for i in range(2)]
            kTn = [a1.alloc(BF16, [4, 256], "kTn%d" % i) for i in range(2)]
            pTn = [a1.alloc(BF16, [32], "pTn%d" % i) for i in range(2)]
            rcs = [a1.alloc(F32, [16], "rcs%d" % i) for i in range(2)]
            for n in range(NSEQ):
                i2 = n % 2
                bsc, bpv, btr = (banks[5], banks[6], banks[7]) if i2 == 0 else (banks[1], banks[2], banks[0])
                P.dma(lambda e, n=n, i2=i2: e.dma_start(out=ckf[i2][:], in_=cmk_in[n].rearrange("(mb p) c -> p mb c", p=128)), key=ckf[i2].k, writes=[ckf[i2].k])
                P.dma(lambda e, n=n, i2=i2: e.dma_start(out=cvf[i2][:], in_=cmv_in[n].rearrange("(mb p) c -> p mb c", p=128)), key=cvf[i2].k, writes=[cvf[i2].k])
                P.op("pool", lambda e, i2=i2: e.tensor_copy(ckb[i2][:], ckf[i2][:]), reads=[ckf[i2].k], writes=[ckb[i2].k])
                P.op("pool", lambda e, i2=i2: e.tensor_copy(cvb[i2][:], cvf[i2][:]), reads=[cvf[i2].k], writes=[cvb[i2].k])
                ktp = btr[:, 0:512].bitcast(BF16).rearrange("p (h m) -> p h m", h=4)
                for h in range(4):
                    for mb in range(2):
                        P.op("pe", lambda e, h=h, mb=mb, i2=i2, ktp=ktp: e.transpose(ktp[:, h, mb * 128:(mb + 1) * 128], ckb[i2][:, mb, h * 128:(h + 1) * 128], ident[:]),
                             reads=[ckb[i2].k, ident.k], writes=[btr.k], sig=(h == 3 and mb == 1))
                P.op("act", lambda e, i2=i2, ktp=ktp: e.copy(kTn[i2][:], ktp), reads=[btr.k], writes=[kTn[i2].k])
                ksc = bsc.k
                for h in range(4):
                    for mb in range(2):
                        c = n * 32 + (h * 2 + mb) * 4
                        P.op("pe", lambda e, h=h, mb=mb, i2=i2, c=c, n=n, bsc=bsc: e.matmul(bsc[:, c:c + 4], kTn[i2][:, h, mb * 128:(mb + 1) * 128], qT[:, h, 512 + 4 * n:516 + 4 * n],
                                                                                   start=True, stop=True),
                             reads=[kTn[i2].k, qT.k], writes=[ksc], sig=(h == 3 and mb == 1))
                P.op("act", lambda e, n=n, i2=i2, bsc=bsc: e.activation(pTn[i2][:], bsc[:, n * 32:(n + 1) * 32], AF.Exp), reads=[ksc], writes=[pTn[i2].k])
                kpv = bpv.k
                for h in range(4):
                    for mb in range(2):
                        P.op("pe", lambda e, h=h, mb=mb, i2=i2, n=n, bpv=bpv: e.matmul(bpv[:, n * 16 + h * 4:n * 16 + h * 4 + 4], cvb[i2][:, mb, h * 128:(h + 1) * 128],
                                                                             pTn[i2][:, (h * 2 + mb) * 4:(h * 2 + mb) * 4 + 4], start=(mb == 0), stop=(mb == 1)),
                             reads=[cvb[i2].k, pTn[i2].k], writes=[kpv], sig=False)
                for h in range(4):
                    for mb in range(2):
                        P.op("pe", lambda e, h=h, mb=mb, i2=i2, n=n, bpv=bpv: e.matmul(bpv[:, 256 + n * 16 + h * 4:256 + n * 16 + h * 4 + 4], ones_bf[:],
                                                                             pTn[i2][:, (h * 2 + mb) * 4:(h * 2 + mb) * 4 + 4], start=(mb == 0), stop=(mb == 1)),
                             reads=[ones_bf.k, pTn[i2].k], writes=[kpv], sig=(h == 3 and mb == 1))
                P.op("dve", lambda e, n=n, i2=i2, bpv=bpv: e.reciprocal(rcs[i2][:], bpv[:, 256 + n * 16:256 + (n + 1) * 16]), reads=[kpv], writes=[rcs[i2].k])
                P.op("dve", lambda e, n=n, i2=i2, bpv=bpv: e.tensor_tensor(oxT[:, :, 512 + 4 * n:516 + 4 * n], bpv[:, n * 16:(n + 1) * 16].rearrange("p (h t) -> p h t", h=4),
                                                                  rcs[i2][:].rearrange("p (h t) -> p h t", h=4), ALU.mult),
                     reads=[kpv, rcs[i2].k], writes=[oxT.k])
        wmo = [wload(U_MO + i) for i in range(2)]
        dcnt = 0
        for (t, rows, c0) in (groups if "mo" in xp_ else []):
            for n in range(4):
                bk = banks[dcnt % 2]
                dcnt += 1
                for kc in range(4):
                    wv = wmo[kc // 2][:].rearrange("p (k c) -> p k c", k=2)
                    P.op("pe", lambda e, kc=kc, wv=wv, n=n, bk=bk, c0=c0, rows=rows: e.matmul(bk[0:rows, :], oxT[:, kc, c0:c0 + rows], wv[:, kc % 2, n * 512:(n + 1) * 512],
                                                                                             start=(kc == 0), stop=(kc == 3)),
                         reads=[oxT.k, wmo[kc // 2].k], writes=[bk.k], sig=(kc == 3))
                resid_add(t, rows, slice(n * 512, (n + 1) * 512), bk)

    Sst = sb("Sst", [128, 8, 128], F32)
    lb = sb("lb_sb", [128, 8], F32)
    onw = sb("onw_sb", [128, 1], F32)

    def setup_hgrn():
        a1.reset()
        lt = a1.alloc(F32, [2, 8], "lt")
        P.dma(lambda e: e.dma_start(out=lt[:], in_=lbl_in.rearrange("s (h c) -> c s h", c=128)), key="lt", writes=[lt.k])
        P.dma(lambda e: e.dma_start(out=onw[:], in_=onw_in.rearrange("(p o) -> p o", o=1)), key="onw", writes=[onw.k])
        P.op("dve", lambda e: e.tensor_tensor(lb[:], lt[:, 1, :], lt[:, 0, :], ALU.subtract), reads=[lt.k], writes=[lb.k])
        P.op("act", lambda e: e.activation(lb[:], lb[:], AF.Exp), reads=[lb.k], writes=[lb.k])
        P.op("dve", lambda e: e.tensor_scalar(lb[:], lb[:], 1.0, None, ALU.add), reads=[lb.k], writes=[lb.k])
        P.op("dve", lambda e: e.reciprocal(lb[:], lb[:]), reads=[lb.k], writes=[lb.k])
        P.op("pool", lambda e: e.memset(Sst[:], 0.0), writes=[Sst.k])

    def hgrn_alloc(has_s):
        a1.reset()
        B = {}
        B["e"] = a1.alloc(F32, [576], "e")
        B["a"] = a1.alloc(F32, [576], "a")
        B["b"] = a1.alloc(F32, [576], "b")
        B["tmp"] = [a1.alloc(F32, [512], "tmp%d" % i) for i in range(2)]
        B["tmp2"] = [a1.alloc(BF16, [576], "tmp2%d" % i) for i in range(2)]
        B["KV"] = a1.alloc(BF16, [9, 512], "KV")
        B["Qh"] = a1.alloc(BF16, [512], "Qh")
        B["Qt"] = a1.alloc(BF16, [576], "Qt")
        B["vsb"] = a1.alloc(BF16, [4, 128], "vsb")
        B["KupT"] = a1.alloc(BF16, [4, 128], "KupT")
        B["Sbf"] = a1.alloc(BF16, [4, 128], "Sbf")
        B["At"] = a1.alloc(BF16, [512], "At")
        B["Rb"] = a1.alloc(F32, [4, 9], "Rb")
        B["egt"] = a1.alloc(F32, [4], "egt")
        P.op("pool", lambda e: e.memset(B["KV"][:], 0.0), writes=[B["KV"].k])
        P.op("pool", lambda e: e.memset(B["Rb"][:], 0.0), writes=[B["Rb"].k])
        if has_s:
            B["Ks0"] = a1.alloc(BF16, [64], "Ks0")
            B["Kups"] = a1.alloc(BF16, [64], "Kups")
            B["KupTs"] = a1.alloc(BF16, [128], "KupTs", rows=64)
            B["Kpad"] = a1.alloc(BF16, [16, 128], "Kpad", rows=64)
            B["vs"] = a1.alloc(BF16, [128], "vs", rows=64)
            B["Ats"] = a1.alloc(BF16, [64], "Ats", rows=64)
            B["S0"] = a1.alloc(F32, [8, 128], "S0")
            B["S0bf"] = a1.alloc(BF16, [8, 128], "S0bf")
            B["Sn"] = a1.alloc(F32, [8, 128], "Sn")
            B["egts"] = a1.alloc(F32, [16], "egts")
            B["osi"] = a1.alloc(F32, [64], "osi")
            B["os"] = a1.alloc(F32, [64], "os")
        return B

    def onorm_gate(B, o_src, o_key, g_src, g_key, cols, out_ap, out_key, bsum):
        n = cols.stop - cols.start
        sq, r_, e2, t1 = B["tmp2"][0], B["e"], B["b"], B["a"]
        P.op("act", lambda e: e.activation(sq[:, cols], o_src, AF.Square), reads=[o_key], writes=[sq.k])
        P.op("pe", lambda e: e.matmul(bsum[:, 0:n], ones_bf[:], sq[:, cols], start=True, stop=True), reads=[ones_bf.k, sq.k], writes=[bsum.k])
        P.op("act", lambda e: e.activation(r_[:, cols], bsum[:, 0:n], AF.Ln, scale=1.0 / 128, bias=EPS), reads=[bsum.k], writes=[r_.k])
        P.op("act", lambda e: e.activation(r_[:, cols], r_[:, cols], AF.Exp, scale=-0.5), reads=[r_.k], writes=[r_.k])
        P.op("act", lambda e: e.activation(e2[:, cols], g_src, AF.Exp, scale=-1.0), reads=[g_key], writes=[e2.k])
        P.op("dve", lambda e: e.tensor_scalar(e2[:, cols], e2[:, cols], 1.0, None, ALU.add), reads=[e2.k], writes=[e2.k])
        P.op("dve", lambda e: e.reciprocal(e2[:, cols], e2[:, cols]), reads=[e2.k], writes=[e2.k])
        P.op("dve", lambda e: e.scalar_tensor_tensor(t1[:, cols], o_src, onw[:, 0:1], r_[:, cols], ALU.mult, ALU.mult), reads=[o_key, onw.k, r_.k], writes=[t1.k])
        P.op("dve", lambda e: e.tensor_tensor(e2[:, cols], g_src, e2[:, cols], ALU.mult), reads=[g_key, e2.k], writes=[e2.k])
        P.op("dve", lambda e: e.tensor_tensor(out_ap, t1[:, cols], e2[:, cols], ALU.mult), reads=[t1.k, e2.k], writes=[out_key])

    def hgrn_wload(h):
        return (wload(U_HGA + h), wload(U_HGB + h))

    def hgrn_head(B, h, has_s, abT, state_only, blk, wts):
        W = 576 if has_s else 512
        bq, bf, bg, bi, bsmp, bA, bo, bst = banks
        e_, a_, b_ = B["e"], B["a"], B["b"]
        KV, Rb, egt = B["KV"], B["Rb"], B["egt"]
        wa, wb2 = wts
        wav = wa[:].rearrange("p (k c) -> p k c", k=NK)
        wbv = wb2[:].rearrange("p (k c) -> p k c", k=NK)

        def proj_fm(bank, wv, wk, c0, ncol0, ncol1, hcols):
            for k in range(NK):
                P.op("pe", lambda e, k=k: e.matmul(bank[:, ncol0:ncol1], wv[:, k, c0:c0 + 128], hT[:, k, hcols], start=(k == 0), stop=(k == NK - 1)),
                     reads=[wk, hT.k], writes=[bank.k], sig=(k == NK - 1))
        if not state_only:
            proj_fm(bq, wav, wa.k, 0, 0, 512, slice(0, 512))
        proj_fm(bf, wav, wa.k, 128, 0, 512, slice(0, 512))
        if not state_only:
            proj_fm(bg, wbv, wb2.k, 128, 0, 512, slice(0, 512))
        for t in range(4):
            for k in range(NK):
                P.op("pe", lambda e, k=k, t=t: e.matmul(bi[:, t * 128:(t + 1) * 128], hT[:, k, t * 128:(t + 1) * 128], wbv[:, k, 0:128], start=(k == 0), stop=(k == NK - 1)),
                     reads=[wb2.k, hT.k], writes=[bi.k], sig=(k == NK - 1 and t == 3))
        if has_s:
            proj_fm(bsmp, wav, wa.k, 0, 0, 64, slice(512, 576))
            proj_fm(bsmp, wav, wa.k, 128, 64, 128, slice(512, 576))
            proj_fm(bsmp, wbv, wb2.k, 128, 128, 192, slice(512, 576))
            for k in range(NK):
                P.op("pe", lambda e, k=k: e.matmul(bsmp[0:64, 192:320], hT[:, k, 512:576], wbv[:, k, 0:128], start=(k == 0), stop=(k == NK - 1)),
                     reads=[wb2.k, hT.k], writes=[bsmp.k], sig=(k == NK - 1))
        P.op("act", lambda e: e.activation(e_[:, 0:512], bf[:, 0:512], AF.Exp, scale=-1.0), reads=[bf.k], writes=[e_.k])
        if has_s:
            P.op("act", lambda e: e.activation(e_[:, 512:576], bsmp[:, 64:128], AF.Exp, scale=-1.0), reads=[bsmp.k], writes=[e_.k])
        P.op("act", lambda e: e.activation(a_[:, 0:W], e_[:, 0:W], AF.Ln, scale=lb[:, h:h + 1], bias=1.0), reads=[e_.k, lb.k], writes=[a_.k])
        P.op("act", lambda e: e.activation(b_[:, 0:W], e_[:, 0:W], AF.Ln, bias=1.0), reads=[e_.k], writes=[b_.k])
        P.op("dve", lambda e: e.tensor_tensor(a_[:, 0:W], a_[:, 0:W], b_[:, 0:W], ALU.subtract), reads=[a_.k, b_.k], writes=[a_.k])
        P.op("act", lambda e: e.activation(b_[:, 0:W], a_[:, 0:W], AF.Exp), reads=[a_.k], writes=[b_.k])
        P.op("dve", lambda e: e.tensor_scalar(b_[:, 0:W], b_[:, 0:W], -1.0, 1.0, ALU.mult, ALU.add), reads=[b_.k], writes=[b_.k])
        P.op("dve", lambda e: e.tensor_tensor_scan(e_[:, 0:W], rmask[:, 0:W], a_[:, 0:W], 0.0, ALU.mult, ALU.add), reads=[rmask.k, a_.k], writes=[e_.k])
        Gl = e_[:, 0:512].rearrange("p (t s) -> p t s", s=128)
        kk = b_[:, 0:512].rearrange("p (t s) -> p t s", s=128)
        P.op("pool", lambda e: e.tensor_copy(Rb[:, :, 1:9], e_[:, 0:512].rearrange("p (t i u) -> p t i u", t=4, u=16)[:, :, :, 15]), reads=[e_.k], writes=[Rb.k])
        vi = 0
        for i in (range(9) if not state_only else [8]):
            L = 16 * (i + 1) if i < 8 else 128
            tm, t2 = B["tmp"][vi % 2], B["tmp2"][vi % 2]
            vi += 1
            tmv = tm[:].rearrange("p (t s) -> p t s", s=128)[:, :, 0:L]
            t2v = t2[:, 0:512].rearrange("p (t s) -> p t s", s=128)[:, :, 0:L]
            P.op("dve", lambda e, i=i, L=L, tmv=tmv: e.tensor_tensor(tmv, bcast(Rb[:, :, i:i + 1], [128, 4, L]), Gl[:, :, 0:L], ALU.subtract), reads=[Rb.k, e_.k], writes=[tm.k])
            P.op("act", lambda e, tmv=tmv, t2v=t2v: e.activation(t2v, tmv, AF.Exp), reads=[tm.k], writes=[t2.k])
            P.op("pool", lambda e, i=i, L=L, t2v=t2v: e.tensor_tensor(KV[:, i, :].rearrange("p (t s) -> p t s", s=128)[:, :, 0:L], t2v, kk[:, :, 0:L], ALU.mult),
                 reads=[t2.k, b_.k], writes=[KV.k])
        P.op("act", lambda e: e.activation(egt[:], Rb[:, :, 8], AF.Exp), reads=[Rb.k], writes=[egt.k])
        vsb, KupT, Sbf, At = B["vsb"], B["KupT"], B["Sbf"], B["At"]
        P.op("act", lambda e: e.copy(vsb[:].rearrange("p t c -> p (t c)"), bi[:, 0:512]), reads=[bi.k], writes=[vsb.k])
        tv = bA[:, 0:256].bitcast(BF16).rearrange("p (a b) -> p a b", a=4)
        for t in range(4):
            P.op("pe", lambda e, t=t: e.transpose(tv[:, t, :], KV[:, 8, t * 128:(t + 1) * 128], ident[:]), reads=[KV.k, ident.k], writes=[bA.k], sig=(t == 3))
        P.op("act", lambda e: e.copy(KupT[:], tv), reads=[bA.k], writes=[KupT.k])
        for t in range(4):
            if not state_only:
                P.op("act", lambda e, t=t: e.copy(Sbf[:, t, :], Sst[:, h, :]), reads=[Sst.k + str(h)], writes=[Sbf.k])
            P.op("pe", lambda e, t=t: e.matmul(bst[:, 0:128], KupT[:, t, :], vsb[:, t, :], start=True, stop=True), reads=[KupT.k, vsb.k], writes=[bst.k])
            P.op("dve", lambda e, t=t: e.scalar_tensor_tensor(Sst[:, h, :], Sst[:, h, :], egt[:, t:t + 1], bst[:, 0:128], ALU.mult, ALU.add),
                 reads=[Sst.k + str(h), egt.k, bst.k], writes=[Sst.k + str(h)])
        if blk == 3 and not state_only:
            P.dma(lambda e: e.dma_start(out=pst_out[h], in_=Sst[:, h, :]), key="pst", reads=[Sst.k + str(h)], eng="pool")
        if state_only:
            return
        Qh, Qt = B["Qh"], B["Qt"]
        tm, t2 = B["tmp"][vi % 2], B["tmp2"][vi % 2]
        vi += 1
        P.op("dve", lambda e: e.tensor_tensor(tm[:].rearrange("p (t i u) -> p t i u", t=4, u=16), e_[:, 0:512].rearrange("p (t i u) -> p t i u", t=4, u=16),
                                              bcast(Rb[:, :, 0:8].unsqueeze(3), [128, 4, 8, 16]), ALU.subtract), reads=[e_.k, Rb.k], writes=[tm.k])
        P.op("act", lambda e: e.activation(t2[:, 0:512], tm[:], AF.Exp), reads=[tm.k], writes=[t2.k])
        P.op("dve", lambda e: e.tensor_tensor(Qh[:], bq[:, 0:512], t2[:, 0:512], ALU.mult), reads=[bq.k, t2.k], writes=[Qh.k])
        t2b = B["tmp2"][vi % 2]
        vi += 1
        P.op("act", lambda e: e.activation(t2b[:, 0:W], e_[:, 0:W], AF.Exp), reads=[e_.k], writes=[t2b.k])
        P.op("dve", lambda e: e.tensor_tensor(Qt[:, 0:512], bq[:, 0:512], t2b[:, 0:512], ALU.mult), reads=[bq.k, t2b.k], writes=[Qt.k])
        if has_s:
            P.op("dve", lambda e: e.tensor_tensor(Qt[:, 512:576], bsmp[:, 0:64], t2b[:, 512:576], ALU.mult), reads=[bsmp.k, t2b.k], writes=[Qt.k])
        for t in range(4):
            for i in range(8):
                c = t * 128 + i * 16
                P.op("pe", lambda e, t=t, i=i, c=c: e.matmul(bA[:, c:c + 16], KV[:, i, t * 128:(t + 1) * 128], Qh[:, c:c + 16], start=True, stop=True),
                     reads=[KV.k, Qh.k], writes=[bA.k], sig=(t == 3 and i == 7))
        P.op("dve", lambda e: e.tensor_tensor(At[:].rearrange("p (t s) -> p t s", s=128), bA[:, 0:512].rearrange("p (t s) -> p t s", s=128),
                                              bcast(cm01[:].unsqueeze(1), [128, 4, 128]), ALU.mult), reads=[bA.k, cm01.k], writes=[At.k])
        for t in range(4):
            P.op("pe", lambda e, t=t: e.matmul(bo[:, t * 128:(t + 1) * 128], Sbf[:, t, :], Qt[:, t * 128:(t + 1) * 128], start=True, stop=False),
                 reads=[Sbf.k, Qt.k], writes=[bo.k], sig=False)
            P.op("pe", lambda e, t=t: e.matmul(bo[:, t * 128:(t + 1) * 128], vsb[:, t, :], At[:, t * 128:(t + 1) * 128], start=False, stop=True),
                 reads=[vsb.k, At.k], writes=[bo.k], sig=(t == 3))
        onorm_gate(B, bo[:, 0:512], bo.k, bg[:, 0:512], bg.k, slice(0, 512), abT[:, h, 0:512], abT.k, bA)
        if has_s:
            hgrn_sample(B, h, abT)

    def hgrn_sample(B, h, abT):
        bq, bf, bg, bi, bsmp, bA, bo, bst = banks
        e_, b_ = B["e"], B["b"]
        Qt = B["Qt"]
        Gs = e_[:, 512:576].rearrange("p (n t) -> p n t", t=4)
        ks = b_[:, 512:576]
        tm, t2 = B["tmp"][0], B["tmp2"][1]
        Ks0, Kups, KupTs, Kpad, vs, Ats = B["Ks0"], B["Kups"], B["KupTs"], B["Kpad"], B["vs"], B["Ats"]
        S0, S0bf, Sn, egts, osi, os_ = B["S0"], B["S0bf"], B["Sn"], B["egts"], B["osi"], B["os"]
        P.op("act", lambda e: e.activation(t2[:, 0:64], e_[:, 512:576], AF.Exp, scale=-1.0), reads=[e_.k], writes=[t2.k])
        P.op("pool", lambda e: e.tensor_tensor(Ks0[:], t2[:, 0:64], ks, ALU.mult), reads=[t2.k, b_.k], writes=[Ks0.k])
        P.op("dve", lambda e: e.tensor_tensor(tm[:, 0:64].rearrange("p (n t) -> p n t", t=4), bcast(Gs[:, :, 3:4], [128, 16, 4]), Gs, ALU.subtract), reads=[e_.k], writes=[tm.k])
        P.op("act", lambda e: e.activation(t2[:, 64:128], tm[:, 0:64], AF.Exp), reads=[tm.k], writes=[t2.k])
        P.op("pool", lambda e: e.tensor_tensor(Kups[:], t2[:, 64:128], ks, ALU.mult), reads=[t2.k, b_.k], writes=[Kups.k])
        P.op("act", lambda e: e.activation(egts[:], Gs[:, :, 3], AF.Exp), reads=[e_.k], writes=[egts.k])
        P.op("act", lambda e: e.copy(vs[:], bsmp[0:64, 192:320]), reads=[bsmp.k], writes=[vs.k])
        P.op("pe", lambda e: e.matmul(bst[0:64, 128:192], Ks0[:], Qt[:, 512:576], start=True, stop=True), reads=[Ks0.k, Qt.k], writes=[bst.k])
        P.op("dve", lambda e: e.tensor_tensor(Ats[:], bst[0:64, 128:192], bdc[:], ALU.mult), reads=[bst.k, bdc.k], writes=[Ats.k])
        tvs = bst[0:64, 256:320].bitcast(BF16)
        P.op("pe", lambda e: e.transpose(tvs, Kups[:], ident[:]), reads=[Kups.k, ident.k], writes=[bst.k])
        P.op("act", lambda e: e.copy(KupTs[:], tvs), reads=[bst.k], writes=[KupTs.k])
        P.op("dve", lambda e: e.tensor_tensor(Kpad[:], bcast(KupTs[:].unsqueeze(1), [64, 16, 128]), bcast(msn[:].unsqueeze(2), [64, 16, 128]), ALU.mult),
             reads=[KupTs.k, msn.k], writes=[Kpad.k])
        P.op("pe", lambda e: e.matmul(bo[:, 64:128], vs[:], Ats[:], start=True, stop=True), reads=[vs.k, Ats.k], writes=[bo.k])
        P.op("act", lambda e: e.copy(osi[:], bo[:, 64:128]), reads=[bo.k], writes=[osi.k])
        for half in range(2):
            n0 = half * 8
            P.dma(lambda e, n0=n0: e.dma_start(out=S0[:], in_=st_in[n0:n0 + 8, h].rearrange("n c e -> c n e")), key=S0.k, writes=[S0.k])
            P.op("pool", lambda e: e.tensor_copy(S0bf[:], S0[:]), reads=[S0.k], writes=[S0bf.k])
            for i in range(8):
                n = n0 + i
                P.op("pe", lambda e, i=i, n=n: e.matmul(bo[:, 4 * n:4 * n + 4], S0bf[:, i, :], Qt[:, 512 + 4 * n:516 + 4 * n], start=True, stop=True),
                     reads=[S0bf.k, Qt.k], writes=[bo.k], sig=(i == 7))
            for i in range(8):
                n = n0 + i
                bk = (bq, bf)[i % 2]
                P.op("pe", lambda e, n=n, bk=bk: e.matmul(bk[:, 0:128], Kpad[:, n, :], vs[:], start=True, stop=True), reads=[Kpad.k, vs.k], writes=[bk.k])
                P.op("dve", lambda e, i=i, n=n, bk=bk: e.scalar_tensor_tensor(Sn[:, i, :], S0[:, i, :], egts[:, n:n + 1], bk[:, 0:128], ALU.mult, ALU.add),
                     reads=[S0.k, egts.k, bk.k], writes=[Sn.k])
            P.dma(lambda e, n0=n0: e.dma_start(out=sst_out[n0:n0 + 8, h].rearrange("n c e -> c n e"), in_=Sn[:]), key=Sn.k + "o", reads=[Sn.k], eng="pool")
        P.op("dve", lambda e: e.tensor_tensor(os_[:], bo[:, 0:64], osi[:], ALU.add), reads=[bo.k, osi.k], writes=[os_.k])
        onorm_gate(B, os_[:], os_.k, bsmp[:, 128:192], bsmp.k, slice(512, 576), abT[:, h, 512:576], abT.k, bA)

    def hgrn_block(blk, has_s, abT):
        B = hgrn_alloc(has_s)
        wts = hgrn_wload(0)
        for h in range(8):
            nxt = hgrn_wload(h + 1) if h < 7 else None
            hgrn_head(B, h, has_s, abT, False, blk, wts)
            wts = nxt

    EBo = sb("EBo", [128, 16, 128], BF16)
    EBp = sb("EBp", [128, 16, 128], BF16)
    mask_new = sb("mask_new", [64, 16, 64], BF16)
    sinkexp = sb("sinkexp", [128, 16], F32)
    flag_sb = sb("flag_sb", [128, 1], F32)
    kT_prev = sb("kT_prev", [128, 2, 128], BF16)
    v1_prev = sb("v1_prev", [128, 2, 65], BF16)
    cm01 = sb("cm01", [128, 128], BF16)
    msn = sb("msn", [64, 16], BF16)
    bdc = sb("bdc", [64, 64], BF16)
    rmask = sb("rmask", [128, 576], F32)

    def setup_tables():
        a1.reset()
        P.dma(lambda e: e.dma_start(out=flag_sb[:], in_=flag_in), key="flag", writes=[flag_sb.k])
        P.dma(lambda e: e.dma_start(out=sinkexp[:], in_=sinks_in.partition_broadcast(128)), key="sink", writes=[sinkexp.k])
        P.op("act", lambda e: e.activation(sinkexp[:], sinkexp[:], AF.Exp), reads=[sinkexp.k], writes=[sinkexp.k])
        P.op("pool", lambda e: e.memset(cm01[:], 1.0), writes=[cm01.k])
        P.op("pool", lambda e: e.affine_select(out=cm01[:], in_=cm01[:], pattern=[[1, 128]], compare_op=ALU.is_ge, fill=0.0, base=0, channel_multiplier=-1),
             reads=[cm01.k], writes=[cm01.k])
        P.op("pool", lambda e: e.memset(msn[:], 1.0), writes=[msn.k])
        P.op("pool", lambda e: e.affine_select(out=msn[:], in_=msn[:], pattern=[[-4, 16]], compare_op=ALU.is_ge, fill=0.0, base=0, channel_multiplier=1),
             reads=[msn.k], writes=[msn.k])
        P.op("pool", lambda e: e.affine_select(out=msn[:], in_=msn[:], pattern=[[4, 16]], compare_op=ALU.is_ge, fill=0.0, base=3, channel_multiplier=-1),
             reads=[msn.k], writes=[msn.k])
        P.op("pool", lambda e: e.tensor_tensor(bdc[:].rearrange("p (n t) -> p n t", t=4), cm01[0:64, 0:64].rearrange("p (n t) -> p n t", t=4),
                                               bcast(msn[:].unsqueeze(2), [64, 16, 4]), ALU.mult), reads=[cm01.k, msn.k], writes=[bdc.k])
        P.op("pool", lambda e: e.memset(rmask[:], 1.0), writes=[rmask.k])
        P.op("pool", lambda e: e.memset(rmask[:, 0:512].rearrange("p (t s) -> p t s", s=128)[:, :, 0:1], 0.0), writes=[rmask.k])
        P.op("pool", lambda e: e.memset(rmask[:, 512:576].rearrange("p (t s) -> p t s", s=4)[:, :, 0:1], 0.0), writes=[rmask.k])
        rbs = a1.alloc(F32, [16], "rbs", rows=32)
        ohs = a1.alloc(F32, [128], "ohs", rows=32)
        Z = a1.alloc(F32, [384], "Z", rows=16)
        P.dma(lambda e: e.dma_start(out=rbs[:], in_=rb_in), key="rbs", writes=[rbs.k])
        P.dma(lambda e: e.dma_start(out=ohs[:], in_=oh_in), key="ohs", writes=[ohs.k])
        b0 = banks[0]
        P.op("pe", lambda e: e.matmul(b0[0:16, 0:128], rbs[:], ohs[:], start=True, stop=True), reads=[rbs.k, ohs.k], writes=[b0.k])
        P.op("pool", lambda e: e.memset(Z[:], 0.0), writes=[Z.k])
        P.op("act", lambda e: e.activation(Z[:, 128:256], b0[0:16, 0:128], AF.Exp), reads=[b0.k, Z.k], writes=[Z.k])
        P.dma(lambda e: e.dma_start(out=zsc, in_=Z[:]), key="zsc", reads=[Z.k], writes=["zsc"])
        HK = a1.alloc(F32, [2, 16, 128], "HK")
        HKb = a1.alloc(BF16, [2, 16, 128], "HKb")
        Jm = a1.alloc(BF16, [128], "Jm")
        for i, off in enumerate((1, 129)):
            P.dma(lambda e, i=i, off=off: e.dma_start(out=HK[:, i], in_=bass.AP(zsc.tensor, off, [[1, 128], [384, 16], [1, 128]])),
                  key="HK%d" % i, reads=["zsc"], writes=[HK.k])
        P.op("dve", lambda e: e.tensor_copy(HKb[:], HK[:]), reads=[HK.k], writes=[HKb.k])
        P.op("pool", lambda e: e.memset(Jm[:], 0.0), writes=[Jm.k])
        P.op("pool", lambda e: e.affine_select(out=Jm[:], in_=Jm[:], pattern=[[1, 128]], compare_op=ALU.not_equal, fill=1.0, base=-127, channel_multiplier=1),
             reads=[Jm.k], writes=[Jm.k])
        for i, tab in enumerate((EBo, EBp)):
            for c in range(4):
                bk = banks[1 + (i * 4 + c) % 2]
                P.op("pe", lambda e, i=i, c=c, bk=bk: e.matmul(bk[:, 0:512], Jm[:], HKb[:, i, c * 4:(c + 1) * 4, :], start=True, stop=True),
                     reads=[Jm.k, HKb.k], writes=[bk.k])
                P.op("act", lambda e, tab=tab, c=c, bk=bk: e.copy(tab[:, c * 4:(c + 1) * 4, :], bk[:, 0:512].rearrange("p (h t) -> p h t", h=4)),
                     reads=[bk.k], writes=[tab.k])
        RepT = a1.alloc(BF16, [16, 4], "RepT")
        Xr = a1.alloc(BF16, [16, 16, 4], "Xr")
        P.op("pool", lambda e: e.memset(RepT[:], 0.0), writes=[RepT.k])
        P.op("pool", lambda e: e.affine_select(out=RepT[:], in_=RepT[:], pattern=[[0, 16], [-1, 4]], compare_op=ALU.not_equal, fill=1.0, base=0, channel_multiplier=1),
             reads=[RepT.k], writes=[RepT.k])
        P.op("dve", lambda e: e.tensor_copy(Xr[:], bcast(EBo[:, :, 0:4].unsqueeze(2), [128, 16, 16, 4])), reads=[EBo.k], writes=[Xr.k])
        for c in range(2):
            bk = banks[3 + c]
            P.op("pe", lambda e, c=c, bk=bk: e.matmul(bk[0:64, 0:512], RepT[:].rearrange("p a b -> p (a b)"), Xr[:, c * 8:(c + 1) * 8].rearrange("p h n t -> p (h n t)"),
                                                      start=True, stop=True), reads=[RepT.k, Xr.k], writes=[bk.k])
            P.op("dve", lambda e, c=c, bk=bk: e.tensor_tensor(mask_new[:, c * 8:(c + 1) * 8, :].rearrange("p h (n t) -> p h n t", t=4),
                                                              bk[0:64, 0:512].rearrange("p (h n t) -> p h n t", h=8, t=4),
                                                              bcast(msn[:].unsqueeze(1).unsqueeze(3), [64, 8, 16, 4]), ALU.mult),
                 reads=[bk.k, msn.k], writes=[mask_new.k])
        P.dma(lambda e: e.dma_start(out=ssk_out[:, 0:124, :], in_=ck_in[:, 4:128, :]), key="sskc", eng="pool")
        P.dma(lambda e: e.dma_start(out=ssv_out[:, 0:124, :], in_=cv_in[:, 4:128, :]), key="ssvc", eng="pool")

    def heads_view(tab, kv, half):
        return tab.rearrange("p (kv j u) x -> p kv j u x", kv=2, u=2)[:, kv, :, half, :]

    def swa_block(blk, groups, abT, first_core_tile):
        W = sum(r for (_, r, _) in groups)
        has_s = W > 512
        a1.reset()
        qT = a1.alloc(BF16, [8, 576], "qT")
        kT = a1.alloc(BF16, [2, 576], "kT")
        v1 = a1.alloc(BF16, [5, 2, 65], "v1")
        mark = a1.off
        kvf = [a1.alloc(F32, [256], "kvf%d" % i) for i in range(2)]
        Eb = [a1.alloc(BF16, [512], "Eb%d" % i) for i in range(2)]
        PT = [a1.alloc(BF16, [512], "PT%d" % i) for i in range(4)]
        btok = [a1.alloc(BF16, [1024], "btok%d" % i) for i in range(2)]
        den = a1.alloc(F32, [4], "den")
        if first_core_tile:
            EBpf = a1.alloc(BF16, [16, 128], "EBpf")
            P.op("dve", lambda e: e.tensor_scalar(EBpf[:], EBp[:], flag_sb[:, 0:1], None, ALU.mult), reads=[EBp.k, flag_sb.k], writes=[EBpf.k])
        P.op("pool", lambda e: e.memset(v1[:, :, :, 64:65], 1.0), writes=[v1.k])
        for kv in range(2):
            for half in range(2):
                wb = wload(U_SQ + kv * 2 + half)
                wv = wb[:].rearrange("p (k c) -> p k c", k=NK)
                for mt in range(2):
                    j = kv * 4 + half * 2 + mt
                    bk = banks[mt]
                    for k in range(NK):
                        P.op("pe", lambda e, wv=wv, k=k, mt=mt, bk=bk: e.matmul(bk[:, 0:512], wv[:, k, mt * 128:(mt + 1) * 128], hT[:, k, 0:512],
                                                                               start=(k == 0), stop=(k == NK - 1)),
                             reads=[wb.k, hT.k], writes=[bk.k], sig=(k == NK - 1))
                    P.op("act", lambda e, bk=bk, j=j: e.activation(qT[:, j, 0:512], bk[:, 0:512], AF.Copy, scale=0.125), reads=[bk.k], writes=[qT.k])
                    if has_s:
                        bs_ = banks[4 + mt]
                        for k in range(NK):
                            P.op("pe", lambda e, wv=wv, k=k, mt=mt, bs_=bs_: e.matmul(bs_[:, 0:64], wv[:, k, mt * 128:(mt + 1) * 128], hT[:, k, 512:576],
                                                                                     start=(k == 0), stop=(k == NK - 1)),
                                 reads=[wb.k, hT.k], writes=[bs_.k], sig=(k == NK - 1))
                        P.op("act", lambda e, bs_=bs_, j=j: e.activation(qT[:, j, 512:576], bs_[:, 0:64], AF.Copy, scale=0.125), reads=[bs_.k], writes=[qT.k])
        wb = wload(U_SKVB)
        wv = wb[:].rearrange("p (k c) -> p k c", k=NK)
        for kv in range(2):
            bk = banks[2 + kv]
            for k in range(NK):
                P.op("pe", lambda e, wv=wv, k=k, kv=kv, bk=bk: e.matmul(bk[:, 0:512], wv[:, k, kv * 128:(kv + 1) * 128], hT[:, k, 0:512],
                                                                       start=(k == 0), stop=(k == NK - 1)),
                     reads=[wb.k, hT.k], writes=[bk.k], sig=(k == NK - 1))
            P.op("act", lambda e, bk=bk, kv=kv: e.copy(kT[:, kv, 0:512], bk[:, 0:512]), reads=[bk.k], writes=[kT.k])
            if has_s:
                bs_ = banks[6 + kv]
                for k in range(NK):
                    P.op("pe", lambda e, wv=wv, k=k, kv=kv, bs_=bs_: e.matmul(bs_[:, 0:64], wv[:, k, kv * 128:(kv + 1) * 128], hT[:, k, 512:576],
                                                                             start=(k == 0), stop=(k == NK - 1)),
                         reads=[wb.k, hT.k], writes=[bs_.k], sig=(k == NK - 1))
                P.op("act", lambda e, bs_=bs_, kv=kv: e.copy(kT[:, kv, 512:576], bs_[:, 0:64]), reads=[bs_.k], writes=[kT.k])
        wb = wload(U_SKVA)
        wv = wb[:].rearrange("p (k c) -> p k c", k=NK)
        for gi, (t, rows, c0) in enumerate(groups):
            bk = banks[gi % 2]
            for k in range(NK):
                P.op("pe", lambda e, wv=wv, k=k, bk=bk, c0=c0, rows=rows: e.matmul(bk[0:rows, 0:256], hT[:, k, c0:c0 + rows], wv[:, k, :],
                                                                                  start=(k == 0), stop=(k == NK - 1)),
                     reads=[wb.k, hT.k], writes=[bk.k], sig=(k == NK - 1))
            P.op("act", lambda e, bk=bk, t=t, rows=rows: e.copy(v1[0:rows, t, :, 0:64], bk[0:rows, 128:256].rearrange("p (kv d) -> p kv d", kv=2)),
                 reads=[bk.k], writes=[v1.k])
            if blk == 3 and (t == 3 or rows != 128):
                kf = kvf[gi % 2]
                P.op("dve", lambda e, kf=kf, bk=bk, rows=rows: e.tensor_copy(kf[0:rows, :], bk[0:rows, 0:256]), reads=[bk.k], writes=[kf.k])
                if rows == 128:
                    P.dma(lambda e, kf=kf: e.dma_start(out=pk_out, in_=kf[:, 0:128]), key=kf.k + "a", reads=[kf.k], eng="pool")
                    P.dma(lambda e, kf=kf: e.dma_start(out=pv_out, in_=kf[:, 128:256]), key=kf.k + "b", reads=[kf.k], eng="pool")
                else:
                    for n in range(NSEQ):
                        P.dma(lambda e, kf=kf, n=n: e.dma_start(out=ssk_out[n, 124:128, :], in_=kf[4 * n:4 * n + 4, 0:128]), key=kf.k + "a", reads=[kf.k], eng="pool")
                        P.dma(lambda e, kf=kf, n=n: e.dma_start(out=ssv_out[n, 124:128, :], in_=kf[4 * n:4 * n + 4, 128:256]), key=kf.k + "b", reads=[kf.k], eng="pool")
        sc = 0
        oc = 0
        for t in range(4):
            bt = btok[t % 2]
            for kv in range(2):
                for half in range(2):
                    hs = slice(half * 64, (half + 1) * 64)
                    pts = []
                    for kb in range(2):
                        if kb == 0:
                            if t == 0:
                                kap, kkey = kT_prev[hs, kv, :], kT_prev.k
                                tab = EBpf if first_core_tile else EBp
                            else:
                                kap, kkey = kT[hs, kv, (t - 1) * 128:t * 128], kT.k
                                tab = EBp
                        else:
                            kap, kkey = kT[hs, kv, t * 128:(t + 1) * 128], kT.k
                            tab = EBo
                        bk = banks[sc % 2]
                        e_, p_ = Eb[sc % 2], PT[sc % 4]
                        sc += 1
                        P.op("pe", lambda e, bk=bk, kap=kap, kv=kv, hs=hs, t=t: e.matmul(bk[:, 0:512], kap, qT[hs, kv * 4:(kv + 1) * 4, t * 128:(t + 1) * 128],
                                                                                        start=True, stop=True), reads=[kkey, qT.k], writes=[bk.k])
                        P.op("act", lambda e, bk=bk, e_=e_: e.activation(e_[:], bk[:, 0:512], AF.Exp), reads=[bk.k], writes=[e_.k])
                        P.op("pool", lambda e, e_=e_, p_=p_, tab=tab, kv=kv, half=half: e.tensor_tensor(p_[:].rearrange("p (j t) -> p j t", j=4),
                                                                                                     e_[:].rearrange("p (j t) -> p j t", j=4),
                                                                                                     heads_view(tab[:], kv, half), ALU.mult),
                             reads=[e_.k, tab.k], writes=[p_.k])
                        pts.append(p_)
                    bo = banks[2 + oc % 2]
                    oc += 1
                    ov = bo[:, 0:260].rearrange("p (j c) -> p j c", j=4)
                    for j in range(4):
                        for kb in range(2):
                            if kb == 0:
                                vap, vkey = (v1_prev[:, kv, :], v1_prev.k) if t == 0 else (v1[:, t - 1, kv, :], v1.k)
                            else:
                                vap, vkey = v1[:, t, kv, :], v1.k
                            P.op("pe", lambda e, ov=ov, j=j, kb=kb, vap=vap, p_=pts[kb]: e.matmul(ov[:, j, :], p_[:, j * 128:(j + 1) * 128], vap, start=(kb == 0), stop=(kb == 1)),
                                 reads=[pts[kb].k, vkey], writes=[bo.k], sig=(j == 3 and kb == 1))
                    P.op("dve", lambda e, ov=ov, kv=kv, half=half: e.tensor_tensor(den[:].unsqueeze(2), ov[:, :, 64:65],
                                                                                  heads_view(sinkexp[:].unsqueeze(2), kv, half), ALU.add),
                         reads=[bo.k, sinkexp.k], writes=[den.k])
                    P.op("dve", lambda e: e.reciprocal(den[:], den[:]), reads=[den.k], writes=[den.k])
                    P.op("dve", lambda e, ov=ov, bt=bt, kv=kv, half=half: e.tensor_tensor(heads_view(bt[:].rearrange("p (h d) -> p h d", d=64), kv, half), ov[:, :, 0:64],
                                                                                         bcast(den[:].unsqueeze(2), [128, 4, 64]), ALU.mult),
                         reads=[bo.k, den.k], writes=[bt.k])
            btr = banks[4 + t % 2]
            tv = btr[:, 0:512].bitcast(BF16).rearrange("p (a b) -> p a b", a=8)
            for pr in range(8):
                P.op("pe", lambda e, tv=tv, pr=pr, bt=bt: e.transpose(tv[:, pr, :], bt[:, pr * 128:(pr + 1) * 128], ident[:]), reads=[bt.k, ident.k], writes=[btr.k], sig=(pr == 7))
            P.op("act", lambda e, tv=tv, t=t: e.copy(abT[:, 8:16, t * 128:(t + 1) * 128], tv), reads=[btr.k], writes=[abT.k])
        if has_s:
            P.barrier()
            a1.off = mark
            swa_sample(qT, kT, v1, abT)
        P.op("pool", lambda e: e.tensor_copy(kT_prev[:], kT[:, :, 384:512]), reads=[kT.k], writes=[kT_prev.k])
        P.op("pool", lambda e: e.tensor_copy(v1_prev[:], v1[:, 3]), reads=[v1.k], writes=[v1_prev.k])

    def swa_sample(qT, kT, v1, abT):
        ckf = [a1.alloc(F32, [4, 128], "ckf0")] * 2
        ckd = a1.alloc(BF16, [8, 2, 2, 64], "ckd")
        v1c = a1.alloc(BF16, [16, 2, 65], "v1c")
        kcT = a1.alloc(BF16, [2, 16, 128], "kcT")
        Ppad = a1.alloc(BF16, [16, 4, 64], "Ppad")
        Ec = a1.alloc(BF16, [256], "Ec")
        En = a1.alloc(BF16, [256], "En", rows=64)
        Pn = a1.alloc(BF16, [4, 64], "Pn", rows=64)
        bts = a1.alloc(BF16, [1024], "bts", rows=64)
        dens = a1.alloc(F32, [4], "dens", rows=64)
        P.op("pool", lambda e: e.memset(Ppad[:], 0.0), writes=[Ppad.k])
        P.op("pool", lambda e: e.memset(v1c[:, :, :, 64:65], 1.0), writes=[v1c.k])
        for g4 in range(4):
            f_ = ckf[g4 % 2]
            P.dma(lambda e, g4=g4, f_=f_: e.dma_start(out=f_[:], in_=cv_in[g4 * 4:(g4 + 1) * 4].rearrange("n s c -> s n c")), key=f_.k, writes=[f_.k])
            P.op("dve", lambda e, g4=g4, f_=f_: e.tensor_copy(v1c[:, g4 * 4:(g4 + 1) * 4, :, 0:64], f_[:].rearrange("p n (kv d) -> p n kv d", kv=2)),
                 reads=[f_.k], writes=[v1c.k])
        tc_ = 0
        for h8 in range(2):
            for g4 in (2 * h8, 2 * h8 + 1):
                f_ = ckf[g4 % 2]
                P.dma(lambda e, g4=g4, f_=f_: e.dma_start(out=f_[:], in_=ck_in[g4 * 4:(g4 + 1) * 4].rearrange("n s c -> s n c")), key=f_.k, writes=[f_.k])
                P.op("dve", lambda e, g4=g4, f_=f_: e.tensor_copy(ckd[:, (g4 % 2) * 4:(g4 % 2) * 4 + 4], bcast(f_[:].rearrange("p n (kv d) -> p n kv d", kv=2).unsqueeze(3), [128, 4, 2, 2, 64])),
                     reads=[f_.k], writes=[ckd.k])
            for kv in range(2):
                for n4 in (2 * h8, 2 * h8 + 1):
                    btr = banks[tc_ % 2]
                    tc_ += 1
                    tv = btr[:, 0:256].bitcast(BF16).rearrange("p (a b) -> p a b", a=4)
                    for i in range(4):
                        n = n4 * 4 + i
                        P.op("pe", lambda e, tv=tv, i=i, n=n, kv=kv: e.transpose(tv[:, i, :], ckd[:, n % 8, kv].rearrange("p a d -> p (a d)"), ident[:]),
                             reads=[ckd.k, ident.k], writes=[btr.k], sig=(i == 3))
                    P.op("act", lambda e, tv=tv, kv=kv, n4=n4: e.copy(kcT[:, kv, n4 * 4:(n4 + 1) * 4, :], tv), reads=[btr.k], writes=[kcT.k])
        pd = Ppad[:]
        oc = 0
        for kv in range(2):
            for half in range(2):
                hs = slice(half * 64, (half + 1) * 64)
                bS, bN = banks[2], banks[3]
                for m in range(NSEQ):
                    P.op("pe", lambda e, m=m, kv=kv, hs=hs: e.matmul(bS[:, m * 16:(m + 1) * 16], kcT[hs, kv, m, :], qT[hs, kv * 4:(kv + 1) * 4, 512 + 4 * m:516 + 4 * m],
                                                                    start=True, stop=True), reads=[kcT.k, qT.k], writes=[bS.k], sig=(m == NSEQ - 1))
                P.op("pe", lambda e, kv=kv, hs=hs: e.matmul(bN[0:64, 0:256], kT[hs, kv, 512:576], qT[hs, kv * 4:(kv + 1) * 4, 512:576], start=True, stop=True),
                     reads=[kT.k, qT.k], writes=[bN.k])
                P.op("act", lambda e: e.activation(Ec[:], bS[:, 0:256], AF.Exp), reads=[bS.k], writes=[Ec.k])
                P.op("act", lambda e: e.activation(En[:], bN[0:64, 0:256], AF.Exp), reads=[bN.k], writes=[En.k])
                diag = bass.AP(pd.tensor, pd.offset, [list(pd.ap[0]), [260, 16], [64, 4], [1, 4]])
                P.op("dve", lambda e, diag=diag, kv=kv, half=half: e.tensor_tensor(diag, Ec[:].rearrange("p (m j t) -> p m j t", m=16, j=4),
                                                                                  bcast(heads_view(EBp[:, :, 0:4], kv, half).unsqueeze(1), [128, 16, 4, 4]), ALU.mult),
                     reads=[Ec.k, EBp.k], writes=[Ppad.k])
                P.op("pool", lambda e, kv=kv, half=half: e.tensor_tensor(Pn[:], En[:].rearrange("p (j x) -> p j x", j=4), heads_view(mask_new[:], kv, half), ALU.mult),
                     reads=[En.k, mask_new.k], writes=[Pn.k])
                bo = banks[4 + oc % 2]
                oc += 1
                ov = bo[0:64, 0:260].rearrange("p (j c) -> p j c", j=4)
                for j in range(4):
                    for m in range(NSEQ):
                        P.op("pe", lambda e, ov=ov, j=j, m=m, kv=kv: e.matmul(ov[:, j, :], Ppad[:, m, j, :], v1c[:, m, kv, :], start=(m == 0), stop=False),
                             reads=[Ppad.k, v1c.k], writes=[bo.k], sig=False)
                    P.op("pe", lambda e, ov=ov, j=j, kv=kv: e.matmul(ov[:, j, :], Pn[:, j, :], v1[0:64, 4, kv, :], start=False, stop=True),
                         reads=[Pn.k, v1.k], writes=[bo.k], sig=(j == 3))
                P.op("dve", lambda e, ov=ov, kv=kv, half=half: e.tensor_tensor(dens[:].unsqueeze(2), ov[:, :, 64:65],
                                                                              heads_view(sinkexp[0:64, :].unsqueeze(2), kv, half), ALU.add),
                     reads=[bo.k, sinkexp.k], writes=[dens.k])
                P.op("dve", lambda e: e.reciprocal(dens[:], dens[:]), reads=[dens.k], writes=[dens.k])
                P.op("dve", lambda e, ov=ov, kv=kv, half=half: e.tensor_tensor(heads_view(bts[:].rearrange("p (h d) -> p h d", d=64), kv, half), ov[:, :, 0:64],
                                                                              bcast(dens[:].unsqueeze(2), [64, 4, 64]), ALU.mult),
                     reads=[bo.k, dens.k], writes=[bts.k])
        btr = banks[6]
        tv = btr[:, 0:256].bitcast(BF16).rearrange("p (a b) -> p a b", a=8)
        for pr in range(8):
            P.op("pe", lambda e, tv=tv, pr=pr: e.transpose(tv[:, pr, :], bts[:, pr * 128:(pr + 1) * 128], ident[0:64, 0:64]), reads=[bts.k, ident.k], writes=[btr.k], sig=(pr == 7))
        P.op("act", lambda e, tv=tv: e.copy(abT[:, 8:16, 512:576], tv), reads=[btr.k], writes=[abT.k])

    def wout_block(groups, abT):
        dcnt = 0
        for n in range(4):
            wo = [wload(U_WO + n * 2 + i) for i in range(2)]
            for (t, rows, c0) in groups:
                bk = banks[6 + dcnt % 2]
                dcnt += 1
                for kc in range(16):
                    wv = wo[kc // 8][:].rearrange("p (k c) -> p k c", k=8)
                    P.op("pe", lambda e, kc=kc, wv=wv, bk=bk, c0=c0, rows=rows: e.matmul(bk[0:rows, :], abT[:, kc, c0:c0 + rows], wv[:, kc % 8, :],
                                                                                        start=(kc == 0), stop=(kc == 15)),
                         reads=[abT.k, wo[kc // 8].k], writes=[bk.k], sig=(kc == 15))
                resid_add(t, rows, slice(n * 512, (n + 1) * 512), bk)

    def prefix_phase():
        for pb in range(4):
            P.barrier()
            for t in range(4):
                P.dma(lambda e, t=t, pb=pb: e.dma_start(out=xres[:, t, :], in_=xp[pb * 512 + t * 128: pb * 512 + (t + 1) * 128, :]), key=xk(t), writes=[xk(t)])
            for t in range(4):
                norm_tile(xres[:, t, :], 128, 0, t * 128, xk(t))
            if cfg.get("hgrn", True):
                B = hgrn_alloc(False)
                wts = hgrn_wload(0)
                for h in range(8):
                    nxt = hgrn_wload(h + 1) if h < 7 else None
                    hgrn_head(B, h, False, None, True, -1, wts)
                    wts = nxt
                    do_dd(2)
                    do_casts(cfg.get("cast_per_head", 7))
        P.barrier()
        a1.reset()
        wb = wload(U_SKVB)
        wv = wb[:].rearrange("p (k c) -> p k c", k=NK)
        for kv in range(2):
            bk = banks[2 + kv]
            for k in range(NK):
                P.op("pe", lambda e, wv=wv, k=k, kv=kv, bk=bk: e.matmul(bk[:, 0:128], wv[:, k, kv * 128:(kv + 1) * 128], hT[:, k, 384:512],
                                                                       start=(k == 0), stop=(k == NK - 1)),
                     reads=[wb.k, hT.k], writes=[bk.k], sig=(k == NK - 1))
            P.op("act", lambda e, bk=bk, kv=kv: e.copy(kT_prev[:, kv, :], bk[:, 0:128]), reads=[bk.k], writes=[kT_prev.k])
        wb = wload(U_SKVA)
        wv = wb[:].rearrange("p (k c) -> p k c", k=NK)
        bk = banks[4]
        for k in range(NK):
            P.op("pe", lambda e, wv=wv, k=k, bk=bk: e.matmul(bk[:, 0:256], hT[:, k, 384:512], wv[:, k, :], start=(k == 0), stop=(k == NK - 1)),
                 reads=[wb.k, hT.k], writes=[bk.k], sig=(k == NK - 1))
        P.op("pool", lambda e: e.memset(v1_prev[:, :, 64:65], 1.0), writes=[v1_prev.k])
        P.op("act", lambda e, bk=bk: e.copy(v1_prev[:, :, 0:64], bk[:, 128:256].rearrange("p (kv d) -> p kv d", kv=2)), reads=[bk.k, v1_prev.k], writes=[v1_prev.k])

    def final_out(groups, blk):
        a1.reset()
        nfw = a1.alloc(F32, [D], "nfw")
        ybuf = [a1.alloc(F32, [D], "ybuf%d" % i) for i in range(2)]
        P.dma(lambda e: e.dma_start(out=nfw[:], in_=nfin_in.partition_broadcast(128)), key=nfw.k, writes=[nfw.k])
        for gi, (t, rows, c0) in enumerate(groups):
            src = xres[0:rows, t, :]
            ss = small[0:rows, 2:3]
            rs = small[0:rows, 3:4]
            yb = ybuf[gi % 2]
            P.op("act", lambda e, src=src, rows=rows, ss=ss: e.activation(junk[0:rows, :], src, AF.Square, accum_out=ss), reads=[xk(t)], writes=[junk.k, "fss"])
            P.op("act", lambda e, rs=rs, ss=ss: e.activation(rs, ss, AF.Ln, scale=1.0 / D, bias=EPS), reads=["fss"], writes=["frs"])
            P.op("act", lambda e, rs=rs: e.activation(rs, rs, AF.Exp, scale=-0.5), reads=["frs"], writes=["frs"])
            P.op("dve", lambda e, src=src, rows=rows, rs=rs, yb=yb: e.scalar_tensor_tensor(yb[0:rows, :], src, rs, nfw[0:rows, :], ALU.mult, ALU.mult),
                 reads=[xk(t), "frs", nfw.k], writes=[yb.k])
            if rows == 128:
                dst = y_out[blk * 512 + gi * 128: blk * 512 + (gi + 1) * 128, :]
            else:
                dst = ys_out
            P.dma(lambda e, dst=dst, yb=yb, rows=rows: e.dma_start(out=dst, in_=yb[0:rows, :]), key=yb.k + "o", reads=[yb.k], eng="pool")

    a2.reset()
    for i in range(3):
        cslots.append((a2.alloc(F32, [1024], "xs32_%d" % i), a2.alloc(BF16, [1024], "xs16_%d" % i)))
    n_win = sum(1 for _ in range(NK)) * 6 if cfg.get("mixer", True) else 0
    do_casts(n_win if cfg.get("prefix", True) and cfg.get("mixer", True) and cfg.get("hgrn", True) else None)
    cstate["engs"] = ("dve", "act")
    if cfg.get("mixer", True):
        P.barrier()
        setup_tables()
        P.barrier()
        setup_hgrn()
        P.barrier()
        if cfg.get("prefix", True):
            prefix_phase()
    do_dd()
    do_casts(None, flush=True)
    P.barrier()
    del cslots[2:]
    if cfg.get("xattn", True) and "memkv" in cfg.get("x_parts", "memkv"):
        P.barrier()
        mem_kv()
    for blk in cfg.get("blocks", (0, 1, 2, 3)):
        P.barrier()
        groups = [(t, 128, t * 128) for t in range(4)]
        if blk == 3:
            groups.append((4, WS, 512))
        for (t, rows, c0) in groups:
            src = xm[blk * 512 + t * 128: blk * 512 + (t + 1) * 128, :] if rows == 128 else xs
            P.dma(lambda e, src=src, t=t, rows=rows: e.dma_start(out=xres[0:rows, t, :], in_=src), key=xk(t), writes=[xk(t)])
        if cfg.get("mixer", True):
            for (t, rows, c0) in groups:
                norm_tile(xres[0:rows, t, :], rows, 0, c0, xk(t))
            a2.reset()
            abT = a2.alloc(BF16, [16, 576], "abT")
            if not cfg.get("hgrn", True):
                P.op("pool", lambda e, abT=abT: e.memset(abT[:, 0:8, :], 0.0), writes=[abT.k])
            else:
                hgrn_block(blk, blk == 3, abT)
                P.barrier()
            if cfg.get("swa", True):
                swa_block(blk, groups, abT, blk == 0)
            else:
                P.op("pool", lambda e, abT=abT: e.memset(abT[:, 8:16, :], 0.0), writes=[abT.k])
            if not cfg.get("swa", True):
                P.barrier()
            wout_block(groups, abT)
            P.barrier()
        if cfg.get("xattn", True):
            for (t, rows, c0) in groups:
                norm_tile(xres[0:rows, t, :], rows, 1, c0, xk(t))
            xattn_block(groups)
            P.barrier()
        if cfg.get("ffn", True):
            for (t, rows, c0) in groups:
                norm_tile(xres[0:rows, t, :], rows, 2, c0, xk(t))
            ffn_block(groups)
        if not cfg.get("ffn", True):
            P.barrier()
        final_out(groups, blk)
    P.emit()


_CACHE = {}


def _rel_bucket_onehot():
    n = np.arange(128)
    nf = np.maximum(n, 1).astype(np.float32)
    large = 16 + (np.log(nf / np.float32(16)) / np.float32(math.log(128 / 16)) * np.float32(16)).astype(np.int32)
    large = np.minimum(large, 31)
    b = np.where(n < 16, n, large)
    oh = np.zeros((32, 128), np.float32)
    oh[b, n] = 1.0
    return oh


def kernel(cfg=None, **inp):
    cfg = cfg or {}
    key = repr(sorted(cfg.items()))
    if key not in _CACHE:
        _CACHE[key] = build_program(cfg)
    nc = _CACHE[key]
    f = lambda a: np.ascontiguousarray(np.asarray(a, dtype=np.float32))
    xpr = f(inp["x_prompt"]); xsa = f(inp["x_sample"])
    shared = {
        "oh": _rel_bucket_onehot(),
        "nmix": f(inp["norm_mix_w"][0]), "w_in": f(inp["w_in"][0]), "lbl": f(inp["hgrn_lb_logits"]),
        "onw": f(inp["hgrn_onorm_w"][0]), "sinks": f(inp["swa_sinks"][0]), "rb": f(inp["rel_bias"]),
        "w_out": f(inp["w_out"][0]), "nx": f(inp["norm_xattn_w"][0]), "nmem": f(inp["mem_norm_w"][0]),
        "w_mq": f(inp["w_mq"][0]), "w_mk": f(inp["w_mk"][0]), "w_mv": f(inp["w_mv"][0]), "w_mo": f(inp["w_mo"][0]),
        "nffn": f(inp["norm_ffn_w"][0]), "w_gate": f(inp["w_gate"][0]), "w_up": f(inp["w_up"][0]),
        "w_down": f(inp["w_down"][0]), "nfin": f(inp["norm_final_w"]),
    }
    zeros_half = np.zeros((2048, D), np.float32)
    in_maps = []
    for c in range(8):
        b, half = c // 2, c % 2
        sl = slice(c * NSEQ, (c + 1) * NSEQ)
        m = dict(shared)
        m["xm"] = f(xpr[b, half * 2048:(half + 1) * 2048])
        m["xp"] = f(xpr[b, 0:2048]) if half == 1 else zeros_half
        m["xs"] = f(xsa[sl].reshape(WS, D))
        m["st"] = f(inp["state_hgrn"][0, sl])
        m["ck"] = f(inp["cache_swa_k"][0, sl].reshape(NSEQ, 128, 128))
        m["cv"] = f(inp["cache_swa_v"][0, sl].reshape(NSEQ, 128, 128))
        m["cmk"] = f(inp["cache_mem_k"][0, sl].reshape(NSEQ, 256, 512))
        m["cmv"] = f(inp["cache_mem_v"][0, sl].reshape(NSEQ, 256, 512))
        m["mem"] = f(inp["mem_prompt"][b])
        m["flag"] = np.full((128, 1), float(half), np.float32)
        in_maps.append(m)
    res = run_bass_kernel_spmd(nc, in_maps, core_ids=list(range(8)))
    R = res.results
    y_prompt = np.stack([np.concatenate([R[2 * b]["y"], R[2 * b + 1]["y"]], 0) for b in range(4)])
    y_sample = np.concatenate([R[c]["ys"].reshape(NSEQ, 4, D) for c in range(8)], 0)
    p_state = np.stack([R[2 * b + 1]["pst"] for b in range(4)])[None]
    p_k = np.stack([R[2 * b + 1]["pk"].reshape(128, 2, 64) for b in range(4)])[None]
    p_v = np.stack([R[2 * b + 1]["pv"].reshape(128, 2, 64) for b in range(4)])[None]
    p_mk = np.stack([R[2 * b]["pmk"].reshape(256, 4, 128) for b in range(4)])[None]
    p_mv = np.stack([R[2 * b]["pmv"].reshape(256, 4, 128) for b in range(4)])[None]
    s_state = np.concatenate([R[c]["sst"] for c in range(8)], 0)[None]
    s_k = np.concatenate([R[c]["ssk"].reshape(NSEQ, 128, 2, 64) for c in range(8)], 0)[None]
    s_v = np.concatenate([R[c]["ssv"].reshape(NSEQ, 128, 2, 64) for c in range(8)], 0)[None]
    outs = (y_prompt, y_sample, p_state, p_k, p_v, p_mk, p_mv, s_state, s_k, s_v)
    outs = tuple(np.ascontiguousarray(o, dtype=np.float32) for o in outs)
    if cfg.get("raw"):
        return outs, R
    return outs
```
